# Optimizing a Trainium2 kernel written in Bass

```python
import math
import jax
import jax.numpy as jnp
from jax import lax
import numpy as np

D_MODEL = 1024
BATCH = 16
SEQ = 256
DEPTH = 2
DEC_BATCH = 4
DEC_SEQ = 4096
PAST_LEN = 256

GRID_W = 64
N_EVEN = (DEPTH + 1) // 2
N_ODD = DEPTH // 2
ADA_CHUNKS = 6
EPS = 1e-6
D_S5 = D_MODEL // 2
S5_GROUP_CH = 16
S5_GROUPS = D_S5 // S5_GROUP_CH
S5_STATE = 64
S5_MIN_DECAY = 1e-4
S5_DT_MIN = 1e-3
S5_DT_MAX = 1e-1
D_NA = D_MODEL // 2
NA_HEAD_DIM = 64
NA_HEADS = D_NA // NA_HEAD_DIM
NA_WIN_ROWS = 8
NA_WIN_COLS = 16
D_INNER = 2 * D_MODEL
SSD_HEAD_DIM = 64
SSD_HEADS = D_INNER // SSD_HEAD_DIM
SSD_GROUPS = 4
SSD_HEADS_PER_GROUP = SSD_HEADS // SSD_GROUPS
SSD_STATE = 128
SSD_CHUNK = 128
SSD_CONV = 5
SSD_DT_MIN = 1e-3
SSD_DT_MAX = 1e-1
SSD_CONV_DIM = D_INNER + 2 * SSD_GROUPS * SSD_STATE
SSD_IN_DIM = D_INNER + SSD_CONV_DIM + 2 * SSD_HEADS
D_FF = -(-(8 * D_MODEL) // (3 * 256)) * 256

kernel_name = 'hybrid_s5_natten_ssd_prefix_diffusion_step'


def _rmsnorm(x, g):
    x32 = x.astype(jnp.float32)
    y = x32 * lax.rsqrt(jnp.mean(x32 * x32, axis=-1, keepdims=True) + EPS)
    return (y * g.astype(jnp.float32)).astype(x.dtype)


def _adaln(cond, w, b):
    m = jax.nn.silu(cond) @ w + b
    return jnp.split(m[:, None, :], ADA_CHUNKS, axis=-1)


def _modulate(h, shift, scale):
    return h * (1 + scale) + shift


def _swiglu(h, w_gate, w_up, w_down):
    return (jax.nn.silu(h @ w_gate) * (h @ w_up)) @ w_down


def _linear_recurrence_op(left, right):
    a_l, b_l = left
    a_r, b_r = right
    return a_r * a_l, a_r * b_l + b_r


def _s5_mixer(u, a_re, a_im, log_dt, b_re, b_im, c_re, c_im, d_skip, w_glu, h0):
    bsz, seq, _ = u.shape
    u32 = u.astype(jnp.float32)
    uc = u32.reshape(bsz, seq, S5_GROUPS, S5_GROUP_CH).astype(jnp.complex64)
    a = lax.complex(jnp.minimum(a_re.astype(jnp.float32), -S5_MIN_DECAY), a_im.astype(jnp.float32))
    dt = jnp.exp(log_dt.astype(jnp.float32))[..., None]
    a_bar = jnp.exp(a * dt)
    b_bar = ((a_bar - 1) / a)[..., None] * lax.complex(b_re.astype(jnp.float32), b_im.astype(jnp.float32))
    c_mat = lax.complex(c_re.astype(jnp.float32), c_im.astype(jnp.float32))
    y = u32 * d_skip.astype(jnp.float32)
    finals = []
    for direction, reverse in ((0, False), (1, True)):
        first = seq - 1 if reverse else 0
        bu = jnp.einsum('gnp,blgp->blgn', b_bar[direction], uc)
        bu = bu.at[:, first].add(a_bar[direction] * h0[:, direction])
        a_seq = jnp.broadcast_to(a_bar[direction], bu.shape)
        _, h = lax.associative_scan(_linear_recurrence_op, (a_seq, bu), axis=1, reverse=reverse)
        y = y + jnp.einsum('gpn,blgn->blgp', c_mat[direction], h).real.reshape(bsz, seq, D_S5)
        finals.append(h[:, seq - 1 - first])
    y = jax.nn.gelu(y)
    y = y * jax.nn.sigmoid(y @ w_glu.astype(jnp.float32))
    return y.astype(u.dtype), jnp.stack(finals, axis=1)


def _context_attention(q, k, v):
    s = jnp.einsum('bhqd,bhkd->bhqk', q * q.shape[-1] ** -0.5, k).astype(jnp.float32)
    p = jax.nn.softmax(s, axis=-1).astype(v.dtype)
    return jnp.einsum('bhqk,bhkd->bhqd', p, v)


def _neighbourhood_attention(q, k, v, ctx_k, ctx_v, rpb):
    bsz, nh, seq, hd = q.shape
    rows = seq // GRID_W
    kh = min(NA_WIN_ROWS, rows)
    kw = NA_WIN_COLS
    qg = q.reshape(bsz, nh, rows, GRID_W, hd) * hd ** -0.5
    kg = k.reshape(bsz, nh, rows, GRID_W, hd)
    vg = v.reshape(bsz, nh, rows, GRID_W, hd)
    r = jnp.arange(rows)
    w = jnp.arange(GRID_W)
    row_idx = jnp.clip(r - kh // 2, 0, rows - kh)[:, None] + jnp.arange(kh)[None, :]
    k_rows = kg[:, :, row_idx]
    v_rows = vg[:, :, row_idx]
    col_start = jnp.clip(w - kw // 2, 0, GRID_W - kw)
    col_ok = (w[None, :] >= col_start[:, None]) & (w[None, :] < col_start[:, None] + kw)
    dr = row_idx - r[:, None] + (NA_WIN_ROWS - 1)
    dc = jnp.clip(w[None, :] - w[:, None], -(kw - 1), kw - 1) + (kw - 1)
    bias = rpb[:, dr[:, None, :, None], dc[None, :, None, :]].astype(jnp.float32)
    s_loc = jnp.einsum('bhrqd,bhrkwd->bhrqkw', qg, k_rows).astype(jnp.float32) + bias[None]
    s_loc = jnp.where(col_ok[:, None, :], s_loc, -jnp.inf)
    s_ctx = jnp.einsum('bhrqd,bhcd->bhrqc', qg, ctx_k).astype(jnp.float32)
    n_loc = kh * GRID_W
    s = jnp.concatenate([s_loc.reshape(bsz, nh, rows, GRID_W, n_loc), s_ctx], axis=-1)
    p = jax.nn.softmax(s, axis=-1).astype(v.dtype)
    p_loc = p[..., :n_loc].reshape(bsz, nh, rows, GRID_W, kh, GRID_W)
    out = (jnp.einsum('bhrqkw,bhrkwd->bhrqd', p_loc, v_rows)
           + jnp.einsum('bhrqc,bhcd->bhrqd', p[..., n_loc:], ctx_v))
    return out.reshape(bsz, nh, seq, hd)


def _even_mixer(h, w_in, w_out, s5_params, rpb, s5_h0, ctx_k, ctx_v):
    bsz, seq, _ = h.shape
    proj = h @ w_in
    y_s5, s5_final = _s5_mixer(proj[..., :D_S5], *s5_params, s5_h0)
    q, k, v = (proj[..., D_S5 + i * D_NA:D_S5 + (i + 1) * D_NA]
               .reshape(bsz, seq, NA_HEADS, NA_HEAD_DIM).transpose(0, 2, 1, 3) for i in range(3))
    if ctx_k is None:
        attn = _context_attention(q, k, v)
        ctx_kv = (k, v)
    else:
        attn = _neighbourhood_attention(q, k, v, ctx_k, ctx_v, rpb)
        ctx_kv = None
    attn = attn.transpose(0, 2, 1, 3).reshape(bsz, seq, D_NA)
    out = jnp.concatenate([y_s5, attn], axis=-1) @ w_out
    return out, ctx_kv, s5_final


def _depthwise_conv(x, w, b):
    width = w.shape[0]
    out = lax.conv_general_dilated(x, w[:, None, :].astype(x.dtype), window_strides=(1,),
                                   padding=[(width // 2, width // 2)],
                                   dimension_numbers=('NWC', 'WIO', 'NWC'),
                                   feature_group_count=x.shape[-1])
    return out + b


def _decay_matrix(cs):
    n = cs.shape[-1]
    diff = cs[..., :, None] - cs[..., None, :]
    tril = jnp.tril(jnp.ones((n, n), dtype=bool))
    return jnp.exp(jnp.where(tril, diff, -jnp.inf))


def _ssd_scan(x, dt, a, bmat, cmat, h0):
    bsz, seq = x.shape[:2]
    q = min(SSD_CHUNK, seq)
    nc = seq // q
    g, j, p, n = SSD_GROUPS, SSD_HEADS_PER_GROUP, SSD_HEAD_DIM, SSD_STATE
    xg = (x * dt[..., None]).reshape(bsz, nc, q, g, j, p)
    a_cs = jnp.cumsum((dt * a).reshape(bsz, nc, q, g, j).transpose(0, 3, 4, 1, 2), axis=-1)
    bc = bmat.reshape(bsz, nc, q, g, n)
    cc = cmat.reshape(bsz, nc, q, g, n)
    cb = jnp.einsum('bclgn,bcsgn->bgcls', cc, bc)
    y_diag = jnp.einsum('bgjcls,bcsgjp->bclgjp', _decay_matrix(a_cs) * cb[:, :, None], xg)
    decay_to_end = jnp.exp(a_cs[..., -1:] - a_cs)
    chunk_states = jnp.einsum('bclgn,bgjcl,bclgjp->bcgjpn', bc, decay_to_end, xg)
    chunk_states = jnp.concatenate([h0.reshape(bsz, 1, g, j, p, n), chunk_states], axis=1)
    chunk_sum = jnp.pad(a_cs[..., -1], ((0, 0), (0, 0), (0, 0), (1, 0)))
    states = jnp.einsum('bgjzc,bcgjpn->bzgjpn', _decay_matrix(jnp.cumsum(chunk_sum, axis=-1)), chunk_states)
    y_off = jnp.einsum('bclgn,bcgjpn,bgjcl->bclgjp', cc, states[:, :-1], jnp.exp(a_cs))
    y = (y_diag + y_off).reshape(bsz, seq, SSD_HEADS, p)
    return y, states[:, -1].reshape(bsz, SSD_HEADS, p, n)


def _odd_mixer(h, w_in, conv_w, conv_b, a_log, dt_bias, d_skip, norm_g, w_out, h0):
    bsz, seq, _ = h.shape
    proj = h @ w_in
    z = proj[..., :D_INNER]
    xbc = proj[..., D_INNER:D_INNER + SSD_CONV_DIM]
    dt_raw = proj[..., D_INNER + SSD_CONV_DIM:]
    xbc = jax.nn.silu(_depthwise_conv(xbc, conv_w, conv_b)).astype(jnp.float32)
    gn = SSD_GROUPS * SSD_STATE
    xs = xbc[..., :D_INNER].reshape(bsz, seq, SSD_HEADS, SSD_HEAD_DIM)
    bm = xbc[..., D_INNER:D_INNER + gn].reshape(bsz, seq, SSD_GROUPS, SSD_STATE)
    cm = xbc[..., D_INNER + gn:].reshape(bsz, seq, SSD_GROUPS, SSD_STATE)
    dt = jax.nn.softplus(dt_raw.astype(jnp.float32).reshape(bsz, seq, 2, SSD_HEADS) + dt_bias.astype(jnp.float32))
    a = -jnp.exp(a_log.astype(jnp.float32))
    y_f, fin_f = _ssd_scan(xs, dt[:, :, 0], a[0], bm, cm, h0[:, 0])
    y_b, fin_b = _ssd_scan(jnp.flip(xs, 1), jnp.flip(dt[:, :, 1], 1), a[1], jnp.flip(bm, 1), jnp.flip(cm, 1), h0[:, 1])
    y = y_f + jnp.flip(y_b, 1) + d_skip.astype(jnp.float32)[:, None] * xs
    y = y.reshape(bsz, seq, D_INNER) * jax.nn.silu(z.astype(jnp.float32))
    y = _rmsnorm(y, norm_g).astype(h.dtype)
    return y @ w_out, jnp.stack([fin_f, fin_b], axis=1)


def setup_inputs(seed: int = 0) -> dict:
    key = jax.random.key(seed)
    ks = iter(jax.random.split(key, 48))

    def nrm(shape, scale=1.0):
        return scale * jax.random.normal(next(ks), shape, jnp.float32)

    d_ev_in = D_S5 + 3 * D_NA
    d_ev_out = D_S5 + D_NA
    n_idx = jnp.arange(S5_STATE, dtype=jnp.float32)
    ssd_dt = jnp.exp(jax.random.uniform(next(ks), (N_ODD, 2, SSD_HEADS), jnp.float32,
                                        math.log(SSD_DT_MIN), math.log(SSD_DT_MAX)))
    return {
        'x_prompt': nrm((BATCH, SEQ, D_MODEL)),
        'x_sample': nrm((DEC_BATCH, DEC_SEQ, D_MODEL)),
        'cache_na_k': nrm((DEC_BATCH, N_EVEN, NA_HEADS, PAST_LEN, NA_HEAD_DIM)),
        'cache_na_v': nrm((DEC_BATCH, N_EVEN, NA_HEADS, PAST_LEN, NA_HEAD_DIM)),
        'state_s5_re': nrm((DEC_BATCH, N_EVEN, 2, S5_GROUPS, S5_STATE), 0.1),
        'state_s5_im': nrm((DEC_BATCH, N_EVEN, 2, S5_GROUPS, S5_STATE), 0.1),
        'state_ssd': nrm((DEC_BATCH, N_ODD, 2, SSD_HEADS, SSD_HEAD_DIM, SSD_STATE), 0.1),
        'c': nrm((DEC_BATCH, D_MODEL)),
        'c_ctx': nrm((D_MODEL,)),
        'norm_mix_g': 1.0 + nrm((DEPTH, D_MODEL), 0.01),
        'norm_ffn_g': 1.0 + nrm((DEPTH, D_MODEL), 0.01),
        'ada_w': nrm((DEPTH, D_MODEL, ADA_CHUNKS * D_MODEL), 0.5 * D_MODEL ** -0.5),
        'ada_b': nrm((DEPTH, ADA_CHUNKS * D_MODEL), 0.01),
        'ffn_w_gate': nrm((DEPTH, D_MODEL, D_FF), D_MODEL ** -0.5),
        'ffn_w_up': nrm((DEPTH, D_MODEL, D_FF), D_MODEL ** -0.5),
        'ffn_w_down': nrm((DEPTH, D_FF, D_MODEL), D_FF ** -0.5),
        'ev_w_in': nrm((N_EVEN, D_MODEL, d_ev_in), D_MODEL ** -0.5),
        'ev_w_out': nrm((N_EVEN, d_ev_out, D_MODEL), d_ev_out ** -0.5),
        's5_a_re': -0.5 + nrm((N_EVEN, 2, S5_GROUPS, S5_STATE), 0.01),
        's5_a_im': math.pi * n_idx + nrm((N_EVEN, 2, S5_GROUPS, S5_STATE), 0.01),
        's5_log_dt': jax.random.uniform(next(ks), (N_EVEN, 2, S5_GROUPS), jnp.float32,
                                        math.log(S5_DT_MIN), math.log(S5_DT_MAX)),
        's5_b_re': nrm((N_EVEN, 2, S5_GROUPS, S5_STATE, S5_GROUP_CH), (2 * S5_GROUP_CH) ** -0.5),
        's5_b_im': nrm((N_EVEN, 2, S5_GROUPS, S5_STATE, S5_GROUP_CH), (2 * S5_GROUP_CH) ** -0.5),
        's5_c_re': nrm((N_EVEN, 2, S5_GROUPS, S5_GROUP_CH, S5_STATE), S5_STATE ** -0.5),
        's5_c_im': nrm((N_EVEN, 2, S5_GROUPS, S5_GROUP_CH, S5_STATE), S5_STATE ** -0.5),
        's5_d': nrm((N_EVEN, D_S5)),
        's5_w_glu': nrm((N_EVEN, D_S5, D_S5), D_S5 ** -0.5),
        'na_rpb': nrm((N_EVEN, NA_HEADS, 2 * NA_WIN_ROWS - 1, 2 * NA_WIN_COLS - 1), 0.1),
        'od_w_in': nrm((N_ODD, D_MODEL, SSD_IN_DIM), D_MODEL ** -0.5),
        'od_conv_w': nrm((N_ODD, SSD_CONV, SSD_CONV_DIM), SSD_CONV ** -0.5),
        'od_conv_b': nrm((N_ODD, SSD_CONV_DIM), 0.01),
        'ssd_a_log': jnp.log(jax.random.uniform(next(ks), (N_ODD, 2, SSD_HEADS), jnp.float32, 1.0, 16.0)),
        'ssd_dt_bias': ssd_dt + jnp.log(-jnp.expm1(-ssd_dt)),
        'ssd_d': 1.0 + nrm((N_ODD, SSD_HEADS), 0.01),
        'ssd_norm_g': 1.0 + nrm((N_ODD, D_INNER), 0.01),
        'od_w_out': nrm((N_ODD, D_INNER, D_MODEL), D_INNER ** -0.5),
        'final_norm_g': 1.0 + nrm((D_MODEL,), 0.01),
    }


def reference(x_prompt, x_sample, cache_na_k, cache_na_v, state_s5_re, state_s5_im, state_ssd, c, c_ctx,
              norm_mix_g, norm_ffn_g, ada_w, ada_b, ffn_w_gate, ffn_w_up, ffn_w_down,
              ev_w_in, ev_w_out, s5_a_re, s5_a_im, s5_log_dt, s5_b_re, s5_b_im, s5_c_re, s5_c_im,
              s5_d, s5_w_glu, na_rpb, od_w_in, od_conv_w, od_conv_b, ssd_a_log, ssd_dt_bias, ssd_d,
              ssd_norm_g, od_w_out, final_norm_g):
    xp, xs = x_prompt, x_sample
    bsz_p = xp.shape[0]
    new_k, new_v, new_s5, new_ssd = [], [], [], []
    for layer in range(DEPTH):
        sh_p, sc_p, gt_p, shf_p, scf_p, gtf_p = _adaln(c_ctx[None, :], ada_w[layer], ada_b[layer])
        sh_s, sc_s, gt_s, shf_s, scf_s, gtf_s = _adaln(c, ada_w[layer], ada_b[layer])
        hp = _modulate(_rmsnorm(xp, norm_mix_g[layer]), sh_p, sc_p)
        hs = _modulate(_rmsnorm(xs, norm_mix_g[layer]), sh_s, sc_s)
        if layer % 2 == 0:
            e = layer // 2
            s5_params = (s5_a_re[e], s5_a_im[e], s5_log_dt[e], s5_b_re[e], s5_b_im[e],
                         s5_c_re[e], s5_c_im[e], s5_d[e], s5_w_glu[e])
            zero_h0 = jnp.zeros((bsz_p, 2, S5_GROUPS, S5_STATE), jnp.complex64)
            out_p, (k_ctx, v_ctx), s5_fin = _even_mixer(hp, ev_w_in[e], ev_w_out[e], s5_params, na_rpb[e],
                                                        zero_h0, None, None)
            h0_s = lax.complex(state_s5_re[:, e].astype(jnp.float32), state_s5_im[:, e].astype(jnp.float32))
            out_s, _, _ = _even_mixer(hs, ev_w_in[e], ev_w_out[e], s5_params, na_rpb[e],
                                      h0_s, cache_na_k[:, e], cache_na_v[:, e])
            new_k.append(k_ctx)
            new_v.append(v_ctx)
            new_s5.append(s5_fin)
        else:
            o = layer // 2
            ssd_params = (od_w_in[o], od_conv_w[o], od_conv_b[o], ssd_a_log[o], ssd_dt_bias[o],
                          ssd_d[o], ssd_norm_g[o], od_w_out[o])
            zero_h0 = jnp.zeros((bsz_p, 2, SSD_HEADS, SSD_HEAD_DIM, SSD_STATE), jnp.float32)
            out_p, ssd_fin = _odd_mixer(hp, *ssd_params, zero_h0)
            out_s, _ = _odd_mixer(hs, *ssd_params, state_ssd[:, o].astype(jnp.float32))
            new_ssd.append(ssd_fin)
        xp = xp + gt_p * out_p
        xs = xs + gt_s * out_s
        ffn = (ffn_w_gate[layer], ffn_w_up[layer], ffn_w_down[layer])
        xp = xp + gtf_p * _swiglu(_modulate(_rmsnorm(xp, norm_ffn_g[layer]), shf_p, scf_p), *ffn)
        xs = xs + gtf_s * _swiglu(_modulate(_rmsnorm(xs, norm_ffn_g[layer]), shf_s, scf_s), *ffn)
    y_prompt = _rmsnorm(xp, final_norm_g)
    y_sample = _rmsnorm(xs, final_norm_g)
    new_cache_na_k = jnp.stack(new_k, axis=1)
    new_cache_na_v = jnp.stack(new_v, axis=1)
    s5_all = jnp.stack(new_s5, axis=1)
    new_state_s5_re = jnp.real(s5_all)
    new_state_s5_im = jnp.imag(s5_all)
    new_state_ssd = jnp.stack(new_ssd, axis=1)
    return (y_prompt, y_sample, new_cache_na_k, new_cache_na_v, new_state_s5_re, new_state_s5_im, new_state_ssd)
```

```python
import math
from contextlib import ExitStack
import numpy as np
import concourse.bass as bass
import concourse.mybir as mybir
from concourse.bass_utils import run_bass_kernel_spmd

F32 = mybir.dt.float32
BF16 = mybir.dt.bfloat16
ALU = mybir.AluOpType
AF = mybir.ActivationFunctionType
AX = mybir.AxisListType
NDS = 40
NSW_BASE = 32
ENABLE_ATTN = True
D = 1024
DFF = 2816
EPS = 1e-6


def _key(x):
    if isinstance(x, str):
        return x
    if hasattr(x, "tensor"):
        return x.tensor.name
    return x.name


class KB:
    def __init__(self, plan=None):
        self.plan = plan
        self.needed = {e: set() for e in ["pe", "act", "dve", "pool"]}
        self.ereal = {e: 0 for e in ["pe", "act", "dve", "pool"]}
        self.omap = {e: {} for e in ["pe", "act", "dve", "pool"]}
        self.nc = nc = bass.Bass("TRN2", target_bir_lowering=False)
        self.eng = {"pe": nc.tensor, "act": nc.scalar, "dve": nc.vector, "pool": nc.gpsimd, "sp": nc.sync}
        self.esem = {e: nc.alloc_semaphore(name=f"es_{e}") for e in ["pe", "act", "dve", "pool"]}
        self.ecnt = {e: 0 for e in self.esem}
        self.dsems = [nc.alloc_semaphore(name=f"ds{i}") for i in range(NDS)]
        self.dcnt = [0] * NDS
        self.dnext = 0
        self.dnext_sw = 0
        self.waited = {}
        self.lastw = {}
        self.readers = {}
        self.nins = 0
        self.psb = [nc.alloc_psum_tensor(f"psb{i}", [128, 512], F32) for i in range(8)]
        self.psn = 0
        self.rr = 0
        self.marks = []

    def sb(self, name, shape, dt=F32):
        return self.nc.alloc_sbuf_tensor(name, list(shape), dt)

    def dram(self, name, shape, dt=F32, kind="Internal"):
        return self.nc.dram_tensor(name, list(shape), dt, kind=kind)

    def bank(self):
        b = self.psb[self.psn % 8]
        self.psn += 1
        return b

    def _wait(self, e, tok):
        sem, val, src = tok
        if src == e and e == "pe":
            return
        k = (e, sem.name)
        if self.waited.get(k, 0) >= val:
            return
        self.waited[k] = val
        if sem.name.startswith("es_"):
            src_e = sem.name[3:]
            self.needed[src_e].add(val)
            val = self.omap[src_e][val]
        self.eng[e].wait_ge(sem, val)

    def _deps(self, e, reads, writes):
        for k in reads:
            lw = self.lastw.get(k)
            if lw is not None:
                self._wait(e, lw)
        for k in writes:
            lw = self.lastw.get(k)
            if lw is not None:
                self._wait(e, lw)
            for tok in self.readers.get(k, {}).values():
                self._wait(e, tok)

    def _record(self, tok, reads, writes):
        for k in writes:
            self.lastw[k] = tok
            self.readers[k] = {}
        for k in reads:
            self.readers.setdefault(k, {})[tok[2]] = tok

    def op(self, e, fn, r=(), w=()):
        reads = [_key(x) for x in r]
        writes = [_key(x) for x in w]
        writes = writes + [k for k in reads if k.startswith("psb")]
        reads = [k for k in reads if not k.startswith("psb")]
        self._deps(e, reads, writes)
        ins = fn()
        self.ecnt[e] += 1
        if self.plan is None or self.ecnt[e] in self.plan[e]:
            self.ereal[e] += 1
            ins.then_inc(self.esem[e], 1)
            self.omap[e][self.ecnt[e]] = self.ereal[e]
        tok = (self.esem[e], self.ecnt[e], e)
        self._record(tok, reads, writes)
        self.nins += 1
        return tok

    def dma(self, out, in_, r=(), w=(), q="sp", **kw):
        reads = [_key(x) for x in r] if r else [_key(in_)]
        writes = [_key(x) for x in w] if w else [_key(out)]
        if q == "pool":
            i = NSW_BASE + self.dnext_sw
            self.dnext_sw = (self.dnext_sw + 1) % (NDS - NSW_BASE)
        else:
            i = self.dnext
            self.dnext = (i + 1) % NSW_BASE
        sem = self.dsems[i]
        if self.dcnt[i] > 0:
            self._wait(q, (sem, self.dcnt[i] * 16, f"dma{i}"))
        self._deps(q, reads, writes)
        ins = self.eng[q].dma_start(out=out, in_=in_, **kw)
        self.dcnt[i] += 1
        ins.then_inc(sem, 16)
        tok = (sem, self.dcnt[i] * 16, f"dma{i}_{self.dcnt[i]}")
        self._record(tok, reads, writes)
        self.nins += 1
        return tok

    def finish(self):
        for i in range(NDS):
            if self.dcnt[i] > 0:
                self._wait("sp", (self.dsems[i], self.dcnt[i] * 16, f"dma{i}"))
        return self.nc

    def mark(self, label):
        self.marks.append((label, dict(self.ecnt)))

    def barrier(self):
        for e in ["pe", "act", "dve", "pool", "sp"]:
            for c in self.esem:
                if self.ecnt[c] > 0:
                    self._wait(e, (self.esem[c], self.ecnt[c], "bar"))
            for i in range(NDS):
                if self.dcnt[i] > 0:
                    self._wait(e, (self.dsems[i], self.dcnt[i] * 16, "bar"))

    def ew_engine(self):
        self.rr += 1
        return ("dve", "pool")[self.rr % 2]


class Seg:
    def __init__(self, name, L, N, cond):
        self.name, self.L, self.N, self.cond = name, L, N, cond
        self.nt = L // N


def build_program(dbg=False, plan=None):
    kb = KB(plan)
    nc = kb.nc
    V, G, S, P = nc.vector, nc.gpsimd, nc.scalar, nc.tensor

    def ein(name, shape):
        return kb.dram(name, shape, F32, kind="ExternalInput").ap()

    def eout(name, shape):
        return kb.dram(name, shape, F32, kind="ExternalOutput").ap()

    x_s = ein("x_s", [4096, D])
    x_p = ein("x_p", [2, 256, D])
    cond = ein("cond", [2, D])
    ident_in = ein("ident", [128, 128])
    norm_mix_g = ein("norm_mix_g", [2, D]); norm_ffn_g = ein("norm_ffn_g", [2, D])
    ada_w = ein("ada_w", [2, D, 6 * D]); ada_b = ein("ada_b", [2, 6 * D])
    ffn_w_gate = ein("ffn_w_gate", [2, D, DFF]); ffn_w_up = ein("ffn_w_up", [2, D, DFF])
    ffn_w_down = ein("ffn_w_down", [2, DFF, D])
    ev_w_in = ein("ev_w_in", [D, 2048]); ev_w_out = ein("ev_w_out", [D, D])
    od_w_in = ein("od_w_in", [D, 5184]); od_w_out = ein("od_w_out", [2048, D])
    final_norm_g = ein("final_norm_g", [D])
    y_s = eout("y_s", [4096, D])
    y_p = eout("y_p", [2, 256, D])

    segs = [Seg("s", 4096, 512, 0), Seg("p0", 256, 256, 1), Seg("p1", 256, 256, 1)]
    xin = {"s": x_s, "p0": x_p[0], "p1": x_p[1]}
    yout = {"s": y_s, "p0": y_p[0], "p1": y_p[1]}
    xT = {sg.name: [kb.dram(f"xT{j}_{sg.name}", [D, sg.L]).ap() for j in range(2)] for sg in segs}
    mixT = {sg.name: kb.dram(f"mixT_{sg.name}", [2048, sg.L], BF16).ap() for sg in segs}

    ident = kb.sb("ident_sb", [128, 128])
    identb = kb.sb("identb", [128, 128], BF16)
    onesb = kb.sb("onesb", [128, 128], BF16)
    kb.dma(ident[:], ident_in)
    kb.op("dve", lambda: V.tensor_copy(out=identb[:], in_=ident[:]), r=[ident], w=[identb])
    kb.op("dve", lambda: V.memset(onesb[:], 1.0), w=[onesb])

    def colload(name, src_1d, n):
        t = kb.sb(name, [128, n])
        kb.dma(t[:], src_1d.rearrange("(k p) -> p k", p=128), allow_slow_non_contiguous=True)
        return t

    gmix = [colload(f"gmix{l}", norm_mix_g[l], 8) for l in range(2)]
    gffn = [colload(f"gffn{l}", norm_ffn_g[l], 8) for l in range(2)]
    adab = [colload(f"adab{l}", ada_b[l], 48) for l in range(2)]
    gfin = kb.sb("gfin", [128, D])
    kb.dma(gfin[:], final_norm_g.partition_broadcast(128))
    condT = kb.sb("condT", [128, 8, 2])
    for j in range(2):
        kb.dma(condT[:, :, j], cond[j].rearrange("(k p) -> p k", p=128), w=[condT], allow_slow_non_contiguous=True)
    scT = kb.sb("scT", [128, 8, 2], BF16)
    kb.op("act", lambda: S.activation(out=scT[:], in_=condT[:], func=AF.Silu), r=[condT], w=[scT])

    GW = 256
    wbufs = [kb.sb(f"wbuf{i}", [128, 22, GW], BF16) for i in range(2)]
    wctr = [0]

    def linear(W, KC, M, rhs, N, evac, rdeps):
        tiled = isinstance(W, tuple)
        if not tiled:
            Wv = W.rearrange("(kc p) m -> p kc m", p=128)
        for g in range(M // GW):
            wb = wbufs[wctr[0] % len(wbufs)]
            wctr[0] += 1
            if tiled:
                kb.dma(wb[:, :KC, :], W[1][g], q="sp", w=[wb])
            else:
                kb.dma(wb[:, :KC, :], Wv[:, :, g * GW:(g + 1) * GW], q="pool", w=[wb])
            for j in range(GW // 128):
                ps = kb.bank()
                for kc in range(KC):
                    kb.op("pe", lambda: P.matmul(ps[:, :N], lhsT=wb[:, kc, j * 128:(j + 1) * 128], rhs=rhs(kc),
                                                 start=(kc == 0), stop=(kc == KC - 1)), r=[wb] + rdeps, w=[ps])
                evac(g * (GW // 128) + j, ps)

    cvt_toks = []

    def to_bf16(name, W, K, M):
        KC_, G_ = K // 128, M // 256
        t = kb.dram(name, [G_, 128, KC_, 256], BF16).ap()
        for kc in range(KC_):
            if len(cvt_toks) >= 3:
                kb._wait("pool", cvt_toks[-3])
            cvt_toks.append(kb.dma(t[:, :, kc, :].rearrange("g p j -> p g j"),
                                   W[kc * 128:(kc + 1) * 128, :].rearrange("p (g j) -> p g j", g=G_), q="pool", w=[name]))
        return ("tiled", t)
    ev_w_in16 = to_bf16("evin16", ev_w_in[:, 0:1536], D, 1536)

    xw = [0]

    def more_wbufs(es, n=2):
        for _ in range(n):
            xw[0] += 1
            wbufs.append(es.enter_context(nc.sbuf_tensor(f"wbufx{xw[0]}", [128, 22, GW], BF16)))

    def less_wbufs():
        del wbufs[2:]

    modT = [kb.sb(f"modT{l}", [128, 48, 2]) for l in range(2)]
    for l in range(2):
        def ev(mc, ps, l=l):
            kb.op("dve", lambda: V.tensor_scalar(out=modT[l][:, mc, :], in0=ps[:, 0:2], scalar1=adab[l][:, mc:mc + 1],
                                                 scalar2=None, op0=ALU.add), r=[ps, adab[l]], w=[modT[l]])
        linear(ada_w[l], 8, 6 * D, lambda kc: scT[:, kc, :], 2, ev, [scT])
    Gm = [[kb.sb(f"Gm{l}_{i}", [128, 8, 2]) for i in range(2)] for l in range(2)]
    for l in range(2):
        for i, gsrc in enumerate((gmix[l], gffn[l])):
            for c in range(2):
                kb.op("dve", lambda: V.scalar_tensor_tensor(out=Gm[l][i][:, :, c], in0=modT[l][:, 24 * i + 8:24 * i + 16, c],
                                                            scalar=1.0, in1=gsrc[:], op0=ALU.add, op1=ALU.mult),
                      r=[modT[l], gsrc], w=[Gm[l][i]])

    xt = [kb.sb(f"xt{i}", [128, 8, 512]) for i in range(2)]
    hT = [kb.sb(f"hT{i}", [128, 8, 512], BF16) for i in range(2)]
    aT = kb.sb("aT", [128, 22, 512], BF16)
    sq = aT[:, 12:20, :]
    rstd = kb.sb("rstd", [128, 512])
    tmpf = kb.sb("tmpf", [128, 512])
    tctr = [0]

    def norm_a(xtile, N, sq):
        kb.op("act", lambda: S.activation(out=sq[:, :, :N], in_=xtile[:, :, :N], func=AF.Square), r=[xtile], w=[sq])

    def norm_mod(xtile, N, l, which, c, ht, sq=sq, rstd=rstd, tmpf=tmpf, part="ab", pool_only=False, scr=None):
        if "a" in part:
            norm_a(xtile, N, sq)
        if "b" not in part:
            return
        ps = kb.bank()
        for kc in range(8):
            kb.op("pe", lambda: P.matmul(ps[:, :N], lhsT=onesb[:], rhs=sq[:, kc, :N], start=(kc == 0), stop=(kc == 7)),
                  r=[onesb, sq], w=[ps])
        kb.op("act", lambda: S.activation(out=tmpf[:, :N], in_=ps[:, :N], func=AF.Sqrt, bias=EPS, scale=1.0 / D),
              r=[ps], w=[tmpf])
        kb.op("dve", lambda: V.reciprocal(out=rstd[:, :N], in_=tmpf[:, :N]), r=[tmpf], w=[rstd])
        g_, sh = Gm[l][which], modT[l]
        for kc in range(8):
            e = "pool" if pool_only else kb.ew_engine()
            E = V if e == "dve" else G
            dst = scr[kc % 2][:, :N] if scr is not None else xtile[:, kc, :N]
            dkey = scr[kc % 2] if scr is not None else xtile
            kb.op(e, lambda: E.tensor_tensor(out=dst, in0=xtile[:, kc, :N], in1=rstd[:, :N], op=ALU.mult),
                  r=[xtile, rstd], w=[dkey])
            e2 = "pool" if pool_only else "dve"
            E2 = G if pool_only else V
            kb.op(e2, lambda: E2.tensor_scalar(out=ht[:, kc, :N], in0=dst, scalar1=g_[:, kc, c:c + 1],
                                               scalar2=sh[:, 24 * which + kc, c:c + 1], op0=ALU.mult, op1=ALU.add),
                  r=[dkey, g_, sh], w=[ht])

    tok4 = kb.sb("tok4", [128, 4, D])

    def transpose_in(sg):
        for t in range(sg.nt):
            ns = sg.N // 128
            xtile = xt[tctr[0] % 2]
            tctr[0] += 1
            r0 = t * sg.N
            kb.dma(tok4[:, :ns, :], xin[sg.name][r0:r0 + sg.N, :].rearrange("(s p) d -> p s d", p=128), w=[tok4])
            for kc in range(8):
                ps = kb.bank()
                for s4 in range(ns):
                    kb.op("pe", lambda: P.transpose(ps[:, s4 * 128:(s4 + 1) * 128], tok4[:, s4, kc * 128:(kc + 1) * 128], ident[:]),
                          r=[tok4, ident], w=[ps])
                kb.op("act", lambda: S.activation(out=xtile[:, kc, :sg.N], in_=ps[:, :sg.N], func=AF.Copy), r=[ps], w=[xtile])
            kb.dma(xT[sg.name][0][:, r0:r0 + sg.N].rearrange("(kc p) n -> p kc n", p=128), xtile[:, :, :sg.N],
                   w=[f"xT0_{sg.name}:{t}"])

    for sg in segs:
        transpose_in(sg)

    def load_xT(sg, buf, t, xtile):
        r0 = t * sg.N
        kb.dma(xtile[:, :, :sg.N], xT[sg.name][buf][:, r0:r0 + sg.N].rearrange("(kc p) n -> p kc n", p=128),
               r=[f"xT{buf}_{sg.name}:{t}"], w=[xtile])

    def store_xT(sg, buf, t, xtile, q="sp"):
        r0 = t * sg.N
        kb.dma(xT[sg.name][buf][:, r0:r0 + sg.N].rearrange("(kc p) n -> p kc n", p=128), xtile[:, :, :sg.N],
               r=[xtile], w=[f"xT{buf}_{sg.name}:{t}"], q=q)

    mixt = [aT, aT]

    def phase_outproj(l, sg, W, KC, src_buf, dst_buf):
        for t in range(sg.nt):
            N = sg.N
            r0 = t * N
            xtile = xt[tctr[0] % 2]
            mt = mixt[tctr[0] % 2]
            tctr[0] += 1
            load_xT(sg, src_buf, t, xtile)
            kb.dma(mt[:, :KC, :N], mixT[sg.name][:KC * 128, r0:r0 + N].rearrange("(kc p) n -> p kc n", p=128),
                   r=[f"mixT_{sg.name}:{t}"], w=[mt])

            def ev(mc, ps):
                kb.op("dve", lambda: V.scalar_tensor_tensor(out=xtile[:, mc, :N], in0=ps[:, :N],
                                                            scalar=modT[l][:, 16 + mc, sg.cond:sg.cond + 1],
                                                            in1=xtile[:, mc, :N], op0=ALU.mult, op1=ALU.add),
                      r=[ps, modT[l], xtile], w=[xtile])
            linear(W, KC, D, lambda kc: mt[:, kc, :N], N, ev, [mt])
            store_xT(sg, dst_buf, t, xtile, q="pool")

    def phase_ffn(l, src_buf, dst_buf, final=False):
        es = ExitStack()
        sqp = es.enter_context(nc.sbuf_tensor(f"ffsq{l}", [128, 8, 512], BF16))
        rsp = es.enter_context(nc.sbuf_tensor(f"ffrs{l}", [128, 512], F32))
        tmp_ = es.enter_context(nc.sbuf_tensor(f"fftm{l}", [128, 512], F32))
        more_wbufs(es, 3)
        scr = [es.enter_context(nc.sbuf_tensor(f"ffscr{l}_{j}", [128, 512], F32)) for j in range(2)]
        tiles = [(sg, t) for sg in segs for t in range(sg.nt)]
        bufs = {}

        def prepA(i):
            sg, t = tiles[i]
            xtile = xt[tctr[0] % 2]
            ht = hT[tctr[0] % 2]
            tctr[0] += 1
            bufs[i] = (xtile, ht)
            load_xT(sg, src_buf, t, xtile)
            norm_mod(xtile, sg.N, l, 1, sg.cond, ht, sq=sqp, rstd=rsp, tmpf=tmp_, part="a")

        def prepB(i):
            sg, t = tiles[i]
            xtile, ht = bufs[i]
            norm_mod(xtile, sg.N, l, 1, sg.cond, ht, sq=sqp, rstd=rsp, tmpf=tmp_, part="b", pool_only=True, scr=scr)

        def compute(i):
            sg, t = tiles[i]
            N = sg.N
            xtile, ht = bufs[i]
            if i + 1 < len(tiles):
                prepA(i + 1)

            def ev_g(mc, ps):
                kb.op("act", lambda: S.activation(out=aT[:, mc, :N], in_=ps[:, :N], func=AF.Silu), r=[ps], w=[aT])
            linear(ffn_w_gate[l], 8, DFF, lambda kc: ht[:, kc, :N], N, ev_g, [ht])
            if i + 1 < len(tiles):
                prepB(i + 1)

            def ev_u(mc, ps):
                kb.op("dve", lambda: V.tensor_tensor(out=aT[:, mc, :N], in0=ps[:, :N], in1=aT[:, mc, :N], op=ALU.mult),
                      r=[ps, aT], w=[aT])
            linear(ffn_w_up[l], 8, DFF, lambda kc: ht[:, kc, :N], N, ev_u, [ht])

            def ev_d(mc, ps):
                kb.op("dve", lambda: V.scalar_tensor_tensor(out=xtile[:, mc, :N], in0=ps[:, :N],
                                                            scalar=modT[l][:, 40 + mc, sg.cond:sg.cond + 1],
                                                            in1=xtile[:, mc, :N], op0=ALU.mult, op1=ALU.add),
                      r=[ps, modT[l], xtile], w=[xtile])
            linear(ffn_w_down[l], 22, D, lambda kc: aT[:, kc, :N], N, ev_d, [aT])
            if not final:
                store_xT(sg, dst_buf, t, xtile, q="pool")
            else:
                final_out(sg, t, xtile)

        prepA(0)
        prepB(0)
        for i in range(len(tiles)):
            compute(i)
        kb.barrier()
        less_wbufs()
        es.close()

    ssq4 = kb.sb("ssq4", [128, 4])
    rs4 = kb.sb("rs4", [128, 4])
    junk = kb.sb("junk", [128, D], BF16)

    def final_out(sg, t, xtile):
        N = sg.N
        ns = N // 128
        r0 = t * N
        for s4 in range(ns):
            for half in range(2):
                ps = kb.bank()
                for k4 in range(4):
                    kc = half * 4 + k4
                    kb.op("pe", lambda: P.transpose(ps[:, k4 * 128:(k4 + 1) * 128], xtile[:, kc, s4 * 128:(s4 + 1) * 128], ident[:]),
                          r=[xtile, ident], w=[ps])
                kb.op("dve", lambda: V.tensor_copy(out=tok4[:, s4, half * 512:(half + 1) * 512], in_=ps[:, :]), r=[ps], w=[tok4])
            kb.op("act", lambda: S.activation(out=junk[:], in_=tok4[:, s4, :], func=AF.Square, accum_out=ssq4[:, s4:s4 + 1]),
                  r=[tok4], w=[junk, ssq4])
        kb.op("act", lambda: S.activation(out=rs4[:, :ns], in_=ssq4[:, :ns], func=AF.Sqrt, bias=EPS, scale=1.0 / D), r=[ssq4], w=[rs4])
        kb.op("dve", lambda: V.reciprocal(out=rs4[:, :ns], in_=rs4[:, :ns]), r=[rs4], w=[rs4])
        for s4 in range(ns):
            kb.op("dve", lambda: V.scalar_tensor_tensor(out=tok4[:, s4, :], in0=tok4[:, s4, :], scalar=rs4[:, s4:s4 + 1],
                                                        in1=gfin[:], op0=ALU.mult, op1=ALU.mult), r=[tok4, rs4, gfin], w=[tok4])
        kb.dma(yout[sg.name][r0:r0 + N, :].rearrange("(s p) d -> p s d", p=128), tok4[:, :ns, :], r=[tok4], w=[f"y_{sg.name}:{t}"], q="pool")

    cache_k = ein("cache_k", [8, 256, 64]); cache_v = ein("cache_v", [8, 256, 64])
    rpbg = ein("rpbg", [21, 8, 128, 128]); nmask = ein("nmask", [21, 128, 128])
    o_k = eout("o_k", [2, 8, 256, 64]); o_v = eout("o_v", [2, 8, 256, 64])
    biasd = kb.dram("biasd", [21, 128, 8, 128], BF16).ap()
    uTd = {sg.name: kb.dram(f"uTd_{sg.name}", [512, sg.L], BF16).ap() for sg in segs}
    qTd = {sg.name: kb.dram(f"qTd_{sg.name}", [512, sg.L], BF16).ap() for sg in segs}
    kTd = {sg.name: kb.dram(f"kTd_{sg.name}", [512, sg.L], BF16).ap() for sg in segs}
    Vd = {sg.name: kb.dram(f"Vd_{sg.name}", [sg.L, 520], BF16).ap() for sg in segs}
    okv = {"p0": (o_k[0], o_v[0]), "p1": (o_k[1], o_v[1])}

    es0 = ExitStack()

    def sb0(name, shape, dt=F32):
        return es0.enter_context(nc.sbuf_tensor(name, list(shape), dt))

    wkv = sb0("wkv", [128, 8, 1024], BF16)
    kb.dma(wkv[:], ev_w_in[:, 1024:2048].rearrange("(kc p) m -> p kc m", p=128), q="pool")
    stage = aT
    vst = sb0("vst", [128, 8, 65], BF16)
    kb.op("dve", lambda: V.memset(vst[:], 1.0), w=[vst])
    kvf = tok4[:, 0, :]

    def phase_evin(sg):
        for t in range(sg.nt):
            N = sg.N
            r0 = t * N
            xtile = xt[tctr[0] % 2]
            ht = hT[tctr[0] % 2]
            tctr[0] += 1
            load_xT(sg, 0, t, xtile)
            norm_mod(xtile, N, 0, 0, sg.cond, ht)

            def ev(mc, ps):
                if 4 <= mc < 8:
                    kb.op("act", lambda: S.activation(out=stage[:, mc, :N], in_=ps[:, :N], func=AF.Copy, scale=0.125), r=[ps], w=[stage])
                else:
                    kb.op("act", lambda: S.activation(out=stage[:, mc, :N], in_=ps[:, :N], func=AF.Copy), r=[ps], w=[stage])
            linear(ev_w_in16, 8, 1536, lambda kc: ht[:, kc, :N], N, ev, [ht])
            for j, dst in enumerate((uTd, qTd, kTd)):
                kb.dma(dst[sg.name][:, r0:r0 + N].rearrange("(c p) n -> p c n", p=128), stage[:, 4 * j:4 * j + 4, :N],
                       r=[stage], w=[f"{dst[sg.name].tensor.name}:{t}"], q="pool")
            for s4 in range(N // 128):
                tok0 = r0 + s4 * 128
                psv = kb.bank()
                for kc in range(8):
                    kb.op("pe", lambda: P.matmul(psv[:, :], lhsT=ht[:, kc, s4 * 128:(s4 + 1) * 128], rhs=wkv[:, kc, 512:1024],
                                                 start=(kc == 0), stop=(kc == 7)), r=[ht, wkv], w=[psv])
                kb.op("act", lambda: S.activation(out=vst[:, :, 0:64], in_=psv[:, :].rearrange("p (h d) -> p h d", h=8), func=AF.Copy),
                      r=[psv], w=[vst])
                kb.dma(Vd[sg.name][tok0:tok0 + 128, :], vst[:].rearrange("p h d -> p (h d)"), r=[vst], w=[f"Vd_{sg.name}:{tok0 // 128}"], q="pool")
                if sg.name in okv:
                    kb.op("dve", lambda: V.tensor_copy(out=kvf[:, 512:1024], in_=psv[:, :]), r=[psv], w=[tok4])
                    psk = kb.bank()
                    for kc in range(8):
                        kb.op("pe", lambda: P.matmul(psk[:, :], lhsT=ht[:, kc, s4 * 128:(s4 + 1) * 128], rhs=wkv[:, kc, 0:512],
                                                     start=(kc == 0), stop=(kc == 7)), r=[ht, wkv], w=[psk])
                    kb.op("dve", lambda: V.tensor_copy(out=kvf[:, 0:512], in_=psk[:, :]), r=[psk], w=[tok4])
                    for j in range(2):
                        kb.dma(okv[sg.name][j][:, tok0:tok0 + 128, :].rearrange("h t d -> t h d"),
                               kvf[:, j * 512:(j + 1) * 512].rearrange("p (h d) -> p h d", h=8), r=[tok4], w=[f"okv_{sg.name}_{j}:{s4}"])

    rb_f = sb0("rb_f", [128, 8, 128]); rb_m = sb0("rb_m", [128, 128]); rb_b = sb0("rb_b", [128, 8, 128], BF16)
    for ci in range(21):
        kb.dma(rb_f[:], rpbg[ci].rearrange("h q k -> q h k"))
        kb.dma(rb_m[:], nmask[ci])
        kb.op("dve", lambda: V.tensor_tensor(out=rb_b[:], in0=rb_f[:], in1=rb_m[:].unsqueeze(1).broadcast_to([128, 8, 128]), op=ALU.add),
              r=[rb_f, rb_m], w=[rb_b])
        kb.dma(biasd[ci], rb_b[:], w=[f"biasd:{ci}"])
    ckT = sb0("ckT", [128, 4, 256], BF16)
    cV = sb0("cV", [128, 2, 8, 65], BF16)
    kb.op("dve", lambda: V.memset(cV[:], 1.0), w=[cV])
    ctmp = rb_f[:].rearrange("p a b -> p (a b)")[:, 0:512].rearrange("p (h d) -> p h d", h=8)
    for tt in range(2):
        kb.dma(ctmp[:], cache_v[:, tt * 128:(tt + 1) * 128, :].rearrange("h t d -> t h d"))
        kb.op("dve", lambda: V.tensor_copy(out=cV[:, tt, :, 0:64], in_=ctmp[:]), r=[ctmp], w=[cV])
    for tt in range(2):
        kb.dma(ctmp[:], cache_k[:, tt * 128:(tt + 1) * 128, :].rearrange("h t d -> t h d"))
        ps = kb.bank()
        for hp in range(4):
            kb.op("pe", lambda: P.transpose(ps[:, hp * 128:(hp + 1) * 128], ctmp[:, 2 * hp:2 * hp + 2, :].rearrange("p h d -> p (h d)"), ident[:]),
                  r=[ctmp, ident], w=[ps])
        kb.op("dve", lambda: V.tensor_copy(out=ckT[:, :, tt * 128:(tt + 1) * 128], in_=ps[:, :].rearrange("p (c n) -> p c n", c=4)),
              r=[ps], w=[ckT])

    qm = [sb0(f"qm{i}", [128, 4, 256], BF16) for i in range(2)]
    km = [sb0(f"km{i}", [128, 4, 640], BF16) for i in range(2)]
    vm = [sb0(f"vm{i}", [128, 5, 520], BF16) for i in range(2)]
    bm = [sb0(f"bm{i}", [128, 5, 8, 128], BF16) for i in range(2)]
    PT = [sb0(f"PT{i}", [128, 1024], BF16) for i in range(2)]
    rden = sb0("rden", [128, 8])
    atok = sb0("atok", [128, 512], BF16)
    aTt = sb0("aTt", [128, 4, 128], BF16)
    actr = [0]

    def attend(sg, q0, NQ, ktiles, bias_ci0, i=None, part="both"):
        if i is None:
            i = actr[0] % 2
        actr[0] += 1
        nk = len(ktiles)
        kt0 = ktiles[0]
        name = sg.name
        if part in ("loads", "both"):
            att_loads(sg, q0, NQ, ktiles, bias_ci0, i)
        if part == "loads":
            return
        use_ctx = bias_ci0 is not None
        att_compute(sg, q0, NQ, ktiles, bias_ci0, i)

    def att_loads(sg, q0, NQ, ktiles, bias_ci0, i):
        nk = len(ktiles)
        kt0 = ktiles[0]
        name = sg.name
        kb.dma(qm[i][:, :, :NQ], qTd[name][:, q0:q0 + NQ].rearrange("(c p) n -> p c n", p=128),
               r=[f"qTd_{name}:{q0 // sg.N}"], w=[qm[i]])
        kb.dma(km[i][:, :, :nk * 128], kTd[name][:, kt0 * 128:(kt0 + nk) * 128].rearrange("(c p) n -> p c n", p=128),
               r=[f"kTd_{name}:{(kt0 * 128) // sg.N}", f"kTd_{name}:{((kt0 + nk) * 128 - 1) // sg.N}"], w=[km[i]])
        kb.dma(vm[i][:, :nk, :], Vd[name][kt0 * 128:(kt0 + nk) * 128, :].rearrange("(j p) f -> p j f", p=128),
               r=[f"Vd_{name}:{kt0 + j}" for j in range(nk)], w=[vm[i]])
        use_ctx = bias_ci0 is not None
        if use_ctx:
            kb.dma(bm[i][:, :nk], biasd[bias_ci0:bias_ci0 + nk].rearrange("j q h k -> q j h k"),
                   r=[f"biasd:{bias_ci0 + j}" for j in range(nk)], w=[bm[i]])

    def att_compute(sg, q0, NQ, ktiles, bias_ci0, i):
        nk = len(ktiles)
        kt0 = ktiles[0]
        name = sg.name
        use_ctx = bias_ci0 is not None
        nq = NQ // 128
        ntile = nk + (2 if use_ctx else 0)
        for qs in range(nq):
            poA, poB = kb.psb[6], kb.psb[7]
            for h in range(8):
                hp, off = h // 2, (h % 2) * 64
                pt = PT[(actr[0] + h) % 2]
                banks = []
                for j in range(ntile):
                    if j % 4 == 0:
                        banks.append(kb.psb[kb.psn % 6]); kb.psn += 1
                    psS = banks[-1]
                    cs = slice((j % 4) * 128, (j % 4 + 1) * 128)
                    rq = qm[i][off:off + 64, hp, qs * 128:(qs + 1) * 128]
                    if j < nk:
                        kb.op("pe", lambda: P.matmul(psS[:, cs], lhsT=km[i][off:off + 64, hp, j * 128:(j + 1) * 128], rhs=rq,
                                                     start=True, stop=not use_ctx), r=[km[i], qm[i]], w=[psS])
                        if use_ctx:
                            kb.op("pe", lambda: P.matmul(psS[:, cs], lhsT=bm[i][:, j, h, :], rhs=identb[:], start=False, stop=True),
                                  r=[bm[i], identb], w=[psS])
                    else:
                        c = j - nk
                        kb.op("pe", lambda: P.matmul(psS[:, cs], lhsT=ckT[off:off + 64, hp, c * 128:(c + 1) * 128], rhs=rq,
                                                     start=True, stop=True), r=[ckT, qm[i]], w=[psS])
                for bi, psS in enumerate(banks):
                    w_ = min(4, ntile - bi * 4) * 128
                    kb.op("act", lambda: S.activation(out=pt[:, bi * 512:bi * 512 + w_], in_=psS[:, :w_], func=AF.Exp), r=[psS], w=[pt])
                po = poA if h < 4 else poB
                oc = slice((h % 4) * 65, (h % 4) * 65 + 65)
                for j in range(ntile):
                    rv = vm[i][:, j, h * 65:(h + 1) * 65] if j < nk else cV[:, j - nk, h, :]
                    kb.op("pe", lambda: P.matmul(po[:, oc], lhsT=pt[:, j * 128:(j + 1) * 128], rhs=rv, start=(j == 0), stop=(j == ntile - 1)),
                          r=[pt, vm[i], cV], w=[po])
            for hb, po in enumerate((poA, poB)):
                pv = po[:, 0:260].rearrange("p (h d) -> p h d", h=4)
                kb.op("dve", lambda: V.reciprocal(out=rden[:, hb * 4:hb * 4 + 4], in_=pv[:, :, 64]), r=[po], w=[rden])
                kb.op("dve", lambda: V.tensor_tensor(out=atok[:, hb * 256:(hb + 1) * 256].rearrange("p (h d) -> p h d", h=4), in0=pv[:, :, 0:64],
                                                     in1=rden[:, hb * 4:hb * 4 + 4].unsqueeze(2).broadcast_to([128, 4, 64]), op=ALU.mult),
                      r=[po, rden], w=[atok])
            pst = kb.psb[kb.psn % 6]; kb.psn += 1
            pstb = pst[:].bitcast(BF16)
            for c4 in range(4):
                kb.op("pe", lambda: P.transpose(pstb[:, c4 * 128:(c4 + 1) * 128], atok[:, c4 * 128:(c4 + 1) * 128], identb[:]),
                      r=[atok, identb], w=[pst])
            kb.op("dve", lambda: V.tensor_copy(out=aTt[:], in_=pstb[:, 0:512].rearrange("p (c n) -> p c n", c=4)), r=[pst], w=[aTt])
            qq = q0 + qs * 128
            kb.dma(mixT[name][512:1024, qq:qq + 128].rearrange("(c p) n -> p c n", p=128), aTt[:], r=[aTt],
                   w=[f"mixT_{name}:{qq // sg.N}"])

    def lo(r):
        return min(max(r - 4, 0), 56)

    def phase_attn(sg):
        if sg.name != "s":
            attend(sg, 0, 256, [0, 1], None)
            return
        def args(m):
            lt, ht_ = lo(2 * m) // 2, (lo(2 * m + 1) + 7) // 2
            cls = {0: 5, 1: 9, 30: 13, 31: 17}.get(m, 0)
            return (sg, m * 128, 128, list(range(lt, ht_ + 1)), cls)
        attend(*args(0), i=0, part="loads")
        for m in range(32):
            if m + 1 < 32:
                attend(*args(m + 1), i=(m + 1) % 2, part="loads")
            attend(*args(m), i=m % 2, part="compute")


    s5_a = ein("s5_a", [128, 3, 32])
    s5_B = ein("s5_B", [128, 2, 32, 16])
    s5_C = ein("s5_C", [128, 2, 32, 16])
    s5_h0 = ein("s5_h0", [128, 2, 32])
    s5_dcol = ein("s5_dcol", [128, 4])
    s5_w_glu = ein("s5_w_glu", [512, 512])
    o_s5 = eout("o_s5", [2, 2, 128, 32])
    tabd = kb.dram("tabd", [32, 128, 2, 512], BF16).ap()
    yaccd = {sg.name: kb.dram(f"yaccd_{sg.name}", [512, sg.L]).ap() for sg in segs}
    ygd = {sg.name: kb.dram(f"ygd_{sg.name}", [512, sg.L], BF16).ap() for sg in segs}
    PI = math.pi

    def phase_s5():
        es = ExitStack()

        def sb1(name, shape, dt=F32):
            return es.enter_context(nc.sbuf_tensor(name, list(shape), dt))

        es2 = ExitStack()

        def sb2(name, shape, dt=F32):
            return es2.enter_context(nc.sbuf_tensor(name, list(shape), dt))

        pa = sb1("s5pa", [128, 3, 32]); ph0 = sb1("s5ph0", [128, 2, 32]); dcol = sb1("s5dcol", [128, 4])
        sm = sb1("s5sm", [128, 16, 32])
        LB = sb1("s5LB", [128, 64, 128], BF16)
        CM = sb1("s5CM", [128, 64, 128], BF16)
        ET = sb1("s5ET", [128, 2, 32])
        E1 = sb1("s5E1", [128, 2, 32])
        tbl = [sb1(f"s5tl{i}", [128, 2, 512], BF16) for i in range(3)]
        tlast = sb1("s5tlast", [128, 2, 32])
        pB = sb2("s5pB", [128, 2, 32, 16]); pC = sb2("s5pC", [128, 2, 32, 16])
        kb.dma(pa[:], s5_a); kb.dma(pB[:], s5_B); kb.dma(pC[:], s5_C); kb.dma(ph0[:], s5_h0); kb.dma(dcol[:], s5_dcol)
        ARE, DT, LRE, TH, RR, CS, SN, ABR, ABI, DEN, WRE, WIM, T0, T1, T2, T3 = [sm[:, i, :] for i in range(16)]

        def vts(out, in0, s1, s2, op0, op1=None):
            if op1 is None:
                kb.op("dve", lambda: V.tensor_scalar(out=out, in0=in0, scalar1=s1, scalar2=None, op0=op0), r=[in0], w=[out])
            else:
                kb.op("dve", lambda: V.tensor_scalar(out=out, in0=in0, scalar1=s1, scalar2=s2, op0=op0, op1=op1), r=[in0], w=[out])

        def vtt(out, a, b, op, e="dve"):
            E = V if e == "dve" else G
            kb.op(e, lambda: E.tensor_tensor(out=out, in0=a, in1=b, op=op), r=[a, b], w=[out])

        vts(ARE, pa[:, 0, :], -1e-4, None, ALU.min)
        kb.op("act", lambda: S.activation(out=DT, in_=pa[:, 2, :], func=AF.Exp), r=[pa], w=[sm])
        vtt(LRE, ARE, DT, ALU.mult)
        vtt(TH, pa[:, 1, :], DT, ALU.mult)
        kb.op("act", lambda: S.activation(out=RR, in_=LRE, func=AF.Exp), r=[sm], w=[sm])

        def sincos(out, shift):
            vts(T0, TH, shift, None, ALU.add)
            vts(T1, T0, 0.0, None, ALU.add)
            for j in range(1, 6):
                vts(T2, T0, (2 * j - 1) * PI, -2 * PI, ALU.is_ge, ALU.mult)
                vtt(T1, T1, T2, ALU.add)
            kb.op("act", lambda: S.activation(out=out, in_=T1, func=AF.Sin), r=[sm], w=[sm])
        sincos(SN, 0.0)
        sincos(CS, PI / 2)
        vtt(ABR, RR, CS, ALU.mult); vtt(ABI, RR, SN, ALU.mult)
        vtt(T0, ARE, ARE, ALU.mult); vtt(T1, pa[:, 1, :], pa[:, 1, :], ALU.mult); vtt(DEN, T0, T1, ALU.add)
        kb.op("dve", lambda: V.reciprocal(out=DEN, in_=DEN), r=[sm], w=[sm])
        vts(T3, ABR, -1.0, None, ALU.add)
        vtt(T0, T3, ARE, ALU.mult); vtt(T1, ABI, pa[:, 1, :], ALU.mult); vtt(T0, T0, T1, ALU.add); vtt(WRE, T0, DEN, ALU.mult)
        vtt(T0, ABI, ARE, ALU.mult); vtt(T1, T3, pa[:, 1, :], ALU.mult); vtt(T0, T0, T1, ALU.subtract); vtt(WIM, T0, DEN, ALU.mult)
        Bb = sb2("s5Bb", [128, 2, 32, 16])
        tB = sb2("s5tB", [128, 32, 16])
        bc = lambda x: x.unsqueeze(2).broadcast_to([128, 32, 16])
        vtt(Bb[:, 0], pB[:, 0], bc(WRE), ALU.mult); vtt(tB[:], pB[:, 1], bc(WIM), ALU.mult); vtt(Bb[:, 0], Bb[:, 0], tB[:], ALU.subtract)
        vtt(Bb[:, 1], pB[:, 1], bc(WRE), ALU.mult); vtt(tB[:], pB[:, 0], bc(WIM), ALU.mult); vtt(Bb[:, 1], Bb[:, 1], tB[:], ALU.add)
        kb.op("dve", lambda: V.memset(CM[:], 0.0), w=[CM])
        Z = sb2("s5Z", [128, 128])
        for db in range(32):
            b = db % 16
            base = 32 * (b % 4)
            for part in range(2):
                kb.op("dve", lambda: V.memset(Z[:], 0.0), w=[Z])
                for gi in range(2):
                    ps_ = slice(64 * gi, 64 * gi + 64)
                    cs_ = slice(base + 16 * gi, base + 16 * gi + 16)
                    kb.op("dve", lambda: V.tensor_copy(out=Z[ps_, cs_], in_=Bb[ps_, part, db, :]), r=[Bb], w=[Z])
                    if part == 0:
                        kb.op("pool", lambda: G.tensor_copy(out=CM[ps_, 2 * db, cs_], in_=pC[ps_, 0, db, :]), r=[pC], w=[CM])
                    else:
                        kb.op("pool", lambda: G.tensor_scalar(out=CM[ps_, 2 * db + 1, cs_], in0=pC[ps_, 1, db, :], scalar1=-1.0, scalar2=None,
                                                              op0=ALU.mult), r=[pC], w=[CM])
                ps = kb.bank()
                kb.op("pe", lambda: P.transpose(ps[:, 0:128], Z[:], ident[:]), r=[Z, ident], w=[ps])
                kb.op("act", lambda: S.activation(out=LB[:, 2 * db + part, :], in_=ps[:, 0:128], func=AF.Copy), r=[ps], w=[LB])
        kb.op("dve", lambda: V.tensor_copy(out=E1[:, 0, :], in_=CS), r=[sm], w=[E1])
        kb.op("dve", lambda: V.tensor_copy(out=E1[:, 1, :], in_=SN), r=[sm], w=[E1])
        ckA = sb2("s5ckA", [128, 2, 32, 11]); sq1 = sb2("s5sq1", [128, 3, 32])
        kb.op("dve", lambda: V.tensor_copy(out=ckA[:, 0, :, 0], in_=CS), r=[sm], w=[ckA])
        kb.op("dve", lambda: V.tensor_copy(out=ckA[:, 1, :, 0], in_=SN), r=[sm], w=[ckA])
        for k in range(9):
            c_k, s_k = ckA[:, 0, :, k], ckA[:, 1, :, k]
            vtt(sq1[:, 0, :], c_k, c_k, ALU.mult); vtt(sq1[:, 1, :], s_k, s_k, ALU.mult)
            vtt(ckA[:, 0, :, k + 1], sq1[:, 0, :], sq1[:, 1, :], ALU.subtract)
            vtt(sq1[:, 2, :], c_k, s_k, ALU.mult)
            vts(ckA[:, 1, :, k + 1], sq1[:, 2, :], 2.0, None, ALU.mult)
        kb.op("dve", lambda: V.tensor_copy(out=ET[:, 0, :], in_=ckA[:, 0, :, 9]), r=[ckA], w=[ET])
        kb.op("dve", lambda: V.tensor_copy(out=ET[:, 1, :], in_=ckA[:, 1, :, 9]), r=[ckA], w=[ET])
        NBT = 2
        tabB = sb2("s5tabB", [128, 2, NBT, 512]); tmA = sb2("s5tmA", [128, NBT, 256]); tmB = sb2("s5tmB", [128, NBT, 256])
        tab16B = sb2("s5tab16B", [128, 2, NBT, 512], BF16)
        for b0 in range(0, 32, NBT):
            kb.op("dve", lambda: V.memset(tabB[:, 0, :, 0:1], 1.0), w=[tabB])
            kb.op("dve", lambda: V.memset(tabB[:, 1, :, 0:1], 0.0), w=[tabB])
            for k in range(9):
                n = 1 << k
                ckc = ckA[:, 0, b0:b0 + NBT, k:k + 1].broadcast_to([128, NBT, n])
                cks = ckA[:, 1, b0:b0 + NBT, k:k + 1].broadcast_to([128, NBT, n])
                sc, ss = tabB[:, 0, :, 0:n], tabB[:, 1, :, 0:n]
                vtt(tmA[:, :, :n], ss, cks, ALU.mult, "pool"); vtt(tmB[:, :, :n], sc, ckc, ALU.mult)
                kb.op("dve", lambda: V.tensor_tensor(out=tabB[:, 0, :, n:2 * n], in0=tmB[:, :, :n], in1=tmA[:, :, :n], op=ALU.subtract),
                      r=[tmA, tmB], w=[tabB])
                vtt(tmA[:, :, :n], ss, ckc, ALU.mult, "pool"); vtt(tmB[:, :, :n], sc, cks, ALU.mult)
                kb.op("dve", lambda: V.tensor_tensor(out=tabB[:, 1, :, n:2 * n], in0=tmB[:, :, :n], in1=tmA[:, :, :n], op=ALU.add),
                      r=[tmA, tmB], w=[tabB])
            kb.op("act", lambda: S.activation(out=tab16B[:], in_=tabB[:], func=AF.Copy), r=[tabB], w=[tab16B])
            kb.op("pool", lambda: G.tensor_copy(out=tlast[:, :, b0:b0 + NBT], in_=tabB[:, :, :, 255]), r=[tabB], w=[tlast])
            kb.dma(tabd[b0:b0 + NBT].rearrange("b p c n -> p c b n"), tab16B[:], r=[tab16B], w=[f"tabd:{b0 + j}" for j in range(NBT)])

        kb.barrier()
        es2.close()
        uTt = [sb1(f"s5u{i}", [128, 4, 512], BF16) for i in range(2)]
        w1s = [sb1("s5w1", [128, 512], BF16)] * 2; w2s = [sb1("s5w2", [128, 512], BF16)] * 2
        w3s = [sb1("s5w3", [128, 512], BF16)] * 2; w4s = [sb1("s5w4", [128, 512], BF16)] * 2
        bt1s = [sb1(f"s5bt1{i}", [128, 512]) for i in range(2)]; bt3s = [sb1(f"s5bt3{i}", [128, 512]) for i in range(2)]
        negs = sb1("s5negs", [128, 2, 32])
        kb.op("dve", lambda: V.tensor_scalar(out=negs[:, 0, :], in0=ET[:, 1, :], scalar1=-1.0, scalar2=None, op0=ALU.mult), r=[ET], w=[negs])
        kb.op("dve", lambda: V.tensor_scalar(out=negs[:, 1, :], in0=tlast[:, 1, :], scalar1=-1.0, scalar2=None, op0=ALU.mult), r=[tlast], w=[negs])
        hres = [sb1("s5hre", [128, 512])]; hims = [sb1("s5him", [128, 512])]
        w2, w4 = w2s[0], w4s[0]
        hbr = [sb1(f"s5hbr{i}", [128, 512], BF16) for i in range(2)]; hbi = [sb1(f"s5hbi{i}", [128, 512], BF16) for i in range(2)]
        bres = [sb1("s5bre", [128, 512], BF16)] * 2; bims = [sb1("s5bim", [128, 512], BF16)] * 2
        hcr = [sb1(f"s5hcr{i}", [128, 512], BF16) for i in range(2)]; hci = [sb1(f"s5hci{i}", [128, 512], BF16) for i in range(2)]
        p1s = [sb1("s5p1", [128, 512], BF16)] * 2; p2s = [sb1("s5p2", [128, 512], BF16)] * 2
        q1s = [sb1("s5q1", [128, 512], BF16)] * 2; q2s = [sb1("s5q2", [128, 512], BF16)] * 2
        carry = sb1("s5carry", [128, 2, 32]); ctmp2 = sb1("s5ct", [128, 4])
        fin = sb1("s5fin", [128, 2, 32])
        yv = sb1("s5yv", [128, 512]); ya = sb1("s5ya", [128, 512]); yb16 = sb1("s5yb", [128, 512], BF16)
        utc = [0]
        cnt = [0]

        def run(sg, d, pj):
            TT = min(512, sg.L)
            ntt = sg.L // TT
            order = list(range(ntt)) if d == 0 else list(range(ntt - 1, -1, -1))
            rv = (lambda ap: ap) if d == 0 else (lambda ap: ap[:, ::-1])
            dsl = slice(16 * d, 16 * d + 16)
            if sg.name == "s":
                vtt(T0[:, dsl], ph0[:, 0, dsl], CS[:, dsl], ALU.mult); vtt(T1[:, dsl], ph0[:, 1, dsl], SN[:, dsl], ALU.mult)
                vtt(carry[:, 0, dsl], T0[:, dsl], T1[:, dsl], ALU.subtract)
                vtt(T0[:, dsl], ph0[:, 0, dsl], SN[:, dsl], ALU.mult); vtt(T1[:, dsl], ph0[:, 1, dsl], CS[:, dsl], ALU.mult)
                vtt(carry[:, 1, dsl], T0[:, dsl], T1[:, dsl], ALU.add)
            else:
                kb.op("dve", lambda: V.memset(carry[:, :, dsl], 0.0), w=[carry])
            blocks = []
            for ti, tt in enumerate(order):
                for c in range(4):
                    for bi in range(4):
                        blocks.append(dict(ti=ti, tt=tt, c=c, bi=bi))
            psy = kb.psb[6:8]

            pending = []

            def stage1(x):
                while pending:
                    pending.pop(0)()
                cnt[0] += 1
                k = cnt[0]
                x["k"] = k
                ti, tt, c, bi = x["ti"], x["tt"], x["c"], x["bi"]
                t0 = tt * TT
                if c == 0 and bi == 0:
                    utc[0] += 1
                    ut = uTt[utc[0] % 2]
                    kb.dma(ut[:, :, :TT], uTd[sg.name][:, t0:t0 + TT].rearrange("(c p) n -> p c n", p=128),
                           r=[f"uTd_{sg.name}:{j}" for j in range(t0 // sg.N, (t0 + TT - 1) // sg.N + 1)], w=[ut])
                ut = uTt[utc[0] % 2]
                x["ut"] = ut
                db = 16 * d + 4 * c + bi
                x["db"] = db
                tl = tbl[k % 3]
                x["tl"] = tl
                kb.dma(tl[:, :, :TT], tabd[db][:, :, :TT], r=[f"tabd:{db}"], w=[tl])
                cs_, sn_ = rv(tl[:, 0, :TT]), rv(tl[:, 1, :TT])
                w1, w2, w3, w4 = (z[k % 2] for z in (w1s, w2s, w3s, w4s))
                bt1, bt3 = bt1s[k % 2], bt3s[k % 2]
                x["w"] = (bt1, bt3)
                pss = []
                for part in range(2):
                    ps = kb.psb[kb.psn % 6]; kb.psn += 1
                    kb.op("pe", lambda: P.matmul(ps[:, :TT], lhsT=LB[:, 2 * db + part, :], rhs=ut[:, c, :TT], start=True, stop=True),
                          r=[LB, ut], w=[ps])
                    pss.append(ps)
                bre, bim = bres[0], bims[0]
                kb.op("act", lambda: S.activation(out=bre[:, :TT], in_=pss[0][:, :TT], func=AF.Copy), r=[pss[0]], w=[bre])
                kb.op("act", lambda: S.activation(out=bim[:, :TT], in_=pss[1][:, :TT], func=AF.Copy), r=[pss[1]], w=[bim])
                vtt(w1[:, :TT], bre[:, :TT], cs_, ALU.mult); vtt(w2[:, :TT], bim[:, :TT], sn_, ALU.mult)
                vtt(w3[:, :TT], bim[:, :TT], cs_, ALU.mult); vtt(w4[:, :TT], bre[:, :TT], sn_, ALU.mult)
                vtt(bt1[:, :TT], w1[:, :TT], w2[:, :TT], ALU.add, "pool")
                vtt(bt3[:, :TT], w3[:, :TT], w4[:, :TT], ALU.subtract, "pool")

            def cplx_act(o_re, o_im, o_t, x_re, x_im, xr_t, xi_t, c_, s_, ns_):
                kb.op("act", lambda: S.activation(out=ctmp2[:, 0:1], in_=x_im, func=AF.Identity, scale=ns_), r=[xi_t, negs], w=[ctmp2])
                kb.op("act", lambda: S.activation(out=o_re, in_=x_re, func=AF.Identity, scale=c_, bias=ctmp2[:, 0:1]), r=[xr_t, ctmp2, ET, tlast], w=[o_t])
                kb.op("act", lambda: S.activation(out=ctmp2[:, 1:2], in_=x_re, func=AF.Identity, scale=s_), r=[xr_t, ET, tlast], w=[ctmp2])
                kb.op("act", lambda: S.activation(out=o_im, in_=x_im, func=AF.Identity, scale=c_, bias=ctmp2[:, 1:2]), r=[xi_t, ctmp2, ET, tlast], w=[o_t])

            def stage2(x):
                k, db, ti = x["k"], x["db"], x["ti"]
                w1, w3 = x["w"]
                hre, him = hres[0], hims[0]
                hbre, hbim = hcr[k % 2], hci[k % 2]
                x["hb"] = (hbre, hbim)
                rbc = RR[:, db:db + 1].broadcast_to([128, TT])
                kb.op("dve", lambda: V.tensor_tensor_scan(out=rv(hre[:, :TT]), data0=rbc, data1=rv(w1[:, :TT]), initial=carry[:, 0, db:db + 1],
                                                          op0=ALU.mult, op1=ALU.add), r=[sm, w1, carry], w=[hre])
                kb.op("dve", lambda: V.tensor_tensor_scan(out=rv(him[:, :TT]), data0=rbc, data1=rv(w3[:, :TT]), initial=carry[:, 1, db:db + 1],
                                                          op0=ALU.mult, op1=ALU.add), r=[sm, w3, carry], w=[him])
                kb.op("act", lambda: S.activation(out=hbre[:, :TT], in_=hre[:, :TT], func=AF.Copy), r=[hre], w=[hbre])
                kb.op("act", lambda: S.activation(out=hbim[:, :TT], in_=him[:, :TT], func=AF.Copy), r=[him], w=[hbim])
                lastc = TT - 1 if d == 0 else 0
                hl_re, hl_im = hre[:, lastc:lastc + 1], him[:, lastc:lastc + 1]
                if ti < len(order) - 1:
                    cplx_act(carry[:, 0, db:db + 1], carry[:, 1, db:db + 1], carry, hl_re, hl_im, hre, him,
                             ET[:, 0, db:db + 1], ET[:, 1, db:db + 1], negs[:, 0, db:db + 1])
                elif pj is not None:
                    assert TT == 256
                    cl, sl = tlast[:, 0, db:db + 1], tlast[:, 1, db:db + 1]
                    cplx_act(fin[:, 0, db:db + 1], fin[:, 1, db:db + 1], fin, hl_re, hl_im, hre, him, cl, sl, negs[:, 1, db:db + 1])

            def stage3(x):
                k, db, c, bi, tt, tl, ut = x["k"], x["db"], x["c"], x["bi"], x["tt"], x["tl"], x["ut"]
                t0 = tt * TT
                hbre, hbim = x["hb"]
                cosf, sinf = rv(tl[:, 0, :TT]), rv(tl[:, 1, :TT])
                hb_r, hb_i = hbr[k % 2], hbi[k % 2]
                p1, p2, q1, q2 = p1s[0], p2s[0], q1s[0], q2s[0]
                vtt(p1[:, :TT], hbre[:, :TT], cosf, ALU.mult, "pool"); vtt(p2[:, :TT], hbim[:, :TT], sinf, ALU.mult)
                vtt(hb_r[:, :TT], p1[:, :TT], p2[:, :TT], ALU.subtract, "pool")
                vtt(q1[:, :TT], hbre[:, :TT], sinf, ALU.mult); vtt(q2[:, :TT], hbim[:, :TT], cosf, ALU.mult)
                vtt(hb_i[:, :TT], q1[:, :TT], q2[:, :TT], ALU.add)
                py = psy[c % 2]
                kb.op("pe", lambda: P.matmul(py[:, :TT], lhsT=CM[:, 2 * db, :], rhs=hb_r[:, :TT], start=(bi == 0), stop=False), r=[CM, hb_r], w=[py])
                kb.op("pe", lambda: P.matmul(py[:, :TT], lhsT=CM[:, 2 * db + 1, :], rhs=hb_i[:, :TT], start=False, stop=(bi == 3)), r=[CM, hb_i], w=[py])
                if bi != 3:
                    return
                key = f"yacc_{sg.name}:{tt}:{c}"
                if d == 0:
                    kb.op("dve", lambda: V.scalar_tensor_tensor(out=yv[:, :TT], in0=ut[:, c, :TT], scalar=dcol[:, c:c + 1], in1=py[:, :TT],
                                                                op0=ALU.mult, op1=ALU.add), r=[ut, dcol, py], w=[yv])
                    pending.append(lambda: kb.dma(yaccd[sg.name][c * 128:(c + 1) * 128, t0:t0 + TT], yv[:, :TT], r=[yv], w=[key]))
                else:
                    kb.dma(ya[:, :TT], yaccd[sg.name][c * 128:(c + 1) * 128, t0:t0 + TT], r=[key], w=[ya])
                    kb.op("dve", lambda: V.tensor_tensor(out=yv[:, :TT], in0=py[:, :TT], in1=ya[:, :TT], op=ALU.add), r=[py, ya], w=[yv])
                    vtt(ya[:, :TT], yv[:, :TT], yv[:, :TT], ALU.mult, "pool")
                    vts(ya[:, :TT], ya[:, :TT], 0.044715, 1.0, ALU.mult, ALU.add)
                    vtt(ya[:, :TT], ya[:, :TT], yv[:, :TT], ALU.mult, "pool")
                    kb.op("act", lambda: S.activation(out=ya[:, :TT], in_=ya[:, :TT], func=AF.Sigmoid, scale=1.5957691216057308), r=[ya], w=[ya])
                    vtt(yb16[:, :TT], ya[:, :TT], yv[:, :TT], ALU.mult)
                    pending.append(lambda: kb.dma(ygd[sg.name][c * 128:(c + 1) * 128, t0:t0 + TT], yb16[:, :TT], r=[yb16],
                                                  w=[f"ygd_{sg.name}:{tt}:{c}"]))

            n = len(blocks)
            for step in range(n + 2):
                if step < n:
                    stage1(blocks[step])
                if 0 <= step - 1 < n:
                    stage2(blocks[step - 1])
                if 0 <= step - 2 < n:
                    stage3(blocks[step - 2])
            while pending:
                pending.pop(0)()
            if pj is not None and d == 1:
                for part in range(2):
                    kb.dma(o_s5[pj, part], fin[:, part, :], r=[fin], w=[f"o_s5:{pj}:{part}"])

        for sg, pj in ((segs[0], None), (segs[1], 0), (segs[2], 1)):
            run(sg, 0, pj)
            run(sg, 1, pj)

        ygt = [hT[0], hT[1]]
        for sg in segs:
            TT = min(512, sg.L)
            for t in range(sg.nt):
                N = sg.N
                r0 = t * N
                yt = ygt[t % 2]
                kb.dma(yt[:, 0:4, :N], ygd[sg.name][:, r0:r0 + N].rearrange("(c p) n -> p c n", p=128),
                       r=[f"ygd_{sg.name}:{r0 // TT}:{c}" for c in range(4)], w=[yt])

                def ev(mc, ps):
                    kb.op("act", lambda: S.activation(out=w4[:, :N], in_=ps[:, :N], func=AF.Sigmoid), r=[ps], w=[w4])
                    kb.op("dve", lambda: V.tensor_tensor(out=aT[:, mc, :N], in0=w4[:, :N], in1=yt[:, mc, :N], op=ALU.mult), r=[w4, yt], w=[aT])
                linear(s5_w_glu, 4, 512, lambda kc: yt[:, kc, :N], N, ev, [yt])
                kb.dma(mixT[sg.name][0:512, r0:r0 + N].rearrange("(c p) n -> p c n", p=128), aT[:, 0:4, :N], r=[aT], w=[f"mixT_{sg.name}:{t}"])
        kb.barrier()
        es.close()


    od_conv_w = ein("od_conv_w", [5, 3072]); od_conv_b = ein("od_conv_b", [3072])
    ssd_alog = ein("ssd_a_log", [64]); ssd_dtb = ein("ssd_dt_bias", [64])
    ssd_dsk = ein("ssd_d", [32]); ssd_norm_g = ein("ssd_norm_g", [2048])
    ssd_h0 = ein("ssd_h0", [2, 2048, 128])
    tri_in = ein("tri", [2, 128, 128]); snm_in = ein("ssd_nmask", [2, 128, 128]); ones_in = ein("ones_f", [128, 128])
    o_ssd = eout("o_ssd", [2, 2, 2048, 128])
    zsd = {sg.name: kb.dram(f"zsd_{sg.name}", [2048, sg.L], BF16).ap() for sg in segs}
    xbcd = {sg.name: kb.dram(f"xbcd_{sg.name}", [3072, sg.L], BF16).ap() for sg in segs}
    dtd = {sg.name: kb.dram(f"dtd_{sg.name}", [sg.L, 64]).ap() for sg in segs}
    xtd = {sg.name: kb.dram(f"xtd_{sg.name}", [sg.L, 2048], BF16).ap() for sg in segs}
    bcTd = {sg.name: kb.dram(f"bcTd_{sg.name}", [1024, sg.L], BF16).ap() for sg in segs}
    btd = {sg.name: kb.dram(f"btd_{sg.name}", [sg.L, 512], BF16).ap() for sg in segs}
    yfd = {sg.name: kb.dram(f"yfd_{sg.name}", [sg.L, 2048]).ap() for sg in segs}
    yTd = {sg.name: kb.dram(f"yTd_{sg.name}", [2048, sg.L], BF16).ap() for sg in segs}

    def phase_odin():
        es = ExitStack()
        sb1 = lambda name, shape, dt=F32: es.enter_context(nc.sbuf_tensor(name, list(shape), dt))
        wdt = sb1("o1wdt", [128, 8, 64], BF16)
        kb.dma(wdt[:], od_w_in[:, 5120:5184].rearrange("(kc p) m -> p kc m", p=128), q="pool")
        dtb = sb1("o1dtb", [128, 64])
        kb.dma(dtb[:], ssd_dtb.partition_broadcast(128))
        xst = [sb1(f"o1xst{i}", [128, 512], BF16) for i in range(3)]
        dtt = sb1("o1dtt", [128, 64]); dte = sb1("o1dte", [128, 64])
        xc = [0]
        sqp = sb1("o1sq", [128, 8, 512], BF16); rsp = sb1("o1rs", [128, 512]); tmp_ = sb1("o1tm", [128, 512])
        more_wbufs(es, 3)
        tiles = [(sg, t) for sg in segs for t in range(sg.nt)]
        bufs = {}

        def prep(i):
            sg, t = tiles[i]
            xtile = xt[tctr[0] % 2]
            ht = hT[tctr[0] % 2]
            tctr[0] += 1
            bufs[i] = ht
            load_xT(sg, 0, t, xtile)
            norm_mod(xtile, sg.N, 1, 0, sg.cond, ht, sq=sqp, rstd=rsp, tmpf=tmp_)

        prep(0)
        for i in range(len(tiles)):
            if i + 1 < len(tiles):
                prep(i + 1)
            if True:
                sg, t = tiles[i]
                N = sg.N
                r0 = t * N
                ht = bufs.pop(i)

                def ev(mc, ps):
                    if mc < 16:
                        kb.op("act", lambda: S.activation(out=aT[:, mc, :N], in_=ps[:, :N], func=AF.Silu), r=[ps], w=[aT])
                    else:
                        xs_ = xst[xc[0] % 3]
                        xc[0] += 1
                        kb.op("dve", lambda: V.tensor_copy(out=xs_[:, :N], in_=ps[:, :N]), r=[ps], w=[xs_])
                        kb.dma(xbcd[sg.name][(mc - 16) * 128:(mc - 15) * 128, r0:r0 + N], xs_[:, :N], r=[xs_],
                               w=[f"xbcd_{sg.name}:{t}"], q="pool")
                linear(od_w_in16, 8, 5120, lambda kc: ht[:, kc, :N], N, ev, [ht])
                kb.dma(zsd[sg.name][:, r0:r0 + N].rearrange("(c p) n -> p c n", p=128), aT[:, 0:16, :N], r=[aT], w=[f"zsd_{sg.name}:{t}"], q="pool")
                for s4 in range(N // 128):
                    tok0 = r0 + s4 * 128
                    ps = kb.bank()
                    for kc in range(8):
                        kb.op("pe", lambda: P.matmul(ps[:, 0:64], lhsT=ht[:, kc, s4 * 128:(s4 + 1) * 128], rhs=wdt[:, kc, :],
                                                     start=(kc == 0), stop=(kc == 7)), r=[ht, wdt], w=[ps])
                    kb.op("dve", lambda: V.tensor_tensor(out=dtt[:], in0=ps[:, 0:64], in1=dtb[:], op=ALU.add), r=[ps, dtb], w=[dtt])
                    kb.op("act", lambda: S.activation(out=dte[:], in_=dtt[:], func=AF.Exp), r=[dtt], w=[dte])
                    kb.op("act", lambda: S.activation(out=dtt[:], in_=dte[:], func=AF.Ln, bias=1.0, scale=1.0), r=[dte], w=[dtt])
                    kb.dma(dtd[sg.name][tok0:tok0 + 128, :], dtt[:], r=[dtt], w=[f"dtd_{sg.name}:{tok0 // 128}"], q="pool")
        kb.barrier()
        less_wbufs()
        es.close()

    def phase_conv():
        es = ExitStack()
        sb1 = lambda name, shape, dt=F32: es.enter_context(nc.sbuf_tensor(name, list(shape), dt))
        cw = sb1("o2cw", [128, 24, 5]); cbias = sb1("o2cb", [128, 24])
        for k in range(5):
            kb.dma(cw[:, :, k], od_conv_w[k].rearrange("(c p) -> p c", p=128), w=[cw], allow_slow_non_contiguous=True)
        kb.dma(cbias[:], od_conv_b.rearrange("(c p) -> p c", p=128), allow_slow_non_contiguous=True)
        xall = sb1("o2xall", [128, 24, 516], BF16)
        cvall = sb1("o2cvall", [128, 8, 512], BF16)
        tkall = sb1("o2tkall", [128, 4, 20, 128], BF16)
        DG = sb1("o2DG", [128, 120, 128], BF16)
        for cc in range(24):
            for k in range(5):
                kb.op("dve" if (cc + k) % 2 else "pool",
                      (lambda: V.tensor_scalar(out=DG[:, cc * 5 + k, :], in0=identb[:], scalar1=cw[:, cc, k:k + 1], scalar2=None, op0=ALU.mult))
                      if (cc + k) % 2 else
                      (lambda: G.tensor_scalar(out=DG[:, cc * 5 + k, :], in0=identb[:], scalar1=cw[:, cc, k:k + 1], scalar2=None, op0=ALU.mult)),
                      r=[identb, cw], w=[DG])
        cvb = [sb1(f"o2cvb{i}", [128, 512], BF16) for i in range(2)]
        ctr = 0
        for sg in segs:
            for t in range(sg.nt):
                N = sg.N
                ns = N // 128
                r0 = t * N
                lo_, hi_ = max(0, r0 - 2), min(sg.L, r0 + N + 2)
                if r0 == 0:
                    kb.op("pool", lambda: G.memset(xall[:, :, 0:2], 0.0), w=[xall])
                if r0 + N == sg.L:
                    kb.op("pool", lambda: G.memset(xall[:, :, N + 2:N + 4], 0.0), w=[xall])
                tl_ = sorted(set([max(0, (lo_) // N), min(sg.nt - 1, (hi_ - 1) // N)] + [t]))
                kb.dma(xall[:, :, lo_ - (r0 - 2):hi_ - (r0 - 2)], xbcd[sg.name][:, lo_:hi_].rearrange("(c p) n -> p c n", p=128),
                       r=[f"xbcd_{sg.name}:{j}" for j in tl_], w=[xall])
                for cc in range(24):
                    ctr += 1
                    cv = cvall[:, cc - 16, :] if cc >= 16 else cvb[ctr % 2]
                    cvk = cvall if cc >= 16 else cvb[ctr % 2]
                    pc = kb.bank()
                    for k in range(5):
                        kb.op("pe", lambda: P.matmul(pc[:, :N], lhsT=DG[:, cc * 5 + k, :], rhs=xall[:, cc, k:k + N], start=(k == 0), stop=(k == 4)),
                              r=[DG, xall], w=[pc])
                    kb.op("act", lambda: S.activation(out=cv[:, :N], in_=pc[:, :N], func=AF.Silu, bias=cbias[:, cc:cc + 1]), r=[pc, cbias], w=[cvk])
                    if cc < 20:
                        ps = kb.bank()
                        psb_ = ps[:].bitcast(BF16)
                        for s4 in range(ns):
                            kb.op("pe", lambda: P.transpose(psb_[:, s4 * 128:(s4 + 1) * 128], cv[:, s4 * 128:(s4 + 1) * 128], identb[:]),
                                  r=[cvk, identb], w=[ps])
                        kb.op("dve", lambda: V.tensor_copy(out=tkall[:, :ns, cc, :], in_=psb_[:, 0:ns * 128].rearrange("p (s c) -> p s c", s=ns)),
                              r=[ps], w=[tkall])
                kb.dma(bcTd[sg.name][:, r0:r0 + N].rearrange("(c p) n -> p c n", p=128), cvall[:, :, :N], r=[cvall], w=[f"bcTd_{sg.name}:{t}"])
                kb.dma(xtd[sg.name][r0:r0 + N, :].rearrange("(s p) (c k) -> p s c k", p=128, k=128), tkall[:, :ns, 0:16, :], r=[tkall],
                       w=[f"tokd_{sg.name}:{t}"])
                kb.dma(btd[sg.name][r0:r0 + N, :].rearrange("(s p) (c k) -> p s c k", p=128, k=128), tkall[:, :ns, 16:20, :], r=[tkall],
                       w=[f"tokd_{sg.name}:{t}"])
        kb.barrier()
        es.close()

    def phase_scan():
        es = ExitStack()
        sb1 = lambda name, shape, dt=F32: es.enter_context(nc.sbuf_tensor(name, list(shape), dt))
        tri = sb1("o3tri", [128, 2, 128]); snm = sb1("o3snm", [128, 2, 128]); snmb = sb1("o3snmb", [128, 2, 128], BF16)
        onesf = sb1("o3ones", [128, 128])
        for d in range(2):
            kb.dma(tri[:, d, :], tri_in[d], w=[tri]); kb.dma(snm[:, d, :], snm_in[d], w=[snm])
        kb.dma(onesf[:], ones_in)
        kb.op("dve", lambda: V.tensor_copy(out=snmb[:], in_=snm[:]), r=[snm], w=[snmb])
        abc = sb1("o3abc", [128, 64]); dsk = sb1("o3dsk", [128, 32])
        kb.dma(abc[:], ssd_alog.partition_broadcast(128)); kb.dma(dsk[:], ssd_dsk.partition_broadcast(128))
        kb.op("act", lambda: S.activation(out=abc[:], in_=abc[:], func=AF.Exp), r=[abc], w=[abc])
        kb.op("dve", lambda: V.tensor_scalar(out=abc[:], in0=abc[:], scalar1=-1.0, scalar2=None, op0=ALU.mult), r=[abc], w=[abc])
        Sst = sb1("o3S", [128, 2048]); Sb = sb1("o3Sb", [128, 2048], BF16)
        xk = [sb1(f"o3xk{i}", [128, 2048], BF16) for i in range(2)]
        dtk = [sb1(f"o3dtk{i}", [128, 64]) for i in range(2)]
        bk = [sb1(f"o3bk{i}", [128, 512], BF16) for i in range(2)]
        bct = [sb1(f"o3bct{i}", [128, 8, 128], BF16) for i in range(2)]
        xg = sb1("o3xg", [128, 2048], BF16); xgw = sb1("o3xgw", [128, 2048], BF16)
        dta = sb1("o3dta", [128, 32]); nacs = sb1("o3nacs", [128, 32])
        dth = sb1("o3dth", [128, 32], BF16); dtl = sb1("o3dtl", [128, 32], BF16)
        trib = sb1("o3trib", [128, 2, 128], BF16)
        kb.op("dve", lambda: V.tensor_copy(out=trib[:], in_=tri[:]), r=[tri], w=[trib])
        snm4 = sb1("o3snm4", [128, 2, 4, 128], BF16)
        for r4 in range(4):
            kb.op("dve", lambda: V.tensor_copy(out=snm4[:, :, r4, :], in_=snm[:]), r=[snm], w=[snm4])
        tot = sb1("o3tot", [128, 32]); wd = sb1("o3wd", [128, 32]); dl = sb1("o3dl", [128, 32])
        CBT = sb1("o3CBT", [128, 4, 128], BF16)
        Da = [sb1(f"o3Da{i}", [128, 4, 128], BF16) for i in range(3)]
        Dm = [sb1(f"o3Dm{i}", [128, 4, 128], BF16) for i in range(3)]
        Cs = [sb1(f"o3Cs{i}", [128, 4, 128], BF16) for i in range(3)]
        Lm = [sb1(f"o3Lm{i}", [128, 4, 128], BF16) for i in range(3)]
        yf = sb1("o3yf", [128, 2048]); yt_ = sb1("o3yt", [128, 2048]); ybf = sb1("o3ybf", [128, 2048], BF16)
        yTt = sb1("o3yTt", [128, 16, 128], BF16)
        h0t = sb1("o3h0t", [128, 128])
        bc64 = lambda ap, nh: ap.unsqueeze(2).broadcast_to([128, nh, 64])
        cctr = [0]

        def rot4():
            b = kb.psb[kb.psn % 4]
            kb.psn += 1
            return b

        def run(sg, d, pj):
            nch = sg.L // 128
            order = list(range(nch)) if d == 0 else list(range(nch - 1, -1, -1))
            hd = slice(32 * d, 32 * d + 32)
            trid = tri[:, d, :]
            if sg.name == "s":
                for j in range(16):
                    kb.dma(h0t[:], ssd_h0[d, j * 128:(j + 1) * 128, :], w=[h0t])
                    ps = rot4()
                    kb.op("pe", lambda: P.transpose(ps[:, 0:128], h0t[:], ident[:]), r=[h0t, ident], w=[ps])
                    kb.op("act", lambda: S.activation(out=Sst[:, j * 128:(j + 1) * 128], in_=ps[:, 0:128], func=AF.Copy), r=[ps], w=[Sst])
            else:
                kb.op("dve", lambda: V.memset(Sst[:], 0.0), w=[Sst])
            kb.op("act", lambda: S.activation(out=Sb[:], in_=Sst[:], func=AF.Copy), r=[Sst], w=[Sb])
            def loads(ch_, i_):
                c0_ = ch_ * 128
                kb.dma(xk[i_][:], xtd[sg.name][c0_:c0_ + 128, :], r=[f"tokd_{sg.name}:{c0_ // sg.N}"], w=[xk[i_]])
                kb.dma(dtk[i_][:], dtd[sg.name][c0_:c0_ + 128, :], r=[f"dtd_{sg.name}:{ch_}"], w=[dtk[i_]])
                kb.dma(bk[i_][:], btd[sg.name][c0_:c0_ + 128, :], r=[f"tokd_{sg.name}:{c0_ // sg.N}"], w=[bk[i_]])
                kb.dma(bct[i_][:], bcTd[sg.name][:, c0_:c0_ + 128].rearrange("(c p) n -> p c n", p=128),
                       r=[f"bcTd_{sg.name}:{c0_ // sg.N}"], w=[bct[i_]])

            cctr[0] += 1
            loads(order[0], cctr[0] % 2)
            for oi, ch in enumerate(order):
                c0 = ch * 128
                i = cctr[0] % 2
                cctr[0] += 1
                if oi + 1 < len(order):
                    loads(order[oi + 1], cctr[0] % 2)
                X, DTK, BK, BCT = xk[i], dtk[i], bk[i], bct[i]
                kb.op("dve", lambda: V.tensor_tensor(out=dta[:], in0=DTK[:, hd], in1=abc[:, hd], op=ALU.mult), r=[DTK, abc], w=[dta])
                kb.op("dve", lambda: V.tensor_copy(out=dth[:], in_=dta[:]), r=[dta], w=[dth])
                kb.op("dve", lambda: V.tensor_tensor(out=dtl[:], in0=dta[:], in1=dth[:], op=ALU.subtract), r=[dta, dth], w=[dtl])
                kb.op("dve", lambda: V.tensor_tensor(out=xg[:].rearrange("p (h d) -> p h d", h=32), in0=X[:].rearrange("p (h d) -> p h d", h=32),
                                                     in1=bc64(DTK[:, hd], 32), op=ALU.mult), r=[X, DTK], w=[xg])
                psa = rot4()
                kb.op("pe", lambda: P.matmul(psa[:, 0:32], lhsT=trid, rhs=dta[:], start=True, stop=False), r=[tri, dta], w=[psa])
                kb.op("pe", lambda: P.matmul(psa[:, 32:64], lhsT=onesf[:], rhs=dta[:], start=False, stop=True), r=[onesf, dta], w=[psa])
                kb.op("act", lambda: S.activation(out=tot[:], in_=psa[:, 32:64], func=AF.Copy), r=[psa], w=[tot])
                kb.op("dve", lambda: V.tensor_scalar(out=nacs[:], in0=psa[:, 0:32], scalar1=-1.0, scalar2=None, op0=ALU.mult), r=[psa], w=[nacs])
                kb.op("dve", lambda: V.tensor_tensor(out=wd[:], in0=tot[:], in1=psa[:, 0:32], op=ALU.subtract), r=[tot, psa], w=[wd])
                kb.op("act", lambda: S.activation(out=wd[:], in_=wd[:], func=AF.Exp), r=[wd], w=[wd])
                kb.op("act", lambda: S.activation(out=dl[:], in_=tot[:], func=AF.Exp), r=[tot], w=[dl])
                kb.op("pool", lambda: G.tensor_tensor(out=xgw[:].rearrange("p (h d) -> p h d", h=32), in0=xg[:].rearrange("p (h d) -> p h d", h=32),
                                                      in1=bc64(wd[:], 32), op=ALU.mult), r=[xg, wd], w=[xgw])
                pcb = rot4()
                for g in range(4):
                    kb.op("pe", lambda: P.matmul(pcb[:, g * 128:(g + 1) * 128], lhsT=BCT[:, g, :], rhs=BCT[:, 4 + g, :], start=(g == 0), stop=(g == 3)),
                          r=[BCT], w=[pcb])
                kb.op("act", lambda: S.activation(out=CBT[:], in_=pcb[:, :].rearrange("p (g t) -> p g t", g=4), func=AF.Copy), r=[pcb], w=[CBT])
                def quadA(q):
                    g = q // 2
                    j2 = q % 3
                    pe_ = rot4()
                    for hh in range(4):
                        h = 4 * q + hh
                        kb.op("pe", lambda: P.matmul(pe_[:, hh * 128:(hh + 1) * 128], lhsT=dth[:, h:h + 1].broadcast_to([128, 128]), rhs=trib[:, d, :],
                                                     start=(hh == 0), stop=False), r=[dth, trib], w=[pe_])
                        kb.op("pe", lambda: P.matmul(pe_[:, hh * 128:(hh + 1) * 128], lhsT=dtl[:, h:h + 1].broadcast_to([128, 128]), rhs=trib[:, d, :],
                                                     start=False, stop=False), r=[dtl, trib], w=[pe_])
                    kb.op("act", lambda: S.activation(out=Da[j2][:], in_=pe_[:, :].rearrange("p (g t) -> p g t", g=4), func=AF.Exp), r=[pe_], w=[Da[j2]])
                    kb.op("pool", lambda: G.tensor_tensor(out=Cs[j2][:], in0=Da[j2][:], in1=BCT[:, 4 + g:5 + g, :].broadcast_to([128, 4, 128]), op=ALU.mult),
                          r=[Da[j2], BCT], w=[Cs[j2]])
                    kb.op("pe", lambda: P.matmul(pe_[:, :], lhsT=identb[:], rhs=snm4[:, d, :, :].rearrange("p r t -> p (r t)"), start=False, stop=True),
                          r=[identb, snm4], w=[pe_])
                    for hh in range(4):
                        h = 4 * q + hh
                        kb.op("act", lambda: S.activation(out=Dm[j2][:, hh, :], in_=pe_[:, hh * 128:(hh + 1) * 128], func=AF.Exp, bias=nacs[:, h:h + 1]),
                              r=[pe_, nacs], w=[Dm[j2]])
                    kb.op("dve", lambda: V.tensor_tensor(out=Lm[j2][:], in0=Dm[j2][:], in1=CBT[:, g:g + 1, :].broadcast_to([128, 4, 128]), op=ALU.mult),
                          r=[Dm[j2], CBT], w=[Lm[j2]])

                def quadB(q):
                    j2 = q % 3
                    for hh in range(4):
                        h = 4 * q + hh
                        py = kb.psb[4 + h // 8]
                        col = slice((h % 8) * 64, (h % 8) * 64 + 64)
                        kb.op("pe", lambda: P.matmul(py[:, col], lhsT=Lm[j2][:, hh, :], rhs=xg[:, h * 64:(h + 1) * 64], start=True, stop=False),
                              r=[Lm[j2], xg], w=[py])
                        kb.op("pe", lambda: P.matmul(py[:, col], lhsT=Cs[j2][:, hh, :], rhs=Sb[:, h * 64:(h + 1) * 64], start=False, stop=True),
                              r=[Cs[j2], Sb], w=[py])

                quadA(0)
                quadA(1)
                for q in range(8):
                    if q + 2 < 8:
                        quadA(q + 2)
                    quadB(q)
                if d == 0:
                    for b4 in range(4):
                        kb.op("act" if b4 % 2 else "dve",
                              (lambda b4=b4: S.activation(out=yf[:, b4 * 512:(b4 + 1) * 512], in_=kb.psb[4 + b4][:, :], func=AF.Copy)) if b4 % 2 else
                              (lambda b4=b4: V.tensor_copy(out=yf[:, b4 * 512:(b4 + 1) * 512], in_=kb.psb[4 + b4][:, :])),
                              r=[kb.psb[4 + b4]], w=[yf])
                    kb.dma(yfd[sg.name][c0:c0 + 128, :], yf[:], r=[yf], w=[f"yfd_{sg.name}:{ch}"])
                else:
                    kb.dma(yf[:], yfd[sg.name][c0:c0 + 128, :], r=[f"yfd_{sg.name}:{ch}"], w=[yf])
                    for b4 in range(4):
                        kb.op("dve", lambda: V.tensor_tensor(out=yt_[:, b4 * 512:(b4 + 1) * 512], in0=kb.psb[4 + b4][:, :], in1=yf[:, b4 * 512:(b4 + 1) * 512],
                                                             op=ALU.add), r=[kb.psb[4 + b4], yf], w=[yt_])
                    kb.op("pool", lambda: G.tensor_tensor(out=yf[:].rearrange("p (h d) -> p h d", h=32), in0=X[:].rearrange("p (h d) -> p h d", h=32),
                                                          in1=bc64(dsk[:], 32), op=ALU.mult), r=[X, dsk], w=[yf])
                    kb.op("dve", lambda: V.tensor_tensor(out=ybf[:], in0=yt_[:], in1=yf[:], op=ALU.add), r=[yt_, yf], w=[ybf])
                    for q4 in range(4):
                        ps = rot4()
                        psb_ = ps[:].bitcast(BF16)
                        for k4 in range(4):
                            cc = q4 * 4 + k4
                            kb.op("pe", lambda: P.transpose(psb_[:, k4 * 128:(k4 + 1) * 128], ybf[:, cc * 128:(cc + 1) * 128], identb[:]),
                                  r=[ybf, identb], w=[ps])
                        kb.op("act", lambda: S.activation(out=yTt[:, q4 * 4:(q4 + 1) * 4, :], in_=psb_[:, 0:512].rearrange("p (c n) -> p c n", c=4), func=AF.Copy),
                              r=[ps], w=[yTt])
                    kb.dma(yTd[sg.name][:, c0:c0 + 128].rearrange("(c p) n -> p c n", p=128), yTt[:], r=[yTt], w=[f"yTd_{sg.name}:{c0 // sg.N}"])
                for g in range(4):
                    gs = slice(g * 512, (g + 1) * 512)
                    pss = rot4()
                    kb.op("pe", lambda: P.matmul(pss[:, :], lhsT=BK[:, g * 128:(g + 1) * 128], rhs=xgw[:, gs], start=True, stop=True), r=[BK, xgw], w=[pss])
                    kb.op("dve", lambda: V.tensor_tensor(out=Sst[:, gs].rearrange("p (h d) -> p h d", h=8), in0=Sst[:, gs].rearrange("p (h d) -> p h d", h=8),
                                                         in1=bc64(dl[:, 8 * g:8 * g + 8], 8), op=ALU.mult), r=[Sst, dl], w=[Sst])
                    kb.op("dve", lambda: V.tensor_tensor(out=Sst[:, gs], in0=Sst[:, gs], in1=pss[:, :], op=ALU.add), r=[Sst, pss], w=[Sst])
                    kb.op("act", lambda: S.activation(out=Sb[:, gs], in_=Sst[:, gs], func=AF.Copy), r=[Sst], w=[Sb])
            if pj is not None:
                for j in range(16):
                    ps = rot4()
                    kb.op("pe", lambda: P.transpose(ps[:, 0:128], Sst[:, j * 128:(j + 1) * 128], ident[:]), r=[Sst, ident], w=[ps])
                    kb.op("dve", lambda: V.tensor_copy(out=h0t[:], in_=ps[:, 0:128]), r=[ps], w=[h0t])
                    kb.dma(o_ssd[pj, d, j * 128:(j + 1) * 128, :], h0t[:], r=[h0t], w=[f"o_ssd:{pj}:{d}:{j}"])

        for sg, pj in ((segs[0], None), (segs[1], 0), (segs[2], 1)):
            run(sg, 0, pj)
            run(sg, 1, pj)
        kb.barrier()
        es.close()

    def phase_odout():
        es = ExitStack()
        sb1 = lambda name, shape, dt=F32: es.enter_context(nc.sbuf_tensor(name, list(shape), dt))
        ng = sb1("o4ng", [128, 16])
        kb.dma(ng[:], ssd_norm_g.rearrange("(c p) -> p c", p=128), allow_slow_non_contiguous=True)
        yTs = [sb1(f"o4yT{i}", [128, 16, 512], BF16) for i in range(2)]; zss = [sb1(f"o4zs{i}", [128, 16, 512], BF16) for i in range(2)]
        more_wbufs(es, 1)
        tiles = [(sg, t) for sg in segs for t in range(sg.nt)]
        xts = {}

        def loads(i):
            sg, t = tiles[i]
            r0 = t * sg.N
            xts[i] = xt[tctr[0] % 2]
            tctr[0] += 1
            load_xT(sg, 0, t, xts[i])
            kb.dma(yTs[i % 2][:, :, :sg.N], yTd[sg.name][:, r0:r0 + sg.N].rearrange("(c p) n -> p c n", p=128), r=[f"yTd_{sg.name}:{t}"], w=[yTs[i % 2]])
            kb.dma(zss[i % 2][:, :, :sg.N], zsd[sg.name][:, r0:r0 + sg.N].rearrange("(c p) n -> p c n", p=128), r=[f"zsd_{sg.name}:{t}"], w=[zss[i % 2]])

        loads(0)
        for i in range(len(tiles)):
            if True:
                sg, t = tiles[i]
                N = sg.N
                r0 = t * N
                xtile = xts.pop(i)
                yT_, zs = yTs[i % 2], zss[i % 2]
                if i + 1 < len(tiles):
                    loads(i + 1)
                kb.op("dve", lambda: V.tensor_tensor(out=yT_[:, :, :N], in0=yT_[:, :, :N], in1=zs[:, :, :N], op=ALU.mult), r=[yT_, zs], w=[yT_])
                kb.op("act", lambda: S.activation(out=zs[:, :, :N], in_=yT_[:, :, :N], func=AF.Square), r=[yT_], w=[zs])
                ps = kb.bank()
                for kc in range(16):
                    kb.op("pe", lambda: P.matmul(ps[:, :N], lhsT=onesb[:], rhs=zs[:, kc, :N], start=(kc == 0), stop=(kc == 15)), r=[onesb, zs], w=[ps])
                kb.op("act", lambda: S.activation(out=tmpf[:, :N], in_=ps[:, :N], func=AF.Sqrt, bias=EPS, scale=1.0 / 2048), r=[ps], w=[tmpf])
                kb.op("dve", lambda: V.reciprocal(out=rstd[:, :N], in_=tmpf[:, :N]), r=[tmpf], w=[rstd])
                for kc in range(16):
                    kb.op("dve", lambda: V.scalar_tensor_tensor(out=yT_[:, kc, :N], in0=yT_[:, kc, :N], scalar=ng[:, kc:kc + 1], in1=rstd[:, :N],
                                                                op0=ALU.mult, op1=ALU.mult), r=[yT_, ng, rstd], w=[yT_])

                def ev(mc, ps2):
                    kb.op("dve", lambda: V.scalar_tensor_tensor(out=xtile[:, mc, :N], in0=ps2[:, :N],
                                                                scalar=modT[1][:, 16 + mc, sg.cond:sg.cond + 1],
                                                                in1=xtile[:, mc, :N], op0=ALU.mult, op1=ALU.add),
                          r=[ps2, modT[1], xtile], w=[xtile])
                linear(od_w_out, 16, D, lambda kc: yT_[:, kc, :N], N, ev, [yT_])
                store_xT(sg, 1, t, xtile, q="pool")
        kb.barrier()
        less_wbufs()
        es.close()

    kb.mark('setup')
    for sg in segs:
        phase_evin(sg)
    kb.mark('evin')
    ev_w_out = to_bf16("evout16", ev_w_out, D, D)
    ffn_w_gate = [to_bf16(f"wg16_{l}", ffn_w_gate[l], D, DFF) for l in range(2)]
    ffn_w_up = [to_bf16(f"wu16_{l}", ffn_w_up[l], D, DFF) for l in range(2)]
    ffn_w_down = [to_bf16(f"wd16_{l}", ffn_w_down[l], DFF, D) for l in range(2)]
    od_w_in16 = to_bf16("odin16", od_w_in[:, 0:5120], D, 5120)
    od_w_out = to_bf16("odout16", od_w_out, 2048, D)
    for sg in segs:
        if ENABLE_ATTN:
            phase_attn(sg)
    kb.mark('attn')
    kb.barrier()
    es0.close()
    phase_s5()
    kb.mark('s5')
    for sg in segs:
        phase_outproj(0, sg, ev_w_out, 8, 0, 1)
    kb.mark('outproj0')
    phase_ffn(0, 1, 0)
    kb.mark('ffn0')
    phase_odin()
    kb.mark('odin')
    phase_conv()
    kb.mark('conv')
    phase_scan()
    kb.mark('scan')
    phase_odout()
    kb.mark('odout')
    phase_ffn(1, 1, 0, final=True)
    kb.mark('ffn1')
    return kb


def _na_consts():
    lo = lambda r: min(max(r - 4, 0), 56)
    combos = [(5, e) for e in range(-2, 3)] + [(0, e) for e in range(0, 4)] + [(1, e) for e in range(-1, 3)] + \
             [(30, e) for e in range(-2, 2)] + [(31, e) for e in range(-3, 1)]
    q = np.arange(128); k = np.arange(128)
    qr, wq = q // 64, q % 64
    kr, wk = k // 64, k % 64
    dr_idx = np.zeros((21, 128, 128), np.int64); mask = np.zeros((21, 128, 128), np.float32)
    cs = np.clip(wq - 8, 0, 48)
    col_ok = (wk[None, :] >= cs[:, None]) & (wk[None, :] < cs[:, None] + 16)
    dc_idx = np.clip(wk[None, :] - wq[:, None], -15, 15) + 15
    for ci, (m, e) in enumerate(combos):
        qrow = 2 * m + qr; krow = 2 * (m + e) + kr
        lo_q = np.array([lo(r) for r in qrow])
        valid = (krow[None, :] >= lo_q[:, None]) & (krow[None, :] < lo_q[:, None] + 8) & col_ok
        dr_idx[ci] = np.clip(krow[None, :] - qrow[:, None] + 7, 0, 14)
        mask[ci] = np.where(valid, 0.0, -30000.0)
    return dr_idx, dc_idx, mask


_NC_CACHE = {}


def kernel(**inp):
    n = 8
    if "nc" not in _NC_CACHE:
        rec = build_program()
        rec.finish()
        _NC_CACHE["nc"] = build_program(plan=rec.needed).finish()
    nc = _NC_CACHE["nc"]
    f = lambda a: np.ascontiguousarray(np.asarray(a, dtype=np.float32))
    shared = {k: f(inp[k]) for k in ["norm_mix_g", "norm_ffn_g", "ada_w", "ada_b", "ffn_w_gate", "ffn_w_up", "ffn_w_down",
                                     "final_norm_g"]}
    shared["ev_w_in"] = f(inp["ev_w_in"][0]); shared["ev_w_out"] = f(inp["ev_w_out"][0])
    shared["od_w_in"] = f(inp["od_w_in"][0]); shared["od_w_out"] = f(inp["od_w_out"][0])
    shared["ident"] = np.eye(128, dtype=np.float32)
    dr_idx, dc_idx, mask = _na_consts()
    rpb = f(inp["na_rpb"][0])
    shared["rpbg"] = np.ascontiguousarray(rpb[:, dr_idx, dc_idx[None]].transpose(1, 0, 2, 3))
    shared["nmask"] = mask

    def lay(a):
        a = np.asarray(a, np.float32)
        rest = a.shape[3:]
        a = a.reshape(2, 16, 2, 64, *rest)
        a = np.moveaxis(a, [2, 3, 0, 1], [0, 1, 2, 3])
        return np.ascontiguousarray(a.reshape(128, 32, *rest))
    ldt = np.broadcast_to(np.asarray(inp["s5_log_dt"][0], np.float32)[:, :, None], (2, 32, 64))
    shared["s5_a"] = np.ascontiguousarray(np.stack([lay(inp["s5_a_re"][0]), lay(inp["s5_a_im"][0]), lay(ldt)], axis=1))
    shared["s5_B"] = np.ascontiguousarray(np.stack([lay(inp["s5_b_re"][0]), lay(inp["s5_b_im"][0])], axis=1))
    ct = lambda a: np.swapaxes(np.asarray(a, np.float32), 2, 3)
    shared["s5_C"] = np.ascontiguousarray(np.stack([lay(ct(inp["s5_c_re"][0])), lay(ct(inp["s5_c_im"][0]))], axis=1))
    shared["s5_dcol"] = np.ascontiguousarray(f(inp["s5_d"][0]).reshape(4, 128).T)
    shared["s5_w_glu"] = f(inp["s5_w_glu"][0])
    shared["od_conv_w"] = f(inp["od_conv_w"][0]); shared["od_conv_b"] = f(inp["od_conv_b"][0])
    shared["ssd_a_log"] = f(inp["ssd_a_log"][0]).reshape(64); shared["ssd_dt_bias"] = f(inp["ssd_dt_bias"][0]).reshape(64)
    shared["ssd_d"] = f(inp["ssd_d"][0]); shared["ssd_norm_g"] = f(inp["ssd_norm_g"][0])
    ar = np.arange(128)
    shared["tri"] = np.stack([(ar[:, None] <= ar[None, :]), (ar[:, None] >= ar[None, :])]).astype(np.float32)
    shared["ssd_nmask"] = np.stack([np.where(ar[:, None] > ar[None, :], -30000.0, 0.0),
                                    np.where(ar[:, None] < ar[None, :], -30000.0, 0.0)]).astype(np.float32)
    shared["ones_f"] = np.ones((128, 128), np.float32)
    in_maps = []
    for i in range(n):
        m = dict(shared)
        m["x_s"] = f(inp["x_sample"][i % 4])
        m["x_p"] = f(inp["x_prompt"][2 * i:2 * i + 2])
        m["cond"] = f(np.stack([np.asarray(inp["c"])[i % 4], np.asarray(inp["c_ctx"])]))
        m["cache_k"] = f(inp["cache_na_k"][i % 4, 0]); m["cache_v"] = f(inp["cache_na_v"][i % 4, 0])
        m["ssd_h0"] = np.ascontiguousarray(f(inp["state_ssd"][i % 4, 0]).reshape(2, 2048, 128))
        m["s5_h0"] = np.ascontiguousarray(np.stack([lay(inp["state_s5_re"][i % 4, 0]), lay(inp["state_s5_im"][i % 4, 0])], axis=1))
        in_maps.append(m)
    res = run_bass_kernel_spmd(nc, in_maps, core_ids=list(range(n)))
    R = res.results
    y_prompt = np.concatenate([R[i]["y_p"] for i in range(n)], axis=0)
    y_sample = np.stack([R[i]["y_s"] for i in range(4)], axis=0)
    nk = np.concatenate([R[i]["o_k"] for i in range(n)], axis=0)[:, None]
    nv = np.concatenate([R[i]["o_v"] for i in range(n)], axis=0)[:, None]
    s5o = np.concatenate([R[i]["o_s5"] for i in range(n)], axis=0)
    s5o = s5o.reshape(16, 2, 2, 64, 2, 16).transpose(0, 1, 4, 5, 2, 3).reshape(16, 2, 2, 32, 64)
    z5 = np.zeros((16, 1, 2, 32, 64), np.float32)
    zs = np.concatenate([R[i]["o_ssd"] for i in range(n)], axis=0).reshape(16, 1, 2, 32, 64, 128)
    return y_prompt, y_sample, nk, nv, np.ascontiguousarray(s5o[:, 0:1]), np.ascontiguousarray(s5o[:, 1:2]), zs
```

```python
import math
from contextlib import ExitStack
import numpy as np
import concourse.bass as bass
import concourse.mybir as mybir
from concourse.bass_utils import run_bass_kernel_spmd

F32 = mybir.dt.float32
BF16 = mybir.dt.bfloat16
ALU = mybir.AluOpType
AF = mybir.ActivationFunctionType
AX = mybir.AxisListType
NDS = 40
NSW_BASE = 32
ENABLE_ATTN = True
D = 1024
DFF = 2816
EPS = 1e-6


def _key(x):
    if isinstance(x, str):
        return x
    if hasattr(x, "tensor"):
        return x.tensor.name
    return x.name


class KB:
    def __init__(self, plan=None):
        self.plan = plan
        self.needed = {e: set() for e in ["pe", "act", "dve", "pool"]}
        self.ereal = {e: 0 for e in ["pe", "act", "dve", "pool"]}
        self.omap = {e: {} for e in ["pe", "act", "dve", "pool"]}
        self.nc = nc = bass.Bass("TRN2", target_bir_lowering=False)
        self.eng = {"pe": nc.tensor, "act": nc.scalar, "dve": nc.vector, "pool": nc.gpsimd, "sp": nc.sync}
        self.esem = {e: nc.alloc_semaphore(name=f"es_{e}") for e in ["pe", "act", "dve", "pool"]}
        self.ecnt = {e: 0 for e in self.esem}
        self.dsems = [nc.alloc_semaphore(name=f"ds{i}") for i in range(NDS)]
        self.dcnt = [0] * NDS
        self.dnext = 0
        self.dnext_sw = 0
        self.waited = {}
        self.lastw = {}
        self.readers = {}
        self.nins = 0
        self.psb = [nc.alloc_psum_tensor(f"psb{i}", [128, 512], F32) for i in range(8)]
        self.psn = 0
        self.rr = 0
        self.marks = []

    def sb(self, name, shape, dt=F32):
        return self.nc.alloc_sbuf_tensor(name, list(shape), dt)

    def dram(self, name, shape, dt=F32, kind="Internal"):
        return self.nc.dram_tensor(name, list(shape), dt, kind=kind)

    def bank(self):
        b = self.psb[self.psn % 8]
        self.psn += 1
        return b

    def _wait(self, e, tok):
        sem, val, src = tok
        if src == e and e == "pe":
            return
        k = (e, sem.name)
        if self.waited.get(k, 0) >= val:
            return
        self.waited[k] = val
        if sem.name.startswith("es_"):
            src_e = sem.name[3:]
            self.needed[src_e].add(val)
            val = self.omap[src_e][val]
        self.eng[e].wait_ge(sem, val)

    def _deps(self, e, reads, writes):
        for k in reads:
            lw = self.lastw.get(k)
            if lw is not None:
                self._wait(e, lw)
        for k in writes:
            lw = self.lastw.get(k)
            if lw is not None:
                self._wait(e, lw)
            for tok in self.readers.get(k, {}).values():
                self._wait(e, tok)

    def _record(self, tok, reads, writes):
        for k in writes:
            self.lastw[k] = tok
            self.readers[k] = {}
        for k in reads:
            self.readers.setdefault(k, {})[tok[2]] = tok

    def op(self, e, fn, r=(), w=()):
        reads = [_key(x) for x in r]
        writes = [_key(x) for x in w]
        writes = writes + [k for k in reads if k.startswith("psb")]
        reads = [k for k in reads if not k.startswith("psb")]
        self._deps(e, reads, writes)
        ins = fn()
        self.ecnt[e] += 1
        if self.plan is None or self.ecnt[e] in self.plan[e]:
            self.ereal[e] += 1
            ins.then_inc(self.esem[e], 1)
            self.omap[e][self.ecnt[e]] = self.ereal[e]
        tok = (self.esem[e], self.ecnt[e], e)
        self._record(tok, reads, writes)
        self.nins += 1
        return tok

    def dma(self, out, in_, r=(), w=(), q="sp", **kw):
        reads = [_key(x) for x in r] if r else [_key(in_)]
        writes = [_key(x) for x in w] if w else [_key(out)]
        if q == "pool":
            i = NSW_BASE + self.dnext_sw
            self.dnext_sw = (self.dnext_sw + 1) % (NDS - NSW_BASE)
        else:
            i = self.dnext
            self.dnext = (i + 1) % NSW_BASE
        sem = self.dsems[i]
        if self.dcnt[i] > 0:
            self._wait(q, (sem, self.dcnt[i] * 16, f"dma{i}"))
        self._deps(q, reads, writes)
        ins = self.eng[q].dma_start(out=out, in_=in_, **kw)
        self.dcnt[i] += 1
        ins.then_inc(sem, 16)
        tok = (sem, self.dcnt[i] * 16, f"dma{i}_{self.dcnt[i]}")
        self._record(tok, reads, writes)
        self.nins += 1
        return tok

    def finish(self):
        for i in range(NDS):
            if self.dcnt[i] > 0:
                self._wait("sp", (self.dsems[i], self.dcnt[i] * 16, f"dma{i}"))
        return self.nc

    def mark(self, label):
        self.marks.append((label, dict(self.ecnt)))

    def barrier(self):
        for e in ["pe", "act", "dve", "pool", "sp"]:
            for c in self.esem:
                if self.ecnt[c] > 0:
                    self._wait(e, (self.esem[c], self.ecnt[c], "bar"))
            for i in range(NDS):
                if self.dcnt[i] > 0:
                    self._wait(e, (self.dsems[i], self.dcnt[i] * 16, "bar"))

    def ew_engine(self):
        self.rr += 1
        return ("dve", "pool")[self.rr % 2]


class Seg:
    def __init__(self, name, L, N, cond):
        self.name, self.L, self.N, self.cond = name, L, N, cond
        self.nt = L // N


def build_program(dbg=False, plan=None):
    kb = KB(plan)
    nc = kb.nc
    V, G, S, P = nc.vector, nc.gpsimd, nc.scalar, nc.tensor

    def ein(name, shape):
        return kb.dram(name, shape, F32, kind="ExternalInput").ap()

    def eout(name, shape):
        return kb.dram(name, shape, F32, kind="ExternalOutput").ap()

    x_s = ein("x_s", [4096, D])
    x_p = ein("x_p", [2, 256, D])
    cond = ein("cond", [2, D])
    ident_in = ein("ident", [128, 128])
    norm_mix_g = ein("norm_mix_g", [2, D]); norm_ffn_g = ein("norm_ffn_g", [2, D])
    ada_w = ein("ada_w", [2, D, 6 * D]); ada_b = ein("ada_b", [2, 6 * D])
    ffn_w_gate = ein("ffn_w_gate", [2, D, DFF]); ffn_w_up = ein("ffn_w_up", [2, D, DFF])
    ffn_w_down = ein("ffn_w_down", [2, DFF, D])
    ev_w_in = ein("ev_w_in", [D, 2048]); ev_w_out = ein("ev_w_out", [D, D])
    od_w_in = ein("od_w_in", [D, 5184]); od_w_out = ein("od_w_out", [2048, D])
    final_norm_g = ein("final_norm_g", [D])
    y_s = eout("y_s", [4096, D])
    y_p = eout("y_p", [2, 256, D])

    segs = [Seg("s", 4096, 512, 0), Seg("p0", 256, 256, 1), Seg("p1", 256, 256, 1)]
    xin = {"s": x_s, "p0": x_p[0], "p1": x_p[1]}
    yout = {"s": y_s, "p0": y_p[0], "p1": y_p[1]}
    xT = {sg.name: [kb.dram(f"xT{j}_{sg.name}", [D, sg.L]).ap() for j in range(2)] for sg in segs}
    mixT = {sg.name: kb.dram(f"mixT_{sg.name}", [2048, sg.L], BF16).ap() for sg in segs}

    ident = kb.sb("ident_sb", [128, 128])
    identb = kb.sb("identb", [128, 128], BF16)
    onesb = kb.sb("onesb", [128, 128], BF16)
    kb.dma(ident[:], ident_in)
    kb.op("dve", lambda: V.tensor_copy(out=identb[:], in_=ident[:]), r=[ident], w=[identb])
    kb.op("dve", lambda: V.memset(onesb[:], 1.0), w=[onesb])

    def colload(name, src_1d, n):
        t = kb.sb(name, [128, n])
        kb.dma(t[:], src_1d.rearrange("(k p) -> p k", p=128), allow_slow_non_contiguous=True)
        return t

    gmix = [colload(f"gmix{l}", norm_mix_g[l], 8) for l in range(2)]
    gffn = [colload(f"gffn{l}", norm_ffn_g[l], 8) for l in range(2)]
    adab = [colload(f"adab{l}", ada_b[l], 48) for l in range(2)]
    gfin = kb.sb("gfin", [128, D])
    kb.dma(gfin[:], final_norm_g.partition_broadcast(128))
    condT = kb.sb("condT", [128, 8, 2])
    for j in range(2):
        kb.dma(condT[:, :, j], cond[j].rearrange("(k p) -> p k", p=128), w=[condT], allow_slow_non_contiguous=True)
    scT = kb.sb("scT", [128, 8, 2], BF16)
    kb.op("act", lambda: S.activation(out=scT[:], in_=condT[:], func=AF.Silu), r=[condT], w=[scT])

    GW = 256
    wbufs = [kb.sb(f"wbuf{i}", [128, 22, GW], BF16) for i in range(2)]
    wctr = [0]

    def linear(W, KC, M, rhs, N, evac, rdeps):
        tiled = isinstance(W, tuple)
        if not tiled:
            Wv = W.rearrange("(kc p) m -> p kc m", p=128)
        for g in range(M // GW):
            wb = wbufs[wctr[0] % len(wbufs)]
            wctr[0] += 1
            if tiled:
                kb.dma(wb[:, :KC, :], W[1][g], q="sp", w=[wb])
            else:
                kb.dma(wb[:, :KC, :], Wv[:, :, g * GW:(g + 1) * GW], q="pool", w=[wb])
            for j in range(GW // 128):
                ps = kb.bank()
                for kc in range(KC):
                    kb.op("pe", lambda: P.matmul(ps[:, :N], lhsT=wb[:, kc, j * 128:(j + 1) * 128], rhs=rhs(kc),
                                                 start=(kc == 0), stop=(kc == KC - 1)), r=[wb] + rdeps, w=[ps])
                evac(g * (GW // 128) + j, ps)

    cvt_toks = []

    def to_bf16(name, W, K, M):
        KC_, G_ = K // 128, M // 256
        t = kb.dram(name, [G_, 128, KC_, 256], BF16).ap()
        for kc in range(KC_):
            if len(cvt_toks) >= 3:
                kb._wait("pool", cvt_toks[-3])
            cvt_toks.append(kb.dma(t[:, :, kc, :].rearrange("g p j -> p g j"),
                                   W[kc * 128:(kc + 1) * 128, :].rearrange("p (g j) -> p g j", g=G_), q="pool", w=[name]))
        return ("tiled", t)
    ev_w_in16 = to_bf16("evin16", ev_w_in[:, 0:1536], D, 1536)

    xw = [0]

    def more_wbufs(es, n=2):
        for _ in range(n):
            xw[0] += 1
            wbufs.append(es.enter_context(nc.sbuf_tensor(f"wbufx{xw[0]}", [128, 22, GW], BF16)))

    def less_wbufs():
        del wbufs[2:]

    modT = [kb.sb(f"modT{l}", [128, 48, 2]) for l in range(2)]
    for l in range(2):
        def ev(mc, ps, l=l):
            kb.op("dve", lambda: V.tensor_scalar(out=modT[l][:, mc, :], in0=ps[:, 0:2], scalar1=adab[l][:, mc:mc + 1],
                                                 scalar2=None, op0=ALU.add), r=[ps, adab[l]], w=[modT[l]])
        linear(ada_w[l], 8, 6 * D, lambda kc: scT[:, kc, :], 2, ev, [scT])
    Gm = [[kb.sb(f"Gm{l}_{i}", [128, 8, 2]) for i in range(2)] for l in range(2)]
    for l in range(2):
        for i, gsrc in enumerate((gmix[l], gffn[l])):
            for c in range(2):
                kb.op("dve", lambda: V.scalar_tensor_tensor(out=Gm[l][i][:, :, c], in0=modT[l][:, 24 * i + 8:24 * i + 16, c],
                                                            scalar=1.0, in1=gsrc[:], op0=ALU.add, op1=ALU.mult),
                      r=[modT[l], gsrc], w=[Gm[l][i]])

    xt = [kb.sb(f"xt{i}", [128, 8, 512]) for i in range(2)]
    hT = [kb.sb(f"hT{i}", [128, 8, 512], BF16) for i in range(2)]
    aT = kb.sb("aT", [128, 22, 512], BF16)
    sq = aT[:, 12:20, :]
    rstd = kb.sb("rstd", [128, 512])
    tmpf = kb.sb("tmpf", [128, 512])
    tctr = [0]

    def norm_a(xtile, N, sq):
        kb.op("act", lambda: S.activation(out=sq[:, :, :N], in_=xtile[:, :, :N], func=AF.Square), r=[xtile], w=[sq])

    def norm_mod(xtile, N, l, which, c, ht, sq=sq, rstd=rstd, tmpf=tmpf, part="ab", pool_only=False, scr=None):
        if "a" in part:
            norm_a(xtile, N, sq)
        if "b" not in part:
            return
        ps = kb.bank()
        for kc in range(8):
            kb.op("pe", lambda: P.matmul(ps[:, :N], lhsT=onesb[:], rhs=sq[:, kc, :N], start=(kc == 0), stop=(kc == 7)),
                  r=[onesb, sq], w=[ps])
        kb.op("act", lambda: S.activation(out=tmpf[:, :N], in_=ps[:, :N], func=AF.Sqrt, bias=EPS, scale=1.0 / D),
              r=[ps], w=[tmpf])
        kb.op("dve", lambda: V.reciprocal(out=rstd[:, :N], in_=tmpf[:, :N]), r=[tmpf], w=[rstd])
        g_, sh = Gm[l][which], modT[l]
        for kc in range(8):
            e = "pool" if pool_only else kb.ew_engine()
            E = V if e == "dve" else G
            dst = scr[kc % 2][:, :N] if scr is not None else xtile[:, kc, :N]
            dkey = scr[kc % 2] if scr is not None else xtile
            kb.op(e, lambda: E.tensor_tensor(out=dst, in0=xtile[:, kc, :N], in1=rstd[:, :N], op=ALU.mult),
                  r=[xtile, rstd], w=[dkey])
            e2 = "pool" if pool_only else "dve"
            E2 = G if pool_only else V
            kb.op(e2, lambda: E2.tensor_scalar(out=ht[:, kc, :N], in0=dst, scalar1=g_[:, kc, c:c + 1],
                                               scalar2=sh[:, 24 * which + kc, c:c + 1], op0=ALU.mult, op1=ALU.add),
                  r=[dkey, g_, sh], w=[ht])

    tok4 = kb.sb("tok4", [128, 4, D])

    def transpose_in(sg):
        for t in range(sg.nt):
            ns = sg.N // 128
            xtile = xt[tctr[0] % 2]
            tctr[0] += 1
            r0 = t * sg.N
            kb.dma(tok4[:, :ns, :], xin[sg.name][r0:r0 + sg.N, :].rearrange("(s p) d -> p s d", p=128), w=[tok4])
            for kc in range(8):
                ps = kb.bank()
                for s4 in range(ns):
                    kb.op("pe", lambda: P.transpose(ps[:, s4 * 128:(s4 + 1) * 128], tok4[:, s4, kc * 128:(kc + 1) * 128], ident[:]),
                          r=[tok4, ident], w=[ps])
                kb.op("act", lambda: S.activation(out=xtile[:, kc, :sg.N], in_=ps[:, :sg.N], func=AF.Copy), r=[ps], w=[xtile])
            kb.dma(xT[sg.name][0][:, r0:r0 + sg.N].rearrange("(kc p) n -> p kc n", p=128), xtile[:, :, :sg.N],
                   w=[f"xT0_{sg.name}:{t}"])

    for sg in segs:
        transpose_in(sg)

    def load_xT(sg, buf, t, xtile):
        r0 = t * sg.N
        kb.dma(xtile[:, :, :sg.N], xT[sg.name][buf][:, r0:r0 + sg.N].rearrange("(kc p) n -> p kc n", p=128),
               r=[f"xT{buf}_{sg.name}:{t}"], w=[xtile])

    def store_xT(sg, buf, t, xtile, q="sp"):
        r0 = t * sg.N
        kb.dma(xT[sg.name][buf][:, r0:r0 + sg.N].rearrange("(kc p) n -> p kc n", p=128), xtile[:, :, :sg.N],
               r=[xtile], w=[f"xT{buf}_{sg.name}:{t}"], q=q)

    mixt = [aT, aT]

    def phase_outproj(l, sg, W, KC, src_buf, dst_buf):
        for t in range(sg.nt):
            N = sg.N
            r0 = t * N
            xtile = xt[tctr[0] % 2]
            mt = mixt[tctr[0] % 2]
            tctr[0] += 1
            load_xT(sg, src_buf, t, xtile)
            kb.dma(mt[:, :KC, :N], mixT[sg.name][:KC * 128, r0:r0 + N].rearrange("(kc p) n -> p kc n", p=128),
                   r=[f"mixT_{sg.name}:{t}"], w=[mt])

            def ev(mc, ps):
                kb.op("dve", lambda: V.scalar_tensor_tensor(out=xtile[:, mc, :N], in0=ps[:, :N],
                                                            scalar=modT[l][:, 16 + mc, sg.cond:sg.cond + 1],
                                                            in1=xtile[:, mc, :N], op0=ALU.mult, op1=ALU.add),
                      r=[ps, modT[l], xtile], w=[xtile])
            linear(W, KC, D, lambda kc: mt[:, kc, :N], N, ev, [mt])
            store_xT(sg, dst_buf, t, xtile, q="pool")

    def phase_ffn(l, src_buf, dst_buf, final=False):
        es = ExitStack()
        sqp = es.enter_context(nc.sbuf_tensor(f"ffsq{l}", [128, 8, 512], BF16))
        rsp = es.enter_context(nc.sbuf_tensor(f"ffrs{l}", [128, 512], F32))
        tmp_ = es.enter_context(nc.sbuf_tensor(f"fftm{l}", [128, 512], F32))
        more_wbufs(es, 3)
        scr = [es.enter_context(nc.sbuf_tensor(f"ffscr{l}_{j}", [128, 512], F32)) for j in range(2)]
        tiles = [(sg, t) for sg in segs for t in range(sg.nt)]
        bufs = {}

        def prepA(i):
            sg, t = tiles[i]
            xtile = xt[tctr[0] % 2]
            ht = hT[tctr[0] % 2]
            tctr[0] += 1
            bufs[i] = (xtile, ht)
            load_xT(sg, src_buf, t, xtile)
            norm_mod(xtile, sg.N, l, 1, sg.cond, ht, sq=sqp, rstd=rsp, tmpf=tmp_, part="a")

        def prepB(i):
            sg, t = tiles[i]
            xtile, ht = bufs[i]
            norm_mod(xtile, sg.N, l, 1, sg.cond, ht, sq=sqp, rstd=rsp, tmpf=tmp_, part="b", pool_only=True, scr=scr)

        def compute(i):
            sg, t = tiles[i]
            N = sg.N
            xtile, ht = bufs[i]
            if i + 1 < len(tiles):
                prepA(i + 1)

            def ev_g(mc, ps):
                kb.op("act", lambda: S.activation(out=aT[:, mc, :N], in_=ps[:, :N], func=AF.Silu), r=[ps], w=[aT])
            linear(ffn_w_gate[l], 8, DFF, lambda kc: ht[:, kc, :N], N, ev_g, [ht])
            if i + 1 < len(tiles):
                prepB(i + 1)

            def ev_u(mc, ps):
                kb.op("dve", lambda: V.tensor_tensor(out=aT[:, mc, :N], in0=ps[:, :N], in1=aT[:, mc, :N], op=ALU.mult),
                      r=[ps, aT], w=[aT])
            linear(ffn_w_up[l], 8, DFF, lambda kc: ht[:, kc, :N], N, ev_u, [ht])

            def ev_d(mc, ps):
                kb.op("dve", lambda: V.scalar_tensor_tensor(out=xtile[:, mc, :N], in0=ps[:, :N],
                                                            scalar=modT[l][:, 40 + mc, sg.cond:sg.cond + 1],
                                                            in1=xtile[:, mc, :N], op0=ALU.mult, op1=ALU.add),
                      r=[ps, modT[l], xtile], w=[xtile])
            linear(ffn_w_down[l], 22, D, lambda kc: aT[:, kc, :N], N, ev_d, [aT])
            if not final:
                store_xT(sg, dst_buf, t, xtile, q="pool")
            else:
                final_out(sg, t, xtile)

        prepA(0)
        prepB(0)
        for i in range(len(tiles)):
            compute(i)
        kb.barrier()
        less_wbufs()
        es.close()

    ssq4 = kb.sb("ssq4", [128, 4])
    rs4 = kb.sb("rs4", [128, 4])
    junk = kb.sb("junk", [128, D], BF16)

    def final_out(sg, t, xtile):
        N = sg.N
        ns = N // 128
        r0 = t * N
        for s4 in range(ns):
            for half in range(2):
                ps = kb.bank()
                for k4 in range(4):
                    kc = half * 4 + k4
                    kb.op("pe", lambda: P.transpose(ps[:, k4 * 128:(k4 + 1) * 128], xtile[:, kc, s4 * 128:(s4 + 1) * 128], ident[:]),
                          r=[xtile, ident], w=[ps])
                kb.op("dve", lambda: V.tensor_copy(out=tok4[:, s4, half * 512:(half + 1) * 512], in_=ps[:, :]), r=[ps], w=[tok4])
            kb.op("act", lambda: S.activation(out=junk[:], in_=tok4[:, s4, :], func=AF.Square, accum_out=ssq4[:, s4:s4 + 1]),
                  r=[tok4], w=[junk, ssq4])
        kb.op("act", lambda: S.activation(out=rs4[:, :ns], in_=ssq4[:, :ns], func=AF.Sqrt, bias=EPS, scale=1.0 / D), r=[ssq4], w=[rs4])
        kb.op("dve", lambda: V.reciprocal(out=rs4[:, :ns], in_=rs4[:, :ns]), r=[rs4], w=[rs4])
        for s4 in range(ns):
            kb.op("dve", lambda: V.scalar_tensor_tensor(out=tok4[:, s4, :], in0=tok4[:, s4, :], scalar=rs4[:, s4:s4 + 1],
                                                        in1=gfin[:], op0=ALU.mult, op1=ALU.mult), r=[tok4, rs4, gfin], w=[tok4])
        kb.dma(yout[sg.name][r0:r0 + N, :].rearrange("(s p) d -> p s d", p=128), tok4[:, :ns, :], r=[tok4], w=[f"y_{sg.name}:{t}"], q="pool")

    cache_k = ein("cache_k", [8, 256, 64]); cache_v = ein("cache_v", [8, 256, 64])
    rpbg = ein("rpbg", [21, 8, 128, 128]); nmask = ein("nmask", [21, 128, 128])
    o_k = eout("o_k", [2, 8, 256, 64]); o_v = eout("o_v", [2, 8, 256, 64])
    biasd = kb.dram("biasd", [21, 128, 8, 128], BF16).ap()
    uTd = {sg.name: kb.dram(f"uTd_{sg.name}", [512, sg.L], BF16).ap() for sg in segs}
    qTd = {sg.name: kb.dram(f"qTd_{sg.name}", [512, sg.L], BF16).ap() for sg in segs}
    kTd = {sg.name: kb.dram(f"kTd_{sg.name}", [512, sg.L], BF16).ap() for sg in segs}
    Vd = {sg.name: kb.dram(f"Vd_{sg.name}", [sg.L, 520], BF16).ap() for sg in segs}
    okv = {"p0": (o_k[0], o_v[0]), "p1": (o_k[1], o_v[1])}

    es0 = ExitStack()

    def sb0(name, shape, dt=F32):
        return es0.enter_context(nc.sbuf_tensor(name, list(shape), dt))

    wkv = sb0("wkv", [128, 8, 1024], BF16)
    kb.dma(wkv[:], ev_w_in[:, 1024:2048].rearrange("(kc p) m -> p kc m", p=128), q="pool")
    stage = aT
    vst = sb0("vst", [128, 8, 65], BF16)
    kb.op("dve", lambda: V.memset(vst[:], 1.0), w=[vst])
    kvf = tok4[:, 0, :]

    def phase_evin(sg):
        for t in range(sg.nt):
            N = sg.N
            r0 = t * N
            xtile = xt[tctr[0] % 2]
            ht = hT[tctr[0] % 2]
            tctr[0] += 1
            load_xT(sg, 0, t, xtile)
            norm_mod(xtile, N, 0, 0, sg.cond, ht)

            def ev(mc, ps):
                if 4 <= mc < 8:
                    kb.op("act", lambda: S.activation(out=stage[:, mc, :N], in_=ps[:, :N], func=AF.Copy, scale=0.125), r=[ps], w=[stage])
                else:
                    kb.op("act", lambda: S.activation(out=stage[:, mc, :N], in_=ps[:, :N], func=AF.Copy), r=[ps], w=[stage])
            linear(ev_w_in16, 8, 1536, lambda kc: ht[:, kc, :N], N, ev, [ht])
            for j, dst in enumerate((uTd, qTd, kTd)):
                kb.dma(dst[sg.name][:, r0:r0 + N].rearrange("(c p) n -> p c n", p=128), stage[:, 4 * j:4 * j + 4, :N],
                       r=[stage], w=[f"{dst[sg.name].tensor.name}:{t}"], q="pool")
            for s4 in range(N // 128):
                tok0 = r0 + s4 * 128
                psv = kb.bank()
                for kc in range(8):
                    kb.op("pe", lambda: P.matmul(psv[:, :], lhsT=ht[:, kc, s4 * 128:(s4 + 1) * 128], rhs=wkv[:, kc, 512:1024],
                                                 start=(kc == 0), stop=(kc == 7)), r=[ht, wkv], w=[psv])
                kb.op("act", lambda: S.activation(out=vst[:, :, 0:64], in_=psv[:, :].rearrange("p (h d) -> p h d", h=8), func=AF.Copy),
                      r=[psv], w=[vst])
                kb.dma(Vd[sg.name][tok0:tok0 + 128, :], vst[:].rearrange("p h d -> p (h d)"), r=[vst], w=[f"Vd_{sg.name}:{tok0 // 128}"], q="pool")
                if sg.name in okv:
                    kb.op("dve", lambda: V.tensor_copy(out=kvf[:, 512:1024], in_=psv[:, :]), r=[psv], w=[tok4])
                    psk = kb.bank()
                    for kc in range(8):
                        kb.op("pe", lambda: P.matmul(psk[:, :], lhsT=ht[:, kc, s4 * 128:(s4 + 1) * 128], rhs=wkv[:, kc, 0:512],
                                                     start=(kc == 0), stop=(kc == 7)), r=[ht, wkv], w=[psk])
                    kb.op("dve", lambda: V.tensor_copy(out=kvf[:, 0:512], in_=psk[:, :]), r=[psk], w=[tok4])
                    for j in range(2):
                        kb.dma(okv[sg.name][j][:, tok0:tok0 + 128, :].rearrange("h t d -> t h d"),
                               kvf[:, j * 512:(j + 1) * 512].rearrange("p (h d) -> p h d", h=8), r=[tok4], w=[f"okv_{sg.name}_{j}:{s4}"])

    rb_f = sb0("rb_f", [128, 8, 128]); rb_m = sb0("rb_m", [128, 128]); rb_b = sb0("rb_b", [128, 8, 128], BF16)
    for ci in range(21):
        kb.dma(rb_f[:], rpbg[ci].rearrange("h q k -> q h k"))
        kb.dma(rb_m[:], nmask[ci])
        kb.op("dve", lambda: V.tensor_tensor(out=rb_b[:], in0=rb_f[:], in1=rb_m[:].unsqueeze(1).broadcast_to([128, 8, 128]), op=ALU.add),
              r=[rb_f, rb_m], w=[rb_b])
        kb.dma(biasd[ci], rb_b[:], w=[f"biasd:{ci}"])
    ckT = sb0("ckT", [128, 4, 256], BF16)
    cV = sb0("cV", [128, 2, 8, 65], BF16)
    kb.op("dve", lambda: V.memset(cV[:], 1.0), w=[cV])
    ctmp = rb_f[:].rearrange("p a b -> p (a b)")[:, 0:512].rearrange("p (h d) -> p h d", h=8)
    for tt in range(2):
        kb.dma(ctmp[:], cache_v[:, tt * 128:(tt + 1) * 128, :].rearrange("h t d -> t h d"))
        kb.op("dve", lambda: V.tensor_copy(out=cV[:, tt, :, 0:64], in_=ctmp[:]), r=[ctmp], w=[cV])
    for tt in range(2):
        kb.dma(ctmp[:], cache_k[:, tt * 128:(tt + 1) * 128, :].rearrange("h t d -> t h d"))
        ps = kb.bank()
        for hp in range(4):
            kb.op("pe", lambda: P.transpose(ps[:, hp * 128:(hp + 1) * 128], ctmp[:, 2 * hp:2 * hp + 2, :].rearrange("p h d -> p (h d)"), ident[:]),
                  r=[ctmp, ident], w=[ps])
        kb.op("dve", lambda: V.tensor_copy(out=ckT[:, :, tt * 128:(tt + 1) * 128], in_=ps[:, :].rearrange("p (c n) -> p c n", c=4)),
              r=[ps], w=[ckT])

    qm = [sb0(f"qm{i}", [128, 4, 256], BF16) for i in range(2)]
    km = [sb0(f"km{i}", [128, 4, 640], BF16) for i in range(2)]
    vm = [sb0(f"vm{i}", [128, 5, 520], BF16) for i in range(2)]
    bm = [sb0(f"bm{i}", [128, 5, 8, 128], BF16) for i in range(2)]
    PT = [sb0(f"PT{i}", [128, 1024], BF16) for i in range(2)]
    rden = sb0("rden", [128, 8])
    atok = sb0("atok", [128, 512], BF16)
    aTt = sb0("aTt", [128, 4, 128], BF16)
    actr = [0]

    def attend(sg, q0, NQ, ktiles, bias_ci0, i=None, part="both"):
        if i is None:
            i = actr[0] % 2
        actr[0] += 1
        nk = len(ktiles)
        kt0 = ktiles[0]
        name = sg.name
        if part in ("loads", "both"):
            att_loads(sg, q0, NQ, ktiles, bias_ci0, i)
        if part == "loads":
            return
        use_ctx = bias_ci0 is not None
        att_compute(sg, q0, NQ, ktiles, bias_ci0, i)

    def att_loads(sg, q0, NQ, ktiles, bias_ci0, i):
        nk = len(ktiles)
        kt0 = ktiles[0]
        name = sg.name
        kb.dma(qm[i][:, :, :NQ], qTd[name][:, q0:q0 + NQ].rearrange("(c p) n -> p c n", p=128),
               r=[f"qTd_{name}:{q0 // sg.N}"], w=[qm[i]])
        kb.dma(km[i][:, :, :nk * 128], kTd[name][:, kt0 * 128:(kt0 + nk) * 128].rearrange("(c p) n -> p c n", p=128),
               r=[f"kTd_{name}:{(kt0 * 128) // sg.N}", f"kTd_{name}:{((kt0 + nk) * 128 - 1) // sg.N}"], w=[km[i]])
        kb.dma(vm[i][:, :nk, :], Vd[name][kt0 * 128:(kt0 + nk) * 128, :].rearrange("(j p) f -> p j f", p=128),
               r=[f"Vd_{name}:{kt0 + j}" for j in range(nk)], w=[vm[i]])
        use_ctx = bias_ci0 is not None
        if use_ctx:
            kb.dma(bm[i][:, :nk], biasd[bias_ci0:bias_ci0 + nk].rearrange("j q h k -> q j h k"),
                   r=[f"biasd:{bias_ci0 + j}" for j in range(nk)], w=[bm[i]])

    def att_compute(sg, q0, NQ, ktiles, bias_ci0, i):
        nk = len(ktiles)
        kt0 = ktiles[0]
        name = sg.name
        use_ctx = bias_ci0 is not None
        nq = NQ // 128
        ntile = nk + (2 if use_ctx else 0)
        for qs in range(nq):
            poA, poB = kb.psb[6], kb.psb[7]
            def headA(h):
                hp, off = h // 2, (h % 2) * 64
                pt = PT[h % 2]
                banks = []
                for j in range(ntile):
                    if j % 4 == 0:
                        banks.append(kb.psb[kb.psn % 6]); kb.psn += 1
                    psS = banks[-1]
                    cs = slice((j % 4) * 128, (j % 4 + 1) * 128)
                    rq = qm[i][off:off + 64, hp, qs * 128:(qs + 1) * 128]
                    if j < nk:
                        kb.op("pe", lambda: P.matmul(psS[:, cs], lhsT=km[i][off:off + 64, hp, j * 128:(j + 1) * 128], rhs=rq,
                                                     start=True, stop=not use_ctx), r=[km[i], qm[i]], w=[psS])
                        if use_ctx:
                            kb.op("pe", lambda: P.matmul(psS[:, cs], lhsT=bm[i][:, j, h, :], rhs=identb[:], start=False, stop=True),
                                  r=[bm[i], identb], w=[psS])
                    else:
                        c = j - nk
                        kb.op("pe", lambda: P.matmul(psS[:, cs], lhsT=ckT[off:off + 64, hp, c * 128:(c + 1) * 128], rhs=rq,
                                                     start=True, stop=True), r=[ckT, qm[i]], w=[psS])
                for bi, psS in enumerate(banks):
                    w_ = min(4, ntile - bi * 4) * 128
                    kb.op("act", lambda: S.activation(out=pt[:, bi * 512:bi * 512 + w_], in_=psS[:, :w_], func=AF.Exp), r=[psS], w=[pt])

            def headB(h):
                pt = PT[h % 2]
                po = poA if h < 4 else poB
                oc = slice((h % 4) * 65, (h % 4) * 65 + 65)
                for j in range(ntile):
                    rv = vm[i][:, j, h * 65:(h + 1) * 65] if j < nk else cV[:, j - nk, h, :]
                    kb.op("pe", lambda: P.matmul(po[:, oc], lhsT=pt[:, j * 128:(j + 1) * 128], rhs=rv, start=(j == 0), stop=(j == ntile - 1)),
                          r=[pt, vm[i], cV], w=[po])

            headA(0)
            for h in range(8):
                if h + 1 < 8:
                    headA(h + 1)
                headB(h)
            for hb, po in enumerate((poA, poB)):
                pv = po[:, 0:260].rearrange("p (h d) -> p h d", h=4)
                kb.op("dve", lambda: V.reciprocal(out=rden[:, hb * 4:hb * 4 + 4], in_=pv[:, :, 64]), r=[po], w=[rden])
                kb.op("dve", lambda: V.tensor_tensor(out=atok[:, hb * 256:(hb + 1) * 256].rearrange("p (h d) -> p h d", h=4), in0=pv[:, :, 0:64],
                                                     in1=rden[:, hb * 4:hb * 4 + 4].unsqueeze(2).broadcast_to([128, 4, 64]), op=ALU.mult),
                      r=[po, rden], w=[atok])
            pst = kb.psb[kb.psn % 6]; kb.psn += 1
            pstb = pst[:].bitcast(BF16)
            for c4 in range(4):
                kb.op("pe", lambda: P.transpose(pstb[:, c4 * 128:(c4 + 1) * 128], atok[:, c4 * 128:(c4 + 1) * 128], identb[:]),
                      r=[atok, identb], w=[pst])
            kb.op("dve", lambda: V.tensor_copy(out=aTt[:], in_=pstb[:, 0:512].rearrange("p (c n) -> p c n", c=4)), r=[pst], w=[aTt])
            qq = q0 + qs * 128
            kb.dma(mixT[name][512:1024, qq:qq + 128].rearrange("(c p) n -> p c n", p=128), aTt[:], r=[aTt],
                   w=[f"mixT_{name}:{qq // sg.N}"])

    def lo(r):
        return min(max(r - 4, 0), 56)

    def phase_attn(sg):
        if sg.name != "s":
            attend(sg, 0, 256, [0, 1], None)
            return
        def args(m):
            lt, ht_ = lo(2 * m) // 2, (lo(2 * m + 1) + 7) // 2
            cls = {0: 5, 1: 9, 30: 13, 31: 17}.get(m, 0)
            return (sg, m * 128, 128, list(range(lt, ht_ + 1)), cls)
        attend(*args(0), i=0, part="loads")
        for m in range(32):
            if m + 1 < 32:
                attend(*args(m + 1), i=(m + 1) % 2, part="loads")
            attend(*args(m), i=m % 2, part="compute")


    s5_a = ein("s5_a", [128, 3, 32])
    s5_B = ein("s5_B", [128, 2, 32, 16])
    s5_C = ein("s5_C", [128, 2, 32, 16])
    s5_h0 = ein("s5_h0", [128, 2, 32])
    s5_dcol = ein("s5_dcol", [128, 4])
    s5_w_glu = ein("s5_w_glu", [512, 512])
    o_s5 = eout("o_s5", [2, 2, 128, 32])
    tabd = kb.dram("tabd", [32, 128, 2, 512], BF16).ap()
    yaccd = {sg.name: kb.dram(f"yaccd_{sg.name}", [512, sg.L]).ap() for sg in segs}
    ygd = {sg.name: kb.dram(f"ygd_{sg.name}", [512, sg.L], BF16).ap() for sg in segs}
    PI = math.pi

    def phase_s5():
        es = ExitStack()

        def sb1(name, shape, dt=F32):
            return es.enter_context(nc.sbuf_tensor(name, list(shape), dt))

        es2 = ExitStack()

        def sb2(name, shape, dt=F32):
            return es2.enter_context(nc.sbuf_tensor(name, list(shape), dt))

        pa = sb1("s5pa", [128, 3, 32]); ph0 = sb1("s5ph0", [128, 2, 32]); dcol = sb1("s5dcol", [128, 4])
        sm = sb1("s5sm", [128, 16, 32])
        LB = sb1("s5LB", [128, 64, 128], BF16)
        CM = sb1("s5CM", [128, 64, 128], BF16)
        ET = sb1("s5ET", [128, 2, 32])
        E1 = sb1("s5E1", [128, 2, 32])
        tbl = [sb1(f"s5tl{i}", [128, 2, 512], BF16) for i in range(3)]
        tlast = sb1("s5tlast", [128, 2, 32])
        pB = sb2("s5pB", [128, 2, 32, 16]); pC = sb2("s5pC", [128, 2, 32, 16])
        kb.dma(pa[:], s5_a); kb.dma(pB[:], s5_B); kb.dma(pC[:], s5_C); kb.dma(ph0[:], s5_h0); kb.dma(dcol[:], s5_dcol)
        ARE, DT, LRE, TH, RR, CS, SN, ABR, ABI, DEN, WRE, WIM, T0, T1, T2, T3 = [sm[:, i, :] for i in range(16)]

        def vts(out, in0, s1, s2, op0, op1=None):
            if op1 is None:
                kb.op("dve", lambda: V.tensor_scalar(out=out, in0=in0, scalar1=s1, scalar2=None, op0=op0), r=[in0], w=[out])
            else:
                kb.op("dve", lambda: V.tensor_scalar(out=out, in0=in0, scalar1=s1, scalar2=s2, op0=op0, op1=op1), r=[in0], w=[out])

        def vtt(out, a, b, op, e="dve"):
            E = V if e == "dve" else G
            kb.op(e, lambda: E.tensor_tensor(out=out, in0=a, in1=b, op=op), r=[a, b], w=[out])

        vts(ARE, pa[:, 0, :], -1e-4, None, ALU.min)
        kb.op("act", lambda: S.activation(out=DT, in_=pa[:, 2, :], func=AF.Exp), r=[pa], w=[sm])
        vtt(LRE, ARE, DT, ALU.mult)
        vtt(TH, pa[:, 1, :], DT, ALU.mult)
        kb.op("act", lambda: S.activation(out=RR, in_=LRE, func=AF.Exp), r=[sm], w=[sm])

        def sincos(out, shift):
            vts(T0, TH, shift, None, ALU.add)
            vts(T1, T0, 0.0, None, ALU.add)
            for j in range(1, 6):
                vts(T2, T0, (2 * j - 1) * PI, -2 * PI, ALU.is_ge, ALU.mult)
                vtt(T1, T1, T2, ALU.add)
            kb.op("act", lambda: S.activation(out=out, in_=T1, func=AF.Sin), r=[sm], w=[sm])
        sincos(SN, 0.0)
        sincos(CS, PI / 2)
        vtt(ABR, RR, CS, ALU.mult); vtt(ABI, RR, SN, ALU.mult)
        vtt(T0, ARE, ARE, ALU.mult); vtt(T1, pa[:, 1, :], pa[:, 1, :], ALU.mult); vtt(DEN, T0, T1, ALU.add)
        kb.op("dve", lambda: V.reciprocal(out=DEN, in_=DEN), r=[sm], w=[sm])
        vts(T3, ABR, -1.0, None, ALU.add)
        vtt(T0, T3, ARE, ALU.mult); vtt(T1, ABI, pa[:, 1, :], ALU.mult); vtt(T0, T0, T1, ALU.add); vtt(WRE, T0, DEN, ALU.mult)
        vtt(T0, ABI, ARE, ALU.mult); vtt(T1, T3, pa[:, 1, :], ALU.mult); vtt(T0, T0, T1, ALU.subtract); vtt(WIM, T0, DEN, ALU.mult)
        Bb = sb2("s5Bb", [128, 2, 32, 16])
        tB = sb2("s5tB", [128, 32, 16])
        bc = lambda x: x.unsqueeze(2).broadcast_to([128, 32, 16])
        vtt(Bb[:, 0], pB[:, 0], bc(WRE), ALU.mult); vtt(tB[:], pB[:, 1], bc(WIM), ALU.mult); vtt(Bb[:, 0], Bb[:, 0], tB[:], ALU.subtract)
        vtt(Bb[:, 1], pB[:, 1], bc(WRE), ALU.mult); vtt(tB[:], pB[:, 0], bc(WIM), ALU.mult); vtt(Bb[:, 1], Bb[:, 1], tB[:], ALU.add)
        kb.op("dve", lambda: V.memset(CM[:], 0.0), w=[CM])
        Z = sb2("s5Z", [128, 128])
        for db in range(32):
            b = db % 16
            base = 32 * (b % 4)
            for part in range(2):
                kb.op("dve", lambda: V.memset(Z[:], 0.0), w=[Z])
                for gi in range(2):
                    ps_ = slice(64 * gi, 64 * gi + 64)
                    cs_ = slice(base + 16 * gi, base + 16 * gi + 16)
                    kb.op("dve", lambda: V.tensor_copy(out=Z[ps_, cs_], in_=Bb[ps_, part, db, :]), r=[Bb], w=[Z])
                    if part == 0:
                        kb.op("pool", lambda: G.tensor_copy(out=CM[ps_, 2 * db, cs_], in_=pC[ps_, 0, db, :]), r=[pC], w=[CM])
                    else:
                        kb.op("pool", lambda: G.tensor_scalar(out=CM[ps_, 2 * db + 1, cs_], in0=pC[ps_, 1, db, :], scalar1=-1.0, scalar2=None,
                                                              op0=ALU.mult), r=[pC], w=[CM])
                ps = kb.bank()
                kb.op("pe", lambda: P.transpose(ps[:, 0:128], Z[:], ident[:]), r=[Z, ident], w=[ps])
                kb.op("act", lambda: S.activation(out=LB[:, 2 * db + part, :], in_=ps[:, 0:128], func=AF.Copy), r=[ps], w=[LB])
        kb.op("dve", lambda: V.tensor_copy(out=E1[:, 0, :], in_=CS), r=[sm], w=[E1])
        kb.op("dve", lambda: V.tensor_copy(out=E1[:, 1, :], in_=SN), r=[sm], w=[E1])
        ckA = sb2("s5ckA", [128, 2, 32, 11]); sq1 = sb2("s5sq1", [128, 3, 32])
        kb.op("dve", lambda: V.tensor_copy(out=ckA[:, 0, :, 0], in_=CS), r=[sm], w=[ckA])
        kb.op("dve", lambda: V.tensor_copy(out=ckA[:, 1, :, 0], in_=SN), r=[sm], w=[ckA])
        for k in range(9):
            c_k, s_k = ckA[:, 0, :, k], ckA[:, 1, :, k]
            vtt(sq1[:, 0, :], c_k, c_k, ALU.mult); vtt(sq1[:, 1, :], s_k, s_k, ALU.mult)
            vtt(ckA[:, 0, :, k + 1], sq1[:, 0, :], sq1[:, 1, :], ALU.subtract)
            vtt(sq1[:, 2, :], c_k, s_k, ALU.mult)
            vts(ckA[:, 1, :, k + 1], sq1[:, 2, :], 2.0, None, ALU.mult)
        kb.op("dve", lambda: V.tensor_copy(out=ET[:, 0, :], in_=ckA[:, 0, :, 9]), r=[ckA], w=[ET])
        kb.op("dve", lambda: V.tensor_copy(out=ET[:, 1, :], in_=ckA[:, 1, :, 9]), r=[ckA], w=[ET])
        NBT = 2
        tabB = sb2("s5tabB", [128, 2, NBT, 512]); tmA = sb2("s5tmA", [128, NBT, 256]); tmB = sb2("s5tmB", [128, NBT, 256])
        tab16B = sb2("s5tab16B", [128, 2, NBT, 512], BF16)
        for b0 in range(0, 32, NBT):
            kb.op("dve", lambda: V.memset(tabB[:, 0, :, 0:1], 1.0), w=[tabB])
            kb.op("dve", lambda: V.memset(tabB[:, 1, :, 0:1], 0.0), w=[tabB])
            for k in range(9):
                n = 1 << k
                ckc = ckA[:, 0, b0:b0 + NBT, k:k + 1].broadcast_to([128, NBT, n])
                cks = ckA[:, 1, b0:b0 + NBT, k:k + 1].broadcast_to([128, NBT, n])
                sc, ss = tabB[:, 0, :, 0:n], tabB[:, 1, :, 0:n]
                vtt(tmA[:, :, :n], ss, cks, ALU.mult, "pool"); vtt(tmB[:, :, :n], sc, ckc, ALU.mult)
                kb.op("dve", lambda: V.tensor_tensor(out=tabB[:, 0, :, n:2 * n], in0=tmB[:, :, :n], in1=tmA[:, :, :n], op=ALU.subtract),
                      r=[tmA, tmB], w=[tabB])
                vtt(tmA[:, :, :n], ss, ckc, ALU.mult, "pool"); vtt(tmB[:, :, :n], sc, cks, ALU.mult)
                kb.op("dve", lambda: V.tensor_tensor(out=tabB[:, 1, :, n:2 * n], in0=tmB[:, :, :n], in1=tmA[:, :, :n], op=ALU.add),
                      r=[tmA, tmB], w=[tabB])
            kb.op("act", lambda: S.activation(out=tab16B[:], in_=tabB[:], func=AF.Copy), r=[tabB], w=[tab16B])
            kb.op("pool", lambda: G.tensor_copy(out=tlast[:, :, b0:b0 + NBT], in_=tabB[:, :, :, 255]), r=[tabB], w=[tlast])
            kb.dma(tabd[b0:b0 + NBT].rearrange("b p c n -> p c b n"), tab16B[:], r=[tab16B], w=[f"tabd:{b0 + j}" for j in range(NBT)])

        kb.barrier()
        es2.close()
        uTt = [sb1(f"s5u{i}", [128, 4, 512], BF16) for i in range(2)]
        w1s = [sb1("s5w1", [128, 512], BF16)] * 2; w2s = [sb1("s5w2", [128, 512], BF16)] * 2
        w3s = [sb1("s5w3", [128, 512], BF16)] * 2; w4s = [sb1("s5w4", [128, 512], BF16)] * 2
        bt1s = [sb1(f"s5bt1{i}", [128, 512]) for i in range(2)]; bt3s = [sb1(f"s5bt3{i}", [128, 512]) for i in range(2)]
        negs = sb1("s5negs", [128, 2, 32])
        kb.op("dve", lambda: V.tensor_scalar(out=negs[:, 0, :], in0=ET[:, 1, :], scalar1=-1.0, scalar2=None, op0=ALU.mult), r=[ET], w=[negs])
        kb.op("dve", lambda: V.tensor_scalar(out=negs[:, 1, :], in0=tlast[:, 1, :], scalar1=-1.0, scalar2=None, op0=ALU.mult), r=[tlast], w=[negs])
        hres = [sb1("s5hre", [128, 512])]; hims = [sb1("s5him", [128, 512])]
        w2, w4 = w2s[0], w4s[0]
        hbr = [sb1(f"s5hbr{i}", [128, 512], BF16) for i in range(2)]; hbi = [sb1(f"s5hbi{i}", [128, 512], BF16) for i in range(2)]
        bres = [sb1("s5bre", [128, 512], BF16)] * 2; bims = [sb1("s5bim", [128, 512], BF16)] * 2
        hcr = [sb1(f"s5hcr{i}", [128, 512], BF16) for i in range(2)]; hci = [sb1(f"s5hci{i}", [128, 512], BF16) for i in range(2)]
        p1s = [sb1("s5p1", [128, 512], BF16)] * 2; p2s = [sb1("s5p2", [128, 512], BF16)] * 2
        q1s = [sb1("s5q1", [128, 512], BF16)] * 2; q2s = [sb1("s5q2", [128, 512], BF16)] * 2
        carry = sb1("s5carry", [128, 2, 32]); ctmp2 = sb1("s5ct", [128, 4])
        fin = sb1("s5fin", [128, 2, 32])
        yv = sb1("s5yv", [128, 512]); ya = sb1("s5ya", [128, 512]); yb16 = sb1("s5yb", [128, 512], BF16)
        utc = [0]
        cnt = [0]

        def run(sg, d, pj):
            TT = min(512, sg.L)
            ntt = sg.L // TT
            order = list(range(ntt)) if d == 0 else list(range(ntt - 1, -1, -1))
            rv = (lambda ap: ap) if d == 0 else (lambda ap: ap[:, ::-1])
            dsl = slice(16 * d, 16 * d + 16)
            if sg.name == "s":
                vtt(T0[:, dsl], ph0[:, 0, dsl], CS[:, dsl], ALU.mult); vtt(T1[:, dsl], ph0[:, 1, dsl], SN[:, dsl], ALU.mult)
                vtt(carry[:, 0, dsl], T0[:, dsl], T1[:, dsl], ALU.subtract)
                vtt(T0[:, dsl], ph0[:, 0, dsl], SN[:, dsl], ALU.mult); vtt(T1[:, dsl], ph0[:, 1, dsl], CS[:, dsl], ALU.mult)
                vtt(carry[:, 1, dsl], T0[:, dsl], T1[:, dsl], ALU.add)
            else:
                kb.op("dve", lambda: V.memset(carry[:, :, dsl], 0.0), w=[carry])
            blocks = []
            for ti, tt in enumerate(order):
                for c in range(4):
                    for bi in range(4):
                        blocks.append(dict(ti=ti, tt=tt, c=c, bi=bi))
            psy = kb.psb[6:8]

            pending = []

            def stage1(x):
                while pending:
                    pending.pop(0)()
                cnt[0] += 1
                k = cnt[0]
                x["k"] = k
                ti, tt, c, bi = x["ti"], x["tt"], x["c"], x["bi"]
                t0 = tt * TT
                if c == 0 and bi == 0:
                    utc[0] += 1
                    ut = uTt[utc[0] % 2]
                    kb.dma(ut[:, :, :TT], uTd[sg.name][:, t0:t0 + TT].rearrange("(c p) n -> p c n", p=128),
                           r=[f"uTd_{sg.name}:{j}" for j in range(t0 // sg.N, (t0 + TT - 1) // sg.N + 1)], w=[ut])
                ut = uTt[utc[0] % 2]
                x["ut"] = ut
                db = 16 * d + 4 * c + bi
                x["db"] = db
                tl = tbl[k % 3]
                x["tl"] = tl
                kb.dma(tl[:, :, :TT], tabd[db][:, :, :TT], r=[f"tabd:{db}"], w=[tl])
                cs_, sn_ = rv(tl[:, 0, :TT]), rv(tl[:, 1, :TT])
                w1, w2, w3, w4 = (z[k % 2] for z in (w1s, w2s, w3s, w4s))
                bt1, bt3 = bt1s[k % 2], bt3s[k % 2]
                x["w"] = (bt1, bt3)
                pss = []
                for part in range(2):
                    ps = kb.psb[kb.psn % 6]; kb.psn += 1
                    kb.op("pe", lambda: P.matmul(ps[:, :TT], lhsT=LB[:, 2 * db + part, :], rhs=ut[:, c, :TT], start=True, stop=True),
                          r=[LB, ut], w=[ps])
                    pss.append(ps)
                bre, bim = bres[0], bims[0]
                kb.op("act", lambda: S.activation(out=bre[:, :TT], in_=pss[0][:, :TT], func=AF.Copy), r=[pss[0]], w=[bre])
                kb.op("act", lambda: S.activation(out=bim[:, :TT], in_=pss[1][:, :TT], func=AF.Copy), r=[pss[1]], w=[bim])
                vtt(w1[:, :TT], bre[:, :TT], cs_, ALU.mult); vtt(w2[:, :TT], bim[:, :TT], sn_, ALU.mult)
                vtt(w3[:, :TT], bim[:, :TT], cs_, ALU.mult); vtt(w4[:, :TT], bre[:, :TT], sn_, ALU.mult)
                vtt(bt1[:, :TT], w1[:, :TT], w2[:, :TT], ALU.add, "pool")
                vtt(bt3[:, :TT], w3[:, :TT], w4[:, :TT], ALU.subtract, "pool")

            def cplx_act(o_re, o_im, o_t, x_re, x_im, xr_t, xi_t, c_, s_, ns_):
                kb.op("act", lambda: S.activation(out=ctmp2[:, 0:1], in_=x_im, func=AF.Identity, scale=ns_), r=[xi_t, negs], w=[ctmp2])
                kb.op("act", lambda: S.activation(out=o_re, in_=x_re, func=AF.Identity, scale=c_, bias=ctmp2[:, 0:1]), r=[xr_t, ctmp2, ET, tlast], w=[o_t])
                kb.op("act", lambda: S.activation(out=ctmp2[:, 1:2], in_=x_re, func=AF.Identity, scale=s_), r=[xr_t, ET, tlast], w=[ctmp2])
                kb.op("act", lambda: S.activation(out=o_im, in_=x_im, func=AF.Identity, scale=c_, bias=ctmp2[:, 1:2]), r=[xi_t, ctmp2, ET, tlast], w=[o_t])

            def stage2(x):
                k, db, ti = x["k"], x["db"], x["ti"]
                w1, w3 = x["w"]
                hre, him = hres[0], hims[0]
                hbre, hbim = hcr[k % 2], hci[k % 2]
                x["hb"] = (hbre, hbim)
                rbc = RR[:, db:db + 1].broadcast_to([128, TT])
                kb.op("dve", lambda: V.tensor_tensor_scan(out=rv(hre[:, :TT]), data0=rbc, data1=rv(w1[:, :TT]), initial=carry[:, 0, db:db + 1],
                                                          op0=ALU.mult, op1=ALU.add), r=[sm, w1, carry], w=[hre])
                kb.op("dve", lambda: V.tensor_tensor_scan(out=rv(him[:, :TT]), data0=rbc, data1=rv(w3[:, :TT]), initial=carry[:, 1, db:db + 1],
                                                          op0=ALU.mult, op1=ALU.add), r=[sm, w3, carry], w=[him])
                kb.op("act", lambda: S.activation(out=hbre[:, :TT], in_=hre[:, :TT], func=AF.Copy), r=[hre], w=[hbre])
                kb.op("act", lambda: S.activation(out=hbim[:, :TT], in_=him[:, :TT], func=AF.Copy), r=[him], w=[hbim])
                lastc = TT - 1 if d == 0 else 0
                hl_re, hl_im = hre[:, lastc:lastc + 1], him[:, lastc:lastc + 1]
                if ti < len(order) - 1:
                    cplx_act(carry[:, 0, db:db + 1], carry[:, 1, db:db + 1], carry, hl_re, hl_im, hre, him,
                             ET[:, 0, db:db + 1], ET[:, 1, db:db + 1], negs[:, 0, db:db + 1])
                elif pj is not None:
                    assert TT == 256
                    cl, sl = tlast[:, 0, db:db + 1], tlast[:, 1, db:db + 1]
                    cplx_act(fin[:, 0, db:db + 1], fin[:, 1, db:db + 1], fin, hl_re, hl_im, hre, him, cl, sl, negs[:, 1, db:db + 1])

            def stage3(x):
                k, db, c, bi, tt, tl, ut = x["k"], x["db"], x["c"], x["bi"], x["tt"], x["tl"], x["ut"]
                t0 = tt * TT
                hbre, hbim = x["hb"]
                cosf, sinf = rv(tl[:, 0, :TT]), rv(tl[:, 1, :TT])
                hb_r, hb_i = hbr[k % 2], hbi[k % 2]
                p1, p2, q1, q2 = p1s[0], p2s[0], q1s[0], q2s[0]
                vtt(p1[:, :TT], hbre[:, :TT], cosf, ALU.mult, "pool"); vtt(p2[:, :TT], hbim[:, :TT], sinf, ALU.mult)
                vtt(hb_r[:, :TT], p1[:, :TT], p2[:, :TT], ALU.subtract, "pool")
                vtt(q1[:, :TT], hbre[:, :TT], sinf, ALU.mult); vtt(q2[:, :TT], hbim[:, :TT], cosf, ALU.mult)
                vtt(hb_i[:, :TT], q1[:, :TT], q2[:, :TT], ALU.add)
                py = psy[c % 2]
                kb.op("pe", lambda: P.matmul(py[:, :TT], lhsT=CM[:, 2 * db, :], rhs=hb_r[:, :TT], start=(bi == 0), stop=False), r=[CM, hb_r], w=[py])
                kb.op("pe", lambda: P.matmul(py[:, :TT], lhsT=CM[:, 2 * db + 1, :], rhs=hb_i[:, :TT], start=False, stop=(bi == 3)), r=[CM, hb_i], w=[py])
                if bi != 3:
                    return
                key = f"yacc_{sg.name}:{tt}:{c}"
                if d == 0:
                    kb.op("dve", lambda: V.scalar_tensor_tensor(out=yv[:, :TT], in0=ut[:, c, :TT], scalar=dcol[:, c:c + 1], in1=py[:, :TT],
                                                                op0=ALU.mult, op1=ALU.add), r=[ut, dcol, py], w=[yv])
                    pending.append(lambda: kb.dma(yaccd[sg.name][c * 128:(c + 1) * 128, t0:t0 + TT], yv[:, :TT], r=[yv], w=[key]))
                else:
                    kb.dma(ya[:, :TT], yaccd[sg.name][c * 128:(c + 1) * 128, t0:t0 + TT], r=[key], w=[ya])
                    kb.op("dve", lambda: V.tensor_tensor(out=yv[:, :TT], in0=py[:, :TT], in1=ya[:, :TT], op=ALU.add), r=[py, ya], w=[yv])
                    vtt(ya[:, :TT], yv[:, :TT], yv[:, :TT], ALU.mult, "pool")
                    vts(ya[:, :TT], ya[:, :TT], 0.044715, 1.0, ALU.mult, ALU.add)
                    vtt(ya[:, :TT], ya[:, :TT], yv[:, :TT], ALU.mult, "pool")
                    kb.op("act", lambda: S.activation(out=ya[:, :TT], in_=ya[:, :TT], func=AF.Sigmoid, scale=1.5957691216057308), r=[ya], w=[ya])
                    vtt(yb16[:, :TT], ya[:, :TT], yv[:, :TT], ALU.mult)
                    pending.append(lambda: kb.dma(ygd[sg.name][c * 128:(c + 1) * 128, t0:t0 + TT], yb16[:, :TT], r=[yb16],
                                                  w=[f"ygd_{sg.name}:{tt}:{c}"]))

            n = len(blocks)
            for step in range(n + 2):
                if step < n:
                    stage1(blocks[step])
                if 0 <= step - 1 < n:
                    stage2(blocks[step - 1])
                if 0 <= step - 2 < n:
                    stage3(blocks[step - 2])
            while pending:
                pending.pop(0)()
            if pj is not None and d == 1:
                for part in range(2):
                    kb.dma(o_s5[pj, part], fin[:, part, :], r=[fin], w=[f"o_s5:{pj}:{part}"])

        for sg, pj in ((segs[0], None), (segs[1], 0), (segs[2], 1)):
            run(sg, 0, pj)
            run(sg, 1, pj)

        ygt = [hT[0], hT[1]]
        for sg in segs:
            TT = min(512, sg.L)
            for t in range(sg.nt):
                N = sg.N
                r0 = t * N
                yt = ygt[t % 2]
                kb.dma(yt[:, 0:4, :N], ygd[sg.name][:, r0:r0 + N].rearrange("(c p) n -> p c n", p=128),
                       r=[f"ygd_{sg.name}:{r0 // TT}:{c}" for c in range(4)], w=[yt])

                def ev(mc, ps):
                    kb.op("act", lambda: S.activation(out=w4[:, :N], in_=ps[:, :N], func=AF.Sigmoid), r=[ps], w=[w4])
                    kb.op("dve", lambda: V.tensor_tensor(out=aT[:, mc, :N], in0=w4[:, :N], in1=yt[:, mc, :N], op=ALU.mult), r=[w4, yt], w=[aT])
                linear(s5_w_glu, 4, 512, lambda kc: yt[:, kc, :N], N, ev, [yt])
                kb.dma(mixT[sg.name][0:512, r0:r0 + N].rearrange("(c p) n -> p c n", p=128), aT[:, 0:4, :N], r=[aT], w=[f"mixT_{sg.name}:{t}"])
        kb.barrier()
        es.close()


    od_conv_w = ein("od_conv_w", [5, 3072]); od_conv_b = ein("od_conv_b", [3072])
    ssd_alog = ein("ssd_a_log", [64]); ssd_dtb = ein("ssd_dt_bias", [64])
    ssd_dsk = ein("ssd_d", [32]); ssd_norm_g = ein("ssd_norm_g", [2048])
    ssd_h0 = ein("ssd_h0", [2, 2048, 128])
    tri_in = ein("tri", [2, 128, 128]); snm_in = ein("ssd_nmask", [2, 128, 128]); ones_in = ein("ones_f", [128, 128])
    o_ssd = eout("o_ssd", [2, 2, 2048, 128])
    zsd = {sg.name: kb.dram(f"zsd_{sg.name}", [2048, sg.L], BF16).ap() for sg in segs}
    xbcd = {sg.name: kb.dram(f"xbcd_{sg.name}", [3072, sg.L], BF16).ap() for sg in segs}
    dtd = {sg.name: kb.dram(f"dtd_{sg.name}", [sg.L, 64]).ap() for sg in segs}
    xtd = {sg.name: kb.dram(f"xtd_{sg.name}", [sg.L, 2048], BF16).ap() for sg in segs}
    bcTd = {sg.name: kb.dram(f"bcTd_{sg.name}", [1024, sg.L], BF16).ap() for sg in segs}
    btd = {sg.name: kb.dram(f"btd_{sg.name}", [sg.L, 512], BF16).ap() for sg in segs}
    yfd = {sg.name: kb.dram(f"yfd_{sg.name}", [sg.L, 2048]).ap() for sg in segs}
    yTd = {sg.name: kb.dram(f"yTd_{sg.name}", [2048, sg.L], BF16).ap() for sg in segs}

    def phase_odin():
        es = ExitStack()
        sb1 = lambda name, shape, dt=F32: es.enter_context(nc.sbuf_tensor(name, list(shape), dt))
        wdt = sb1("o1wdt", [128, 8, 64], BF16)
        kb.dma(wdt[:], od_w_in[:, 5120:5184].rearrange("(kc p) m -> p kc m", p=128), q="pool")
        dtb = sb1("o1dtb", [128, 64])
        kb.dma(dtb[:], ssd_dtb.partition_broadcast(128))
        xst = [sb1(f"o1xst{i}", [128, 512], BF16) for i in range(3)]
        dtt = sb1("o1dtt", [128, 64]); dte = sb1("o1dte", [128, 64])
        xc = [0]
        sqp = sb1("o1sq", [128, 8, 512], BF16); rsp = sb1("o1rs", [128, 512]); tmp_ = sb1("o1tm", [128, 512])
        more_wbufs(es, 3)
        tiles = [(sg, t) for sg in segs for t in range(sg.nt)]
        bufs = {}

        def prep(i):
            sg, t = tiles[i]
            xtile = xt[tctr[0] % 2]
            ht = hT[tctr[0] % 2]
            tctr[0] += 1
            bufs[i] = ht
            load_xT(sg, 0, t, xtile)
            norm_mod(xtile, sg.N, 1, 0, sg.cond, ht, sq=sqp, rstd=rsp, tmpf=tmp_)

        prep(0)
        for i in range(len(tiles)):
            if i + 1 < len(tiles):
                prep(i + 1)
            if True:
                sg, t = tiles[i]
                N = sg.N
                r0 = t * N
                ht = bufs.pop(i)

                def ev(mc, ps):
                    if mc < 16:
                        kb.op("act", lambda: S.activation(out=aT[:, mc, :N], in_=ps[:, :N], func=AF.Silu), r=[ps], w=[aT])
                    else:
                        xs_ = xst[xc[0] % 3]
                        xc[0] += 1
                        kb.op("dve", lambda: V.tensor_copy(out=xs_[:, :N], in_=ps[:, :N]), r=[ps], w=[xs_])
                        kb.dma(xbcd[sg.name][(mc - 16) * 128:(mc - 15) * 128, r0:r0 + N], xs_[:, :N], r=[xs_],
                               w=[f"xbcd_{sg.name}:{t}"], q="pool")
                linear(od_w_in16, 8, 5120, lambda kc: ht[:, kc, :N], N, ev, [ht])
                kb.dma(zsd[sg.name][:, r0:r0 + N].rearrange("(c p) n -> p c n", p=128), aT[:, 0:16, :N], r=[aT], w=[f"zsd_{sg.name}:{t}"], q="pool")
                for s4 in range(N // 128):
                    tok0 = r0 + s4 * 128
                    ps = kb.bank()
                    for kc in range(8):
                        kb.op("pe", lambda: P.matmul(ps[:, 0:64], lhsT=ht[:, kc, s4 * 128:(s4 + 1) * 128], rhs=wdt[:, kc, :],
                                                     start=(kc == 0), stop=(kc == 7)), r=[ht, wdt], w=[ps])
                    kb.op("dve", lambda: V.tensor_tensor(out=dtt[:], in0=ps[:, 0:64], in1=dtb[:], op=ALU.add), r=[ps, dtb], w=[dtt])
                    kb.op("act", lambda: S.activation(out=dte[:], in_=dtt[:], func=AF.Exp), r=[dtt], w=[dte])
                    kb.op("act", lambda: S.activation(out=dtt[:], in_=dte[:], func=AF.Ln, bias=1.0, scale=1.0), r=[dte], w=[dtt])
                    kb.dma(dtd[sg.name][tok0:tok0 + 128, :], dtt[:], r=[dtt], w=[f"dtd_{sg.name}:{tok0 // 128}"], q="pool")
        kb.barrier()
        less_wbufs()
        es.close()

    def phase_conv():
        es = ExitStack()
        sb1 = lambda name, shape, dt=F32: es.enter_context(nc.sbuf_tensor(name, list(shape), dt))
        cw = sb1("o2cw", [128, 24, 5]); cbias = sb1("o2cb", [128, 24])
        for k in range(5):
            kb.dma(cw[:, :, k], od_conv_w[k].rearrange("(c p) -> p c", p=128), w=[cw], allow_slow_non_contiguous=True)
        kb.dma(cbias[:], od_conv_b.rearrange("(c p) -> p c", p=128), allow_slow_non_contiguous=True)
        xall = sb1("o2xall", [128, 24, 516], BF16)
        cvall = sb1("o2cvall", [128, 8, 512], BF16)
        tkall = sb1("o2tkall", [128, 4, 20, 128], BF16)
        DG = sb1("o2DG", [128, 120, 128], BF16)
        for cc in range(24):
            for k in range(5):
                kb.op("dve" if (cc + k) % 2 else "pool",
                      (lambda: V.tensor_scalar(out=DG[:, cc * 5 + k, :], in0=identb[:], scalar1=cw[:, cc, k:k + 1], scalar2=None, op0=ALU.mult))
                      if (cc + k) % 2 else
                      (lambda: G.tensor_scalar(out=DG[:, cc * 5 + k, :], in0=identb[:], scalar1=cw[:, cc, k:k + 1], scalar2=None, op0=ALU.mult)),
                      r=[identb, cw], w=[DG])
        cvb = [sb1(f"o2cvb{i}", [128, 512], BF16) for i in range(2)]
        ctr = 0
        for sg in segs:
            for t in range(sg.nt):
                N = sg.N
                ns = N // 128
                r0 = t * N
                lo_, hi_ = max(0, r0 - 2), min(sg.L, r0 + N + 2)
                if r0 == 0:
                    kb.op("pool", lambda: G.memset(xall[:, :, 0:2], 0.0), w=[xall])
                if r0 + N == sg.L:
                    kb.op("pool", lambda: G.memset(xall[:, :, N + 2:N + 4], 0.0), w=[xall])
                tl_ = sorted(set([max(0, (lo_) // N), min(sg.nt - 1, (hi_ - 1) // N)] + [t]))
                kb.dma(xall[:, :, lo_ - (r0 - 2):hi_ - (r0 - 2)], xbcd[sg.name][:, lo_:hi_].rearrange("(c p) n -> p c n", p=128),
                       r=[f"xbcd_{sg.name}:{j}" for j in tl_], w=[xall])
                for cc in range(24):
                    ctr += 1
                    cv = cvall[:, cc - 16, :] if cc >= 16 else cvb[ctr % 2]
                    cvk = cvall if cc >= 16 else cvb[ctr % 2]
                    pc = kb.bank()
                    for k in range(5):
                        kb.op("pe", lambda: P.matmul(pc[:, :N], lhsT=DG[:, cc * 5 + k, :], rhs=xall[:, cc, k:k + N], start=(k == 0), stop=(k == 4)),
                              r=[DG, xall], w=[pc])
                    kb.op("act", lambda: S.activation(out=cv[:, :N], in_=pc[:, :N], func=AF.Silu, bias=cbias[:, cc:cc + 1]), r=[pc, cbias], w=[cvk])
                    if cc < 20:
                        ps = kb.bank()
                        psb_ = ps[:].bitcast(BF16)
                        for s4 in range(ns):
                            kb.op("pe", lambda: P.transpose(psb_[:, s4 * 128:(s4 + 1) * 128], cv[:, s4 * 128:(s4 + 1) * 128], identb[:]),
                                  r=[cvk, identb], w=[ps])
                        kb.op("dve", lambda: V.tensor_copy(out=tkall[:, :ns, cc, :], in_=psb_[:, 0:ns * 128].rearrange("p (s c) -> p s c", s=ns)),
                              r=[ps], w=[tkall])
                kb.dma(bcTd[sg.name][:, r0:r0 + N].rearrange("(c p) n -> p c n", p=128), cvall[:, :, :N], r=[cvall], w=[f"bcTd_{sg.name}:{t}"])
                kb.dma(xtd[sg.name][r0:r0 + N, :].rearrange("(s p) (c k) -> p s c k", p=128, k=128), tkall[:, :ns, 0:16, :], r=[tkall],
                       w=[f"tokd_{sg.name}:{t}"])
                kb.dma(btd[sg.name][r0:r0 + N, :].rearrange("(s p) (c k) -> p s c k", p=128, k=128), tkall[:, :ns, 16:20, :], r=[tkall],
                       w=[f"tokd_{sg.name}:{t}"])
        kb.barrier()
        es.close()

    def phase_scan():
        es = ExitStack()
        sb1 = lambda name, shape, dt=F32: es.enter_context(nc.sbuf_tensor(name, list(shape), dt))
        tri = sb1("o3tri", [128, 2, 128]); snm = sb1("o3snm", [128, 2, 128]); snmb = sb1("o3snmb", [128, 2, 128], BF16)
        onesf = sb1("o3ones", [128, 128])
        for d in range(2):
            kb.dma(tri[:, d, :], tri_in[d], w=[tri]); kb.dma(snm[:, d, :], snm_in[d], w=[snm])
        kb.dma(onesf[:], ones_in)
        kb.op("dve", lambda: V.tensor_copy(out=snmb[:], in_=snm[:]), r=[snm], w=[snmb])
        abc = sb1("o3abc", [128, 64]); dsk = sb1("o3dsk", [128, 32])
        kb.dma(abc[:], ssd_alog.partition_broadcast(128)); kb.dma(dsk[:], ssd_dsk.partition_broadcast(128))
        kb.op("act", lambda: S.activation(out=abc[:], in_=abc[:], func=AF.Exp), r=[abc], w=[abc])
        kb.op("dve", lambda: V.tensor_scalar(out=abc[:], in0=abc[:], scalar1=-1.0, scalar2=None, op0=ALU.mult), r=[abc], w=[abc])
        Sst = sb1("o3S", [128, 2048]); Sb = sb1("o3Sb", [128, 2048], BF16)
        xk = [sb1(f"o3xk{i}", [128, 2048], BF16) for i in range(2)]
        dtk = [sb1(f"o3dtk{i}", [128, 64]) for i in range(2)]
        bk = [sb1(f"o3bk{i}", [128, 512], BF16) for i in range(2)]
        bct = [sb1(f"o3bct{i}", [128, 8, 128], BF16) for i in range(2)]
        xg = sb1("o3xg", [128, 2048], BF16); xgw = sb1("o3xgw", [128, 2048], BF16)
        dta = sb1("o3dta", [128, 32]); nacs = sb1("o3nacs", [128, 32])
        dth = sb1("o3dth", [128, 32], BF16); dtl = sb1("o3dtl", [128, 32], BF16)
        trib = sb1("o3trib", [128, 2, 128], BF16)
        kb.op("dve", lambda: V.tensor_copy(out=trib[:], in_=tri[:]), r=[tri], w=[trib])
        snm4 = sb1("o3snm4", [128, 2, 4, 128], BF16)
        for r4 in range(4):
            kb.op("dve", lambda: V.tensor_copy(out=snm4[:, :, r4, :], in_=snm[:]), r=[snm], w=[snm4])
        tot = sb1("o3tot", [128, 32]); wd = sb1("o3wd", [128, 32]); dl = sb1("o3dl", [128, 32])
        CBT = sb1("o3CBT", [128, 4, 128], BF16)
        Da = [sb1(f"o3Da{i}", [128, 4, 128], BF16) for i in range(2)]
        Dm = [sb1(f"o3Dm{i}", [128, 4, 128], BF16) for i in range(2)]
        Cs = [sb1(f"o3Cs{i}", [128, 4, 128], BF16) for i in range(2)]
        Lm = [sb1(f"o3Lm{i}", [128, 4, 128], BF16) for i in range(2)]
        yf = sb1("o3yf", [128, 2048]); yt_ = sb1("o3yt", [128, 2048]); ybf = sb1("o3ybf", [128, 2048], BF16)
        yTt = sb1("o3yTt", [128, 16, 128], BF16)
        h0t = sb1("o3h0t", [128, 128])
        bc64 = lambda ap, nh: ap.unsqueeze(2).broadcast_to([128, nh, 64])
        cctr = [0]

        def rot4():
            b = kb.psb[kb.psn % 4]
            kb.psn += 1
            return b

        def run(sg, d, pj):
            nch = sg.L // 128
            order = list(range(nch)) if d == 0 else list(range(nch - 1, -1, -1))
            hd = slice(32 * d, 32 * d + 32)
            trid = tri[:, d, :]
            if sg.name == "s":
                for j in range(16):
                    kb.dma(h0t[:], ssd_h0[d, j * 128:(j + 1) * 128, :], w=[h0t])
                    ps = rot4()
                    kb.op("pe", lambda: P.transpose(ps[:, 0:128], h0t[:], ident[:]), r=[h0t, ident], w=[ps])
                    kb.op("act", lambda: S.activation(out=Sst[:, j * 128:(j + 1) * 128], in_=ps[:, 0:128], func=AF.Copy), r=[ps], w=[Sst])
            else:
                kb.op("dve", lambda: V.memset(Sst[:], 0.0), w=[Sst])
            kb.op("act", lambda: S.activation(out=Sb[:], in_=Sst[:], func=AF.Copy), r=[Sst], w=[Sb])
            def loads(ch_, i_):
                c0_ = ch_ * 128
                kb.dma(xk[i_][:], xtd[sg.name][c0_:c0_ + 128, :], r=[f"tokd_{sg.name}:{c0_ // sg.N}"], w=[xk[i_]])
                kb.dma(dtk[i_][:], dtd[sg.name][c0_:c0_ + 128, :], r=[f"dtd_{sg.name}:{ch_}"], w=[dtk[i_]])
                kb.dma(bk[i_][:], btd[sg.name][c0_:c0_ + 128, :], r=[f"tokd_{sg.name}:{c0_ // sg.N}"], w=[bk[i_]])
                kb.dma(bct[i_][:], bcTd[sg.name][:, c0_:c0_ + 128].rearrange("(c p) n -> p c n", p=128),
                       r=[f"bcTd_{sg.name}:{c0_ // sg.N}"], w=[bct[i_]])

            cctr[0] += 1
            loads(order[0], cctr[0] % 2)
            for oi, ch in enumerate(order):
                c0 = ch * 128
                i = cctr[0] % 2
                cctr[0] += 1
                if oi + 1 < len(order):
                    loads(order[oi + 1], cctr[0] % 2)
                X, DTK, BK, BCT = xk[i], dtk[i], bk[i], bct[i]
                kb.op("dve", lambda: V.tensor_tensor(out=dta[:], in0=DTK[:, hd], in1=abc[:, hd], op=ALU.mult), r=[DTK, abc], w=[dta])
                kb.op("dve", lambda: V.tensor_copy(out=dth[:], in_=dta[:]), r=[dta], w=[dth])
                kb.op("dve", lambda: V.tensor_tensor(out=dtl[:], in0=dta[:], in1=dth[:], op=ALU.subtract), r=[dta, dth], w=[dtl])
                kb.op("dve", lambda: V.tensor_tensor(out=xg[:].rearrange("p (h d) -> p h d", h=32), in0=X[:].rearrange("p (h d) -> p h d", h=32),
                                                     in1=bc64(DTK[:, hd], 32), op=ALU.mult), r=[X, DTK], w=[xg])
                psa = rot4()
                kb.op("pe", lambda: P.matmul(psa[:, 0:32], lhsT=trid, rhs=dta[:], start=True, stop=False), r=[tri, dta], w=[psa])
                kb.op("pe", lambda: P.matmul(psa[:, 32:64], lhsT=onesf[:], rhs=dta[:], start=False, stop=True), r=[onesf, dta], w=[psa])
                kb.op("act", lambda: S.activation(out=tot[:], in_=psa[:, 32:64], func=AF.Copy), r=[psa], w=[tot])
                kb.op("dve", lambda: V.tensor_scalar(out=nacs[:], in0=psa[:, 0:32], scalar1=-1.0, scalar2=None, op0=ALU.mult), r=[psa], w=[nacs])
                kb.op("dve", lambda: V.tensor_tensor(out=wd[:], in0=tot[:], in1=psa[:, 0:32], op=ALU.subtract), r=[tot, psa], w=[wd])
                kb.op("act", lambda: S.activation(out=wd[:], in_=wd[:], func=AF.Exp), r=[wd], w=[wd])
                kb.op("act", lambda: S.activation(out=dl[:], in_=tot[:], func=AF.Exp), r=[tot], w=[dl])
                kb.op("pool", lambda: G.tensor_tensor(out=xgw[:].rearrange("p (h d) -> p h d", h=32), in0=xg[:].rearrange("p (h d) -> p h d", h=32),
                                                      in1=bc64(wd[:], 32), op=ALU.mult), r=[xg, wd], w=[xgw])
                pcb = rot4()
                for g in range(4):
                    kb.op("pe", lambda: P.matmul(pcb[:, g * 128:(g + 1) * 128], lhsT=BCT[:, g, :], rhs=BCT[:, 4 + g, :], start=(g == 0), stop=(g == 3)),
                          r=[BCT], w=[pcb])
                kb.op("act", lambda: S.activation(out=CBT[:], in_=pcb[:, :].rearrange("p (g t) -> p g t", g=4), func=AF.Copy), r=[pcb], w=[CBT])
                def quadA(q):
                    g = q // 2
                    j2 = q % 2
                    pe_ = rot4()
                    for hh in range(4):
                        h = 4 * q + hh
                        kb.op("pe", lambda: P.matmul(pe_[:, hh * 128:(hh + 1) * 128], lhsT=dth[:, h:h + 1].broadcast_to([128, 128]), rhs=trib[:, d, :],
                                                     start=(hh == 0), stop=False), r=[dth, trib], w=[pe_])
                        kb.op("pe", lambda: P.matmul(pe_[:, hh * 128:(hh + 1) * 128], lhsT=dtl[:, h:h + 1].broadcast_to([128, 128]), rhs=trib[:, d, :],
                                                     start=False, stop=False), r=[dtl, trib], w=[pe_])
                    kb.op("act", lambda: S.activation(out=Da[j2][:], in_=pe_[:, :].rearrange("p (g t) -> p g t", g=4), func=AF.Exp), r=[pe_], w=[Da[j2]])
                    kb.op("pool", lambda: G.tensor_tensor(out=Cs[j2][:], in0=Da[j2][:], in1=BCT[:, 4 + g:5 + g, :].broadcast_to([128, 4, 128]), op=ALU.mult),
                          r=[Da[j2], BCT], w=[Cs[j2]])
                    kb.op("pe", lambda: P.matmul(pe_[:, :], lhsT=identb[:], rhs=snm4[:, d, :, :].rearrange("p r t -> p (r t)"), start=False, stop=True),
                          r=[identb, snm4], w=[pe_])
                    for hh in range(4):
                        h = 4 * q + hh
                        kb.op("act", lambda: S.activation(out=Dm[j2][:, hh, :], in_=pe_[:, hh * 128:(hh + 1) * 128], func=AF.Exp, bias=nacs[:, h:h + 1]),
                              r=[pe_, nacs], w=[Dm[j2]])
                    kb.op("dve", lambda: V.tensor_tensor(out=Lm[j2][:], in0=Dm[j2][:], in1=CBT[:, g:g + 1, :].broadcast_to([128, 4, 128]), op=ALU.mult),
                          r=[Dm[j2], CBT], w=[Lm[j2]])

                def quadB(q):
                    j2 = q % 2
                    for hh in range(4):
                        h = 4 * q + hh
                        py = kb.psb[4 + h // 8]
                        col = slice((h % 8) * 64, (h % 8) * 64 + 64)
                        kb.op("pe", lambda: P.matmul(py[:, col], lhsT=Lm[j2][:, hh, :], rhs=xg[:, h * 64:(h + 1) * 64], start=True, stop=False),
                              r=[Lm[j2], xg], w=[py])
                        kb.op("pe", lambda: P.matmul(py[:, col], lhsT=Cs[j2][:, hh, :], rhs=Sb[:, h * 64:(h + 1) * 64], start=False, stop=True),
                              r=[Cs[j2], Sb], w=[py])

                quadA(0)
                for q in range(8):
                    if q + 1 < 8:
                        quadA(q + 1)
                    quadB(q)
                if d == 0:
                    for b4 in range(4):
                        kb.op("act" if b4 % 2 else "dve",
                              (lambda b4=b4: S.activation(out=yf[:, b4 * 512:(b4 + 1) * 512], in_=kb.psb[4 + b4][:, :], func=AF.Copy)) if b4 % 2 else
                              (lambda b4=b4: V.tensor_copy(out=yf[:, b4 * 512:(b4 + 1) * 512], in_=kb.psb[4 + b4][:, :])),
                              r=[kb.psb[4 + b4]], w=[yf])
                    kb.dma(yfd[sg.name][c0:c0 + 128, :], yf[:], r=[yf], w=[f"yfd_{sg.name}:{ch}"])
                else:
                    kb.dma(yf[:], yfd[sg.name][c0:c0 + 128, :], r=[f"yfd_{sg.name}:{ch}"], w=[yf])
                    for b4 in range(4):
                        kb.op("dve", lambda: V.tensor_tensor(out=yt_[:, b4 * 512:(b4 + 1) * 512], in0=kb.psb[4 + b4][:, :], in1=yf[:, b4 * 512:(b4 + 1) * 512],
                                                             op=ALU.add), r=[kb.psb[4 + b4], yf], w=[yt_])
                    kb.op("pool", lambda: G.tensor_tensor(out=yf[:].rearrange("p (h d) -> p h d", h=32), in0=X[:].rearrange("p (h d) -> p h d", h=32),
                                                          in1=bc64(dsk[:], 32), op=ALU.mult), r=[X, dsk], w=[yf])
                    kb.op("dve", lambda: V.tensor_tensor(out=ybf[:], in0=yt_[:], in1=yf[:], op=ALU.add), r=[yt_, yf], w=[ybf])
                    for q4 in range(4):
                        ps = rot4()
                        psb_ = ps[:].bitcast(BF16)
                        for k4 in range(4):
                            cc = q4 * 4 + k4
                            kb.op("pe", lambda: P.transpose(psb_[:, k4 * 128:(k4 + 1) * 128], ybf[:, cc * 128:(cc + 1) * 128], identb[:]),
                                  r=[ybf, identb], w=[ps])
                        kb.op("act", lambda: S.activation(out=yTt[:, q4 * 4:(q4 + 1) * 4, :], in_=psb_[:, 0:512].rearrange("p (c n) -> p c n", c=4), func=AF.Copy),
                              r=[ps], w=[yTt])
                    kb.dma(yTd[sg.name][:, c0:c0 + 128].rearrange("(c p) n -> p c n", p=128), yTt[:], r=[yTt], w=[f"yTd_{sg.name}:{c0 // sg.N}"])
                for g in range(4):
                    gs = slice(g * 512, (g + 1) * 512)
                    pss = rot4()
                    kb.op("pe", lambda: P.matmul(pss[:, :], lhsT=BK[:, g * 128:(g + 1) * 128], rhs=xgw[:, gs], start=True, stop=True), r=[BK, xgw], w=[pss])
                    kb.op("dve", lambda: V.tensor_tensor(out=Sst[:, gs].rearrange("p (h d) -> p h d", h=8), in0=Sst[:, gs].rearrange("p (h d) -> p h d", h=8),
                                                         in1=bc64(dl[:, 8 * g:8 * g + 8], 8), op=ALU.mult), r=[Sst, dl], w=[Sst])
                    kb.op("dve", lambda: V.tensor_tensor(out=Sst[:, gs], in0=Sst[:, gs], in1=pss[:, :], op=ALU.add), r=[Sst, pss], w=[Sst])
                    kb.op("act", lambda: S.activation(out=Sb[:, gs], in_=Sst[:, gs], func=AF.Copy), r=[Sst], w=[Sb])
            if pj is not None:
                for j in range(16):
                    ps = rot4()
                    kb.op("pe", lambda: P.transpose(ps[:, 0:128], Sst[:, j * 128:(j + 1) * 128], ident[:]), r=[Sst, ident], w=[ps])
                    kb.op("dve", lambda: V.tensor_copy(out=h0t[:], in_=ps[:, 0:128]), r=[ps], w=[h0t])
                    kb.dma(o_ssd[pj, d, j * 128:(j + 1) * 128, :], h0t[:], r=[h0t], w=[f"o_ssd:{pj}:{d}:{j}"])

        for sg, pj in ((segs[0], None), (segs[1], 0), (segs[2], 1)):
            run(sg, 0, pj)
            run(sg, 1, pj)
        kb.barrier()
        es.close()

    def phase_odout():
        es = ExitStack()
        sb1 = lambda name, shape, dt=F32: es.enter_context(nc.sbuf_tensor(name, list(shape), dt))
        ng = sb1("o4ng", [128, 16])
        kb.dma(ng[:], ssd_norm_g.rearrange("(c p) -> p c", p=128), allow_slow_non_contiguous=True)
        yTs = [sb1(f"o4yT{i}", [128, 16, 512], BF16) for i in range(2)]; zss = [sb1(f"o4zs{i}", [128, 16, 512], BF16) for i in range(2)]
        more_wbufs(es, 1)
        tiles = [(sg, t) for sg in segs for t in range(sg.nt)]
        xts = {}

        def loads(i):
            sg, t = tiles[i]
            r0 = t * sg.N
            xts[i] = xt[tctr[0] % 2]
            tctr[0] += 1
            load_xT(sg, 0, t, xts[i])
            kb.dma(yTs[i % 2][:, :, :sg.N], yTd[sg.name][:, r0:r0 + sg.N].rearrange("(c p) n -> p c n", p=128), r=[f"yTd_{sg.name}:{t}"], w=[yTs[i % 2]])
            kb.dma(zss[i % 2][:, :, :sg.N], zsd[sg.name][:, r0:r0 + sg.N].rearrange("(c p) n -> p c n", p=128), r=[f"zsd_{sg.name}:{t}"], w=[zss[i % 2]])

        loads(0)
        for i in range(len(tiles)):
            if True:
                sg, t = tiles[i]
                N = sg.N
                r0 = t * N
                xtile = xts.pop(i)
                yT_, zs = yTs[i % 2], zss[i % 2]
                if i + 1 < len(tiles):
                    loads(i + 1)
                kb.op("dve", lambda: V.tensor_tensor(out=yT_[:, :, :N], in0=yT_[:, :, :N], in1=zs[:, :, :N], op=ALU.mult), r=[yT_, zs], w=[yT_])
                kb.op("act", lambda: S.activation(out=zs[:, :, :N], in_=yT_[:, :, :N], func=AF.Square), r=[yT_], w=[zs])
                ps = kb.bank()
                for kc in range(16):
                    kb.op("pe", lambda: P.matmul(ps[:, :N], lhsT=onesb[:], rhs=zs[:, kc, :N], start=(kc == 0), stop=(kc == 15)), r=[onesb, zs], w=[ps])
                kb.op("act", lambda: S.activation(out=tmpf[:, :N], in_=ps[:, :N], func=AF.Sqrt, bias=EPS, scale=1.0 / 2048), r=[ps], w=[tmpf])
                kb.op("dve", lambda: V.reciprocal(out=rstd[:, :N], in_=tmpf[:, :N]), r=[tmpf], w=[rstd])
                for kc in range(16):
                    kb.op("dve", lambda: V.scalar_tensor_tensor(out=yT_[:, kc, :N], in0=yT_[:, kc, :N], scalar=ng[:, kc:kc + 1], in1=rstd[:, :N],
                                                                op0=ALU.mult, op1=ALU.mult), r=[yT_, ng, rstd], w=[yT_])

                def ev(mc, ps2):
                    kb.op("dve", lambda: V.scalar_tensor_tensor(out=xtile[:, mc, :N], in0=ps2[:, :N],
                                                                scalar=modT[1][:, 16 + mc, sg.cond:sg.cond + 1],
                                                                in1=xtile[:, mc, :N], op0=ALU.mult, op1=ALU.add),
                          r=[ps2, modT[1], xtile], w=[xtile])
                linear(od_w_out, 16, D, lambda kc: yT_[:, kc, :N], N, ev, [yT_])
                store_xT(sg, 1, t, xtile, q="pool")
        kb.barrier()
        less_wbufs()
        es.close()

    kb.mark('setup')
    for sg in segs:
        phase_evin(sg)
    kb.mark('evin')
    ev_w_out = to_bf16("evout16", ev_w_out, D, D)
    ffn_w_gate = [to_bf16(f"wg16_{l}", ffn_w_gate[l], D, DFF) for l in range(2)]
    ffn_w_up = [to_bf16(f"wu16_{l}", ffn_w_up[l], D, DFF) for l in range(2)]
    ffn_w_down = [to_bf16(f"wd16_{l}", ffn_w_down[l], DFF, D) for l in range(2)]
    od_w_in16 = to_bf16("odin16", od_w_in[:, 0:5120], D, 5120)
    od_w_out = to_bf16("odout16", od_w_out, 2048, D)
    for sg in segs:
        if ENABLE_ATTN:
            phase_attn(sg)
    kb.mark('attn')
    kb.barrier()
    es0.close()
    phase_s5()
    kb.mark('s5')
    for sg in segs:
        phase_outproj(0, sg, ev_w_out, 8, 0, 1)
    kb.mark('outproj0')
    phase_ffn(0, 1, 0)
    kb.mark('ffn0')
    phase_odin()
    kb.mark('odin')
    phase_conv()
    kb.mark('conv')
    phase_scan()
    kb.mark('scan')
    phase_odout()
    kb.mark('odout')
    phase_ffn(1, 1, 0, final=True)
    kb.mark('ffn1')
    return kb


def _na_consts():
    lo = lambda r: min(max(r - 4, 0), 56)
    combos = [(5, e) for e in range(-2, 3)] + [(0, e) for e in range(0, 4)] + [(1, e) for e in range(-1, 3)] + \
             [(30, e) for e in range(-2, 2)] + [(31, e) for e in range(-3, 1)]
    q = np.arange(128); k = np.arange(128)
    qr, wq = q // 64, q % 64
    kr, wk = k // 64, k % 64
    dr_idx = np.zeros((21, 128, 128), np.int64); mask = np.zeros((21, 128, 128), np.float32)
    cs = np.clip(wq - 8, 0, 48)
    col_ok = (wk[None, :] >= cs[:, None]) & (wk[None, :] < cs[:, None] + 16)
    dc_idx = np.clip(wk[None, :] - wq[:, None], -15, 15) + 15
    for ci, (m, e) in enumerate(combos):
        qrow = 2 * m + qr; krow = 2 * (m + e) + kr
        lo_q = np.array([lo(r) for r in qrow])
        valid = (krow[None, :] >= lo_q[:, None]) & (krow[None, :] < lo_q[:, None] + 8) & col_ok
        dr_idx[ci] = np.clip(krow[None, :] - qrow[:, None] + 7, 0, 14)
        mask[ci] = np.where(valid, 0.0, -30000.0)
    return dr_idx, dc_idx, mask


_NC_CACHE = {}


def kernel(**inp):
    n = 8
    if "nc" not in _NC_CACHE:
        rec = build_program()
        rec.finish()
        _NC_CACHE["nc"] = build_program(plan=rec.needed).finish()
    nc = _NC_CACHE["nc"]
    f = lambda a: np.ascontiguousarray(np.asarray(a, dtype=np.float32))
    shared = {k: f(inp[k]) for k in ["norm_mix_g", "norm_ffn_g", "ada_w", "ada_b", "ffn_w_gate", "ffn_w_up", "ffn_w_down",
                                     "final_norm_g"]}
    shared["ev_w_in"] = f(inp["ev_w_in"][0]); shared["ev_w_out"] = f(inp["ev_w_out"][0])
    shared["od_w_in"] = f(inp["od_w_in"][0]); shared["od_w_out"] = f(inp["od_w_out"][0])
    shared["ident"] = np.eye(128, dtype=np.float32)
    dr_idx, dc_idx, mask = _na_consts()
    rpb = f(inp["na_rpb"][0])
    shared["rpbg"] = np.ascontiguousarray(rpb[:, dr_idx, dc_idx[None]].transpose(1, 0, 2, 3))
    shared["nmask"] = mask

    def lay(a):
        a = np.asarray(a, np.float32)
        rest = a.shape[3:]
        a = a.reshape(2, 16, 2, 64, *rest)
        a = np.moveaxis(a, [2, 3, 0, 1], [0, 1, 2, 3])
        return np.ascontiguousarray(a.reshape(128, 32, *rest))
    ldt = np.broadcast_to(np.asarray(inp["s5_log_dt"][0], np.float32)[:, :, None], (2, 32, 64))
    shared["s5_a"] = np.ascontiguousarray(np.stack([lay(inp["s5_a_re"][0]), lay(inp["s5_a_im"][0]), lay(ldt)], axis=1))
    shared["s5_B"] = np.ascontiguousarray(np.stack([lay(inp["s5_b_re"][0]), lay(inp["s5_b_im"][0])], axis=1))
    ct = lambda a: np.swapaxes(np.asarray(a, np.float32), 2, 3)
    shared["s5_C"] = np.ascontiguousarray(np.stack([lay(ct(inp["s5_c_re"][0])), lay(ct(inp["s5_c_im"][0]))], axis=1))
    shared["s5_dcol"] = np.ascontiguousarray(f(inp["s5_d"][0]).reshape(4, 128).T)
    shared["s5_w_glu"] = f(inp["s5_w_glu"][0])
    shared["od_conv_w"] = f(inp["od_conv_w"][0]); shared["od_conv_b"] = f(inp["od_conv_b"][0])
    shared["ssd_a_log"] = f(inp["ssd_a_log"][0]).reshape(64); shared["ssd_dt_bias"] = f(inp["ssd_dt_bias"][0]).reshape(64)
    shared["ssd_d"] = f(inp["ssd_d"][0]); shared["ssd_norm_g"] = f(inp["ssd_norm_g"][0])
    ar = np.arange(128)
    shared["tri"] = np.stack([(ar[:, None] <= ar[None, :]), (ar[:, None] >= ar[None, :])]).astype(np.float32)
    shared["ssd_nmask"] = np.stack([np.where(ar[:, None] > ar[None, :], -30000.0, 0.0),
                                    np.where(ar[:, None] < ar[None, :], -30000.0, 0.0)]).astype(np.float32)
    shared["ones_f"] = np.ones((128, 128), np.float32)
    in_maps = []
    for i in range(n):
        m = dict(shared)
        m["x_s"] = f(inp["x_sample"][i % 4])
        m["x_p"] = f(inp["x_prompt"][2 * i:2 * i + 2])
        m["cond"] = f(np.stack([np.asarray(inp["c"])[i % 4], np.asarray(inp["c_ctx"])]))
        m["cache_k"] = f(inp["cache_na_k"][i % 4, 0]); m["cache_v"] = f(inp["cache_na_v"][i % 4, 0])
        m["ssd_h0"] = np.ascontiguousarray(f(inp["state_ssd"][i % 4, 0]).reshape(2, 2048, 128))
        m["s5_h0"] = np.ascontiguousarray(np.stack([lay(inp["state_s5_re"][i % 4, 0]), lay(inp["state_s5_im"][i % 4, 0])], axis=1))
        in_maps.append(m)
    res = run_bass_kernel_spmd(nc, in_maps, core_ids=list(range(n)))
    R = res.results
    y_prompt = np.concatenate([R[i]["y_p"] for i in range(n)], axis=0)
    y_sample = np.stack([R[i]["y_s"] for i in range(4)], axis=0)
    nk = np.concatenate([R[i]["o_k"] for i in range(n)], axis=0)[:, None]
    nv = np.concatenate([R[i]["o_v"] for i in range(n)], axis=0)[:, None]
    s5o = np.concatenate([R[i]["o_s5"] for i in range(n)], axis=0)
    s5o = s5o.reshape(16, 2, 2, 64, 2, 16).transpose(0, 1, 4, 5, 2, 3).reshape(16, 2, 2, 32, 64)
    z5 = np.zeros((16, 1, 2, 32, 64), np.float32)
    zs = np.concatenate([R[i]["o_ssd"] for i in range(n)], axis=0).reshape(16, 1, 2, 32, 64, 128)
    return y_prompt, y_sample, nk, nv, np.ascontiguousarray(s5o[:, 0:1]), np.ascontiguousarray(s5o[:, 1:2]), zs
```
